# Optimizing a Trainium2 kernel written in Bass

```python
import math
import jax, jax.numpy as jnp
from jax import lax
import numpy as np

D_MODEL = 1024
BATCH = 4
SEQ = 8192
DEPTH = 2

D_PLE = 256
N_EVEN = (DEPTH + 1) // 2
N_ODD = DEPTH // 2
NORM_EPS = 1e-6

SSD_HEADS = 16
SSD_HEAD_DIM = 64
SSD_WIDTH = SSD_HEADS * SSD_HEAD_DIM
SSD_GROUPS = 2
SSD_STATE = 128
SSD_CONV = 4
SSD_CHUNK = 128
SSD_CONV_CH = SSD_WIDTH + 2 * SSD_GROUPS * SSD_STATE

DIFF_HEADS = 8
DIFF_HEAD_DIM = 64
DIFF_V_DIM = 2 * DIFF_HEAD_DIM
DIFF_QK_WIDTH = DIFF_HEADS * 2 * DIFF_HEAD_DIM
DIFF_WIDTH = DIFF_HEADS * DIFF_V_DIM
Q_BLOCK = 128
ROPE_THETA = 10000.0

EVEN_SPLITS = [
    SSD_WIDTH,
    SSD_WIDTH + SSD_CONV_CH,
    SSD_WIDTH + SSD_CONV_CH + SSD_HEADS,
    SSD_WIDTH + SSD_CONV_CH + SSD_HEADS + DIFF_QK_WIDTH,
    SSD_WIDTH + SSD_CONV_CH + SSD_HEADS + 2 * DIFF_QK_WIDTH,
    SSD_WIDTH + SSD_CONV_CH + SSD_HEADS + 2 * DIFF_QK_WIDTH + DIFF_WIDTH,
]
EVEN_IN = SSD_WIDTH + SSD_CONV_CH + SSD_HEADS + 2 * DIFF_QK_WIDTH + 2 * DIFF_WIDTH
EVEN_MIX = SSD_WIDTH + DIFF_WIDTH

CONF_WIDTH = 2 * D_MODEL
CONF_KERNEL = 31
CONF_IN = 3 * CONF_WIDTH

kernel_name = "hybrid_ssd_diffattn_conformer_block"


def rmsnorm(x, w):
    xf = x.astype(jnp.float32)
    y = xf * lax.rsqrt(jnp.mean(xf * xf, axis=-1, keepdims=True) + NORM_EPS)
    return (y * w.astype(jnp.float32)).astype(x.dtype)


def group_rmsnorm(x, w, groups):
    shp = x.shape
    xf = x.astype(jnp.float32).reshape(shp[:-1] + (groups, shp[-1] // groups))
    y = xf * lax.rsqrt(jnp.mean(xf * xf, axis=-1, keepdims=True) + NORM_EPS)
    return (y.reshape(shp) * w.astype(jnp.float32)).astype(x.dtype)


def layernorm(x, w, b):
    xf = x.astype(jnp.float32)
    mu = jnp.mean(xf, axis=-1, keepdims=True)
    var = jnp.mean(jnp.square(xf - mu), axis=-1, keepdims=True)
    y = (xf - mu) * lax.rsqrt(var + NORM_EPS)
    return (y * w.astype(jnp.float32) + b.astype(jnp.float32)).astype(x.dtype)


def causal_dwconv(x, w, b):
    k = w.shape[0]
    y = lax.conv_general_dilated(
        x, w[:, None, :], window_strides=(1,), padding=[(k - 1, 0)],
        dimension_numbers=("NWC", "WIO", "NWC"), feature_group_count=x.shape[-1])
    return y + b


def rope_tables(seq, dim):
    pos = jnp.arange(seq, dtype=jnp.float32)
    inv = ROPE_THETA ** (-jnp.arange(0, dim, 2, dtype=jnp.float32) / dim)
    ang = pos[:, None] * inv[None, :]
    return jnp.cos(ang), jnp.sin(ang)


def apply_rope(x, cos, sin):
    c = cos[:, None, None, :].astype(x.dtype)
    s = sin[:, None, None, :].astype(x.dtype)
    x1, x2 = jnp.split(x, 2, axis=-1)
    return jnp.concatenate([x1 * c - x2 * s, x2 * c + x1 * s], axis=-1)


def segsum_exp(a):
    t = a.shape[-1]
    cs = jnp.cumsum(a, axis=-1)
    diff = cs[..., :, None] - cs[..., None, :]
    mask = jnp.tril(jnp.ones((t, t), dtype=bool))
    return jnp.where(mask, jnp.exp(jnp.where(mask, diff, 0.0)), 0.0)


def ssd_scan(x, dt, a_neg, bm, cm):
    b, s, h, p = x.shape
    g, n = bm.shape[2], bm.shape[3]
    r = h // g
    c = s // SSD_CHUNK
    l = SSD_CHUNK
    xd = (x * dt[..., None]).reshape(b, c, l, g, r, p)
    a = (a_neg * dt).reshape(b, c, l, g, r).transpose(0, 3, 4, 1, 2)
    bc = bm.reshape(b, c, l, g, n)
    cc = cm.reshape(b, c, l, g, n)
    a_cs = jnp.cumsum(a, axis=-1)
    lmat = segsum_exp(a)
    cb = jnp.einsum("bclgn,bcsgn->bgcls", cc, bc)
    y_diag = jnp.einsum("bgcls,bgrcls,bcsgrp->bclgrp", cb, lmat, xd)
    decay_states = jnp.exp(a_cs[..., -1:] - a_cs)
    states = jnp.einsum("bclgn,bgrcl,bclgrp->bcgrpn", bc, decay_states, xd)
    chunk_decay = jnp.exp(a_cs[..., -1]).transpose(3, 0, 1, 2)

    def step(carry, inp):
        dec, st = inp
        return carry * dec[..., None, None] + st, carry

    init = jnp.zeros((b, g, r, p, n), dtype=states.dtype)
    _, prev = lax.scan(step, init, (chunk_decay, states.transpose(1, 0, 2, 3, 4, 5)))
    y_off = jnp.einsum("bclgn,cbgrpn,bgrcl->bclgrp", cc, prev, jnp.exp(a_cs))
    return (y_diag + y_off).reshape(b, s, h, p)


def diff_attention(q, k, v, lam):
    b, s, hh, _, d = q.shape
    nb = s // Q_BLOCK
    scale = d ** -0.5
    qb = q.reshape(b, nb, Q_BLOCK, hh, 2, d).transpose(1, 0, 2, 3, 4, 5)
    kpos = jnp.arange(s)

    def block(args):
        qi, start = args
        sc = jnp.einsum("bqhcd,bkhcd->bhcqk", qi, k).astype(jnp.float32) * scale
        qpos = start + jnp.arange(Q_BLOCK)
        mask = kpos[None, :] <= qpos[:, None]
        sc = jnp.where(mask, sc, -jnp.inf)
        att = jax.nn.softmax(sc, axis=-1)
        wgt = (att[:, :, 0] - lam * att[:, :, 1]).astype(v.dtype)
        return jnp.einsum("bhqk,bkhe->bqhe", wgt, v)

    out = lax.map(block, (qb, jnp.arange(nb) * Q_BLOCK))
    return out.transpose(1, 0, 2, 3, 4).reshape(b, s, hh, 2 * d)


def ssd_diff_mixer(hn, w_in, conv_w, conv_b, dt_bias, a_log, d_skip, ssd_norm_w,
                   lam_vecs, subln_w, w_out, lambda_init, cos, sin):
    b, s, _ = hn.shape
    proj = hn @ w_in
    z, xbc, dt, q, k, v, g = jnp.split(proj, EVEN_SPLITS, axis=-1)
    xbc = jax.nn.silu(causal_dwconv(xbc, conv_w, conv_b))
    xs, bm, cm = jnp.split(xbc, [SSD_WIDTH, SSD_WIDTH + SSD_GROUPS * SSD_STATE], axis=-1)
    xs = xs.reshape(b, s, SSD_HEADS, SSD_HEAD_DIM)
    bm = bm.reshape(b, s, SSD_GROUPS, SSD_STATE)
    cm = cm.reshape(b, s, SSD_GROUPS, SSD_STATE)
    dt = jax.nn.softplus(dt + dt_bias)
    a_neg = -jnp.exp(a_log)
    y = ssd_scan(xs, dt, a_neg, bm, cm) + xs * d_skip[:, None]
    y = group_rmsnorm(y.reshape(b, s, SSD_WIDTH) * jax.nn.silu(z), ssd_norm_w, SSD_GROUPS)
    q = apply_rope(q.reshape(b, s, DIFF_HEADS, 2, DIFF_HEAD_DIM), cos, sin)
    k = apply_rope(k.reshape(b, s, DIFF_HEADS, 2, DIFF_HEAD_DIM), cos, sin)
    v = v.reshape(b, s, DIFF_HEADS, DIFF_V_DIM)
    lv = lam_vecs.astype(jnp.float32)
    lam = jnp.exp(jnp.sum(lv[0] * lv[1])) - jnp.exp(jnp.sum(lv[2] * lv[3])) + lambda_init
    o = diff_attention(q, k, v, lam)
    o = rmsnorm(o, subln_w) * (1.0 - lambda_init)
    o = o.reshape(b, s, DIFF_WIDTH) * jax.nn.silu(g)
    return jnp.concatenate([y, o], axis=-1) @ w_out


def conformer_conv_mixer(hn, w_in, conv_w, conv_b, ln_w, ln_b, w_out):
    proj = hn @ w_in
    u, ug, g = jnp.split(proj, 3, axis=-1)
    u = u * jax.nn.sigmoid(ug)
    u = causal_dwconv(u, conv_w, conv_b)
    u = jax.nn.silu(layernorm(u, ln_w, ln_b))
    return (u * jax.nn.silu(g)) @ w_out


def setup_inputs(seed: int = 0) -> dict:
    key = jax.random.key(seed)
    ks = jax.random.split(key, 24)
    f32 = jnp.float32

    def nrm(k, shape, scale):
        return jax.random.normal(k, shape, f32) * scale

    dt0 = jnp.exp(jax.random.uniform(ks[8], (N_EVEN, SSD_HEADS), f32)
                  * (math.log(0.1) - math.log(0.001)) + math.log(0.001))
    return {
        "x": nrm(ks[0], (BATCH, SEQ, D_MODEL), 1.0),
        "p": nrm(ks[1], (DEPTH, BATCH, SEQ, D_PLE), 1.0),
        "norm_w": 1.0 + nrm(ks[2], (DEPTH, D_MODEL), 0.01),
        "ple_w": nrm(ks[3], (DEPTH, D_PLE, D_MODEL), D_PLE ** -0.5),
        "ple_gate_w": nrm(ks[4], (DEPTH, D_MODEL, D_MODEL), D_MODEL ** -0.5),
        "even_w_in": nrm(ks[5], (N_EVEN, D_MODEL, EVEN_IN), D_MODEL ** -0.5),
        "ssd_conv_w": nrm(ks[6], (N_EVEN, SSD_CONV, SSD_CONV_CH), SSD_CONV ** -0.5),
        "ssd_conv_b": nrm(ks[7], (N_EVEN, SSD_CONV_CH), 0.01),
        "ssd_dt_bias": dt0 + jnp.log(-jnp.expm1(-dt0)),
        "ssd_a_log": jnp.log(jax.random.uniform(ks[9], (N_EVEN, SSD_HEADS), f32, 1.0, 16.0)),
        "ssd_d": 1.0 + nrm(ks[10], (N_EVEN, SSD_HEADS), 0.1),
        "ssd_norm_w": 1.0 + nrm(ks[11], (N_EVEN, SSD_WIDTH), 0.01),
        "diff_lambda": nrm(ks[12], (N_EVEN, 4, DIFF_HEAD_DIM), 0.1),
        "diff_subln_w": 1.0 + nrm(ks[13], (N_EVEN, DIFF_V_DIM), 0.01),
        "even_w_out": nrm(ks[14], (N_EVEN, EVEN_MIX, D_MODEL), EVEN_MIX ** -0.5),
        "conf_w_in": nrm(ks[15], (N_ODD, D_MODEL, CONF_IN), D_MODEL ** -0.5),
        "conf_conv_w": nrm(ks[16], (N_ODD, CONF_KERNEL, CONF_WIDTH), CONF_KERNEL ** -0.5),
        "conf_conv_b": nrm(ks[17], (N_ODD, CONF_WIDTH), 0.01),
        "conf_ln_w": 1.0 + nrm(ks[18], (N_ODD, CONF_WIDTH), 0.01),
        "conf_ln_b": nrm(ks[19], (N_ODD, CONF_WIDTH), 0.01),
        "conf_w_out": nrm(ks[20], (N_ODD, CONF_WIDTH, D_MODEL), CONF_WIDTH ** -0.5),
        "final_norm_w": 1.0 + nrm(ks[21], (D_MODEL,), 0.01),
    }


def reference(x, p, norm_w, ple_w, ple_gate_w, even_w_in, ssd_conv_w, ssd_conv_b,
              ssd_dt_bias, ssd_a_log, ssd_d, ssd_norm_w, diff_lambda, diff_subln_w,
              even_w_out, conf_w_in, conf_conv_w, conf_conv_b, conf_ln_w, conf_ln_b,
              conf_w_out, final_norm_w):
    cos, sin = rope_tables(x.shape[1], DIFF_HEAD_DIM)
    h = x
    for i in range(DEPTH):
        hn = rmsnorm(h, norm_w[i])
        j = i // 2
        if i % 2 == 0:
            lambda_init = 0.8 - 0.6 * math.exp(-0.3 * i)
            h = h + ssd_diff_mixer(hn, even_w_in[j], ssd_conv_w[j], ssd_conv_b[j],
                                   ssd_dt_bias[j], ssd_a_log[j], ssd_d[j], ssd_norm_w[j],
                                   diff_lambda[j], diff_subln_w[j], even_w_out[j],
                                   lambda_init, cos, sin)
        else:
            h = h + conformer_conv_mixer(hn, conf_w_in[j], conf_conv_w[j], conf_conv_b[j],
                                         conf_ln_w[j], conf_ln_b[j], conf_w_out[j])
        e = p[i] @ ple_w[i]
        h = h + e * jax.nn.sigmoid(h @ ple_gate_w[i])
    return rmsnorm(h, final_norm_w)
```

```python
import contextlib
import math
import numpy as np
import concourse.bass as bass
import concourse.mybir as mybir
from concourse.bass_utils import run_bass_kernel_spmd

F32 = mybir.dt.float32
BF16 = mybir.dt.bfloat16
AF = mybir.ActivationFunctionType
ALU = mybir.AluOpType

EPS = 1e-6
NBLK = 64
HALO = 31
FB0 = 28
NFB = NBLK - FB0
NHB = NBLK - HALO
import os
NGDBG = int(os.environ.get('K_NG', '16'))
KSTOP = int(os.environ.get('K_STOP', '99'))
NPT = int(os.environ.get('K_NPT', '6'))
QKD = int(os.environ.get('K_QKD', '2'))
EMBED_WAIT = int(os.environ.get('K_EMB', '2'))
NPS = int(os.environ.get('K_NPS', '3'))


class Sched:
    def __init__(self, nc, n_dma_sems=14):
        self.nc = nc
        self.engs = ("pe", "act", "dve", "pool", "sp")
        self.ops = {k: [] for k in self.engs}
        self.psem = {k: nc.alloc_semaphore(f"prog_{k}") for k in ("pe", "act", "dve", "pool")}
        self.cnt = {k: 0 for k in self.psem}
        self.waited = {k: {} for k in self.engs}
        self.rings = {q: [[nc.alloc_semaphore(f"dma_{q}_{i}"), 0] for i in range(n_dma_sems)]
                      for q in ("sp", "pool", "act")}
        self.ring_pos = {q: 0 for q in self.rings}
        self.last_w = {}
        self.readers = {}
        self.sems = {}
        self.unsignaled = {k: False for k in self.engs}

    def _waits(self, eng, deps):
        need = {}
        for (sem, val, src) in deps:
            if src == "pe" and eng == "pe":
                continue
            sid = id(sem)
            self.sems[sid] = sem
            if self.waited[eng].get(sid, 0) >= val:
                continue
            if need.get(sid, 0) < val:
                need[sid] = val
        out = []
        for sid, val in need.items():
            self.waited[eng][sid] = val
            out.append((self.sems[sid], val))
        return out

    def _deps(self, reads, writes):
        deps = []
        for k in reads:
            t = self.last_w.get(k)
            if t is not None:
                deps.append(t)
        for k in writes:
            t = self.last_w.get(k)
            if t is not None:
                deps.append(t)
            deps.extend(self.readers.get(k, {}).values())
        return deps

    def _record(self, tok, reads, writes):
        sid = id(tok[0])
        for k in writes:
            self.last_w[k] = tok
            self.readers[k] = {}
        for k in reads:
            self.readers.setdefault(k, {})[sid] = tok

    def emit(self, eng, fn, reads=(), writes=(), signal=True):
        xr = [k for k in reads if k.startswith("P:")]
        if xr:
            writes = list(writes) + xr
            reads = [k for k in reads if not k.startswith("P:")]
        waits = self._waits(eng, self._deps(reads, writes))
        if signal:
            self.cnt[eng] += 1
            tok = (self.psem[eng], self.cnt[eng], eng)
            self.ops[eng].append((waits, fn, (self.psem[eng], 1)))
            self.unsignaled[eng] = False
        else:
            tok = (self.psem[eng], self.cnt[eng] + 1, eng)
            self.ops[eng].append((waits, fn, None))
            self.unsignaled[eng] = True
        self._record(tok, reads, writes)

    def dma(self, q, out, in_, reads=(), writes=(), **kw):
        ring = self.rings[q]
        i = self.ring_pos[q]
        self.ring_pos[q] = (i + 1) % len(ring)
        sem, c = ring[i]
        deps = self._deps(reads, writes)
        if c > 0:
            deps.append((sem, c, None))
        waits = self._waits(q, deps)
        ring[i][1] = c + 16
        tok = (sem, c + 16, None)
        self.ops[q].append((waits, lambda e: e.dma_start(out=out, in_=in_, **kw), (sem, 16)))
        self._record(tok, reads, writes)

    def barrier(self):
        assert not any(self.unsignaled.values()), self.unsignaled
        toks = [(self.psem[k], self.cnt[k], None) for k in self.psem if self.cnt[k] > 0]
        for q in self.rings:
            for sem, c in self.rings[q]:
                if c > 0:
                    toks.append((sem, c, None))
        for e in self.engs:
            waits = self._waits(e, toks)
            if waits:
                self.ops[e].append((waits, None, None))
        self.last_w = {}
        self.readers = {}

    def finish(self):
        nc = self.nc
        with nc.allow_non_contiguous_dma(reason="small strided parameter loads"), nc.Block() as block:
            def replay(name):
                def f(e):
                    emb = EMBED_WAIT and name != "sp" or EMBED_WAIT == 2
                    for waits, fn, inc in self.ops[name]:
                        if fn is None or not emb or not waits:
                            for sem, val in waits:
                                e.wait_ge(sem, val)
                            if fn is not None:
                                ins = fn(e)
                                if inc is not None:
                                    ins.then_inc(inc[0], inc[1])
                        else:
                            for sem, val in waits[:-1]:
                                e.wait_ge(sem, val)
                            ins = fn(e)
                            ins._wait_ge(waits[-1][0], waits[-1][1])
                            if inc is not None:
                                ins.then_inc(inc[0], inc[1])
                return f
            block.sync(replay("sp"))
            block.scalar(replay("act"))
            block.vector(replay("dve"))
            block.gpsimd(replay("pool"))
            block.tensor(replay("pe"))


class B:
    def __init__(self, t, k):
        self.t = t
        self.k = k

    def __getitem__(self, idx):
        return self.t[idx]


def build(nc, dbg=False, upto=99):
    S = Sched(nc)
    kind_s = "ExternalOutput" if dbg else "Internal"

    def din(name, shape, dt=F32):
        return nc.dram_tensor(name, list(shape), dt, kind="ExternalInput").ap()

    def dscr(name, shape, dt):
        return nc.dram_tensor(name, list(shape), dt, kind=kind_s).ap()

    x_loc = din("x_loc", [NBLK, 128, 1024])
    p0 = din("p0", [NHB, 128, 256])
    p1 = din("p1", [32, 128, 256])
    flagc = din("flagc", [128, 2])
    cosT = din("cosT", [128, 8192])
    sinT = din("sinT", [128, 8192])
    cst = din("cst", [5, 128, 128])
    w_in0 = din("w_in0", [1024, 6672])
    nw = din("nw", [2, 1024, 1])
    cw0 = din("cw0", [1536, 4])
    cb0 = din("cb0", [1536, 1])
    dtb = din("dtb", [1, 16])
    alog = din("alog", [1, 16])
    dfull = din("dfull", [1, 1024])
    snw = din("snw", [1, 1024])
    lamv = din("lamv", [1, 256])
    subw = din("subw", [128, 1])
    w_out0 = din("w_out0", [2048, 1024])
    ple_w = din("ple_w", [2, 256, 1024])
    gate_w = din("gate_w", [2, 1024, 1024])
    w_in1 = din("w_in1", [1024, 6144])
    ccw = din("ccw", [2048, 31])
    ccb = din("ccb", [2048, 1])
    lnw = din("lnw", [2048, 1])
    lnb = din("lnb", [2048, 1])
    w_out1 = din("w_out1", [2048, 1024])
    fnw = din("fnw", [1, 1024])
    out = nc.dram_tensor("out", [32, 128, 1024], F32, kind="ExternalOutput").ap()

    kT = dscr("kT", [8, 128, 8192], BF16)
    vS = dscr("vS", [NBLK, 128, 1024], BF16)
    qT = dscr("qT", [8, 128, NFB * 128], BF16)
    zs = dscr("zs", [NFB, 128, 1024], BF16)
    gsT = dscr("gsT", [8, 128, NFB * 128], BF16)
    xsS = dscr("xsS", [NBLK, 128, 1024], BF16)
    bS = dscr("bS", [NBLK, 128, 256], BF16)
    bTS = dscr("bTS", [NBLK, 128, 2, 128], BF16)
    cTS = dscr("cTS", [NFB, 128, 2, 128], BF16)
    dtS = dscr("dtS", [NBLK, 128, 16], F32)
    ymix = dscr("ymix", [NHB, 128, 16, 128], BF16)
    h1S = dscr("h1S", [NHB, 128, 1024], F32)
    hn1T = dscr("hn1T", [NHB, 128, 8, 128], BF16)
    gluT = dscr("gluT", [16, 128, NHB * 128], BF16)
    sg1T = dscr("sg1T", [16, 128, 4096], BF16)

    uid = [0]

    def mk(es, name, shape, dt, psum=False):
        uid[0] += 1
        nm = f"{name}_{uid[0]}"
        if psum:
            t = es.enter_context(nc.psum_tensor(nm, list(shape), dt))
        else:
            t = es.enter_context(nc.sbuf_tensor(nm, list(shape), dt))
        return B(t, ("P:" + nm) if psum else nm)

    def MM(o, lhsT, rhs, start, stop, r, w, sig=True):
        S.emit("pe", lambda e: e.matmul(out=o, lhsT=lhsT, rhs=rhs, start=start, stop=stop), r, w, signal=sig)

    def TR(o, in_, ident, r, w, sig=True):
        S.emit("pe", lambda e: e.transpose(out=o, in_=in_, identity=ident), r, w, signal=sig)

    def ACT(o, in_, func, r, w, **kw):
        S.emit("act", lambda e: e.activation(out=o, in_=in_, func=func, **kw), r, w)

    def TT(eng, o, a, b, op, r, w):
        S.emit(eng, lambda e: e.tensor_tensor(out=o, in0=a, in1=b, op=op), r, w)

    def TS(eng, o, a, s1, op0, r, w, s2=None, op1=None):
        if op1 is None:
            S.emit(eng, lambda e: e.tensor_scalar(out=o, in0=a, scalar1=s1, scalar2=None, op0=op0), r, w)
        else:
            S.emit(eng, lambda e: e.tensor_scalar(out=o, in0=a, scalar1=s1, scalar2=s2, op0=op0, op1=op1), r, w)

    def STT(o, a, s, b, op0, op1, r, w):
        S.emit("dve", lambda e: e.scalar_tensor_tensor(out=o, in0=a, scalar=s, in1=b, op0=op0, op1=op1), r, w)

    def CP(eng, o, a, r, w):
        if eng == "act":
            ACT(o, a, AF.Copy, r, w)
        else:
            S.emit(eng, lambda e: e.tensor_copy(out=o, in_=a), r, w)

    def RECIP(o, a, r, w):
        S.emit("dve", lambda e: e.reciprocal(out=o, in_=a), r, w)

    def LD(o, in_, r, w, q="sp", **kw):
        S.dma(q, o, in_, r, w, **kw)

    def ST(o, in_, r, w, q="pool", **kw):
        S.dma(q, o, in_, r, w, **kw)

    def rstd_from_ss(ss, n, eps=EPS):
        ACT(ss[:], ss[:], AF.Sqrt, [ss.k], [ss.k], scale=1.0 / n, bias=eps)
        RECIP(ss[:], ss[:], [ss.k], [ss.k])

    gs = contextlib.ExitStack()
    with gs:
        cstf = mk(gs, "cstf", [128, 5, 128], F32)
        LD(cstf[:], cst.rearrange("c p f -> p c f"), [], [cstf.k])
        identb = mk(gs, "identb", [128, 128], BF16)
        permb = mk(gs, "permb", [128, 128], BF16)
        CP("dve", identb[:], cstf[:, 0, :], [cstf.k], [identb.k])
        CP("dve", permb[:], cstf[:, 1, :], [cstf.k], [permb.k])
        mle = cstf[:, 2, :]
        mgt = cstf[:, 3, :]
        onesf = cstf[:, 4, :]
        mleb = mk(gs, "mleb", [128, 128], BF16)
        CP("dve", mleb[:], mle, [cstf.k], [mleb.k])
        flg = mk(gs, "flg", [128, 2], F32)
        LD(flg[:], flagc, [], [flg.k])

        def load_w_bf16(es, src, rows, cols, scale_col=None, name="w", prefetch=False):
            kcs = rows // 128
            wt = mk(es, name, [128, kcs, cols], BF16)
            CH = 512
            ls = es if prefetch else contextlib.ExitStack()
            nst = 3 if prefetch else 6
            stg = [mk(ls, "wstg", [128, CH], F32) for _ in range(nst)]
            n = 0
            for kc in range(kcs):
                for c0 in range(0, cols, CH):
                    cw = min(CH, cols - c0)
                    st = stg[n % nst]
                    LD(st[:, 0:cw], src[kc * 128:(kc + 1) * 128, c0:c0 + cw], [], [st.k])
                    eng = "pool" if prefetch else ("dve", "act")[n % 2]
                    o_, i_ = wt[:, kc, c0:c0 + cw], st[:, 0:cw]
                    wk = wt.k + ("" if not prefetch else "")
                    if eng == "act":
                        if scale_col is not None:
                            ACT(o_, i_, AF.Copy, [st.k, scale_col.k], [wk], scale=scale_col[:, kc:kc + 1])
                        else:
                            ACT(o_, i_, AF.Copy, [st.k], [wk])
                    elif scale_col is not None:
                        TS(eng, o_, i_, scale_col[:, kc:kc + 1], ALU.mult, [st.k, scale_col.k], [wk],
                           s2=0.0, op1=ALU.add)
                    else:
                        TS(eng, o_, i_, 1.0, ALU.mult, [st.k], [wk], s2=0.0, op1=ALU.add)
                    n += 1
            if not prefetch:
                S.barrier()
                ls.close()
            return wt

        with contextlib.ExitStack() as es:
            nwc = mk(es, "nwc", [128, 8], F32)
            LD(nwc[:], nw[0].rearrange("(k p) o -> p (k o)", p=128), [], [nwc.k])
            W0 = load_w_bf16(es, w_in0, 1024, 6672, nwc, "W0")
            cwc = mk(es, "cwc", [128, 12, 4], F32)
            LD(cwc[:], cw0.rearrange("(c p) j -> p c j", p=128), [], [cwc.k])
            cbc = mk(es, "cbc", [128, 12], F32)
            LD(cbc[:], cb0.rearrange("(c p) o -> p (c o)", p=128), [], [cbc.k])
            dtbb = mk(es, "dtbb", [128, 16], F32)
            LD(dtbb[:], dtb.partition_broadcast(128), [], [dtbb.k])
            hist = mk(es, "hist", [128, 12, 3], F32)
            S.emit("dve", lambda e: e.memset(hist[:], 0.0), [], [hist.k])
            xblk = [mk(es, "xblk", [128, 1024], F32) for _ in range(2)]
            junk = mk(es, "junk", [128, 1024], BF16)
            ssb = [mk(es, "ssb", [128, 1], F32) for _ in range(2)]
            xn = [mk(es, "xn", [128, 1024], BF16) for _ in range(4)]
            hnTs = [mk(es, "hnT", [128, 8, 512], BF16) for _ in range(2)]
            hnT = hnTs[0]
            ssb4 = [mk(es, "ssb4", [128, 1], F32) for _ in range(4)]
            pend = []
            raw = [mk(es, "raw", [128, 515], F32) for _ in range(3)]
            cacc = [mk(es, "cacc", [128, 512], F32) for _ in range(3)]
            xa = [mk(es, "xa", [128, 512], BF16) for _ in range(3)]
            xsg = mk(es, "xsg", [128, 4, 1024], BF16)
            bg = mk(es, "bg", [128, 4, 256], BF16)
            cosg = mk(es, "cosg", [128, 512], F32)
            sing = mk(es, "sing", [128, 512], F32)
            qb = [mk(es, "qb", [128, 512], BF16) for _ in range(3)]
            t1 = [mk(es, "t1", [128, 512], F32) for _ in range(3)]
            t2 = [mk(es, "t2", [128, 512], F32) for _ in range(3)]
            qr = [mk(es, "qr", [128, 512], BF16) for _ in range(3)]
            fo = [mk(es, "fo", [128, 512], BF16) for _ in range(2)]
            tmo = [mk(es, "tmo", [128, 1024], BF16) for _ in range(2)]
            dts = mk(es, "dts", [128, 4, 16], F32)
            pacc = [mk(es, "pacc", [128, 512], F32, psum=True) for _ in range(3)]
            ptr = [mk(es, "ptr", [128, 1024], BF16, psum=True) for _ in range(2)]
            pperm = [mk(es, "pperm", [128, 512], F32, psum=True) for _ in range(2)]
            pdt = mk(es, "pdt", [128, 64], F32, psum=True)
            rot = {"acc": 0, "i": 0}

            PDEPTH = 2

            def flush(min_age=0):
                keep, run = [], []
                for ent in pend:
                    (run if ent[0] >= min_age else keep).append(ent)
                pend[:] = keep
                for ent in run:
                    ent[1]()

            def defer(fn):
                pend.append([0, fn])

            def proj_fm(col0, M=128):
                pa = pacc[rot["acc"] % 3]
                rot["acc"] += 1
                for kc in range(8):
                    MM(pa[0:M, :], W0[:, kc, col0:col0 + M], hnT[:, kc, :], kc == 0, kc == 7,
                       [W0.k, hnT.k], [pa.k], sig=(kc == 7))
                for ent in pend:
                    ent[0] += 1
                flush(PDEPTH)
                return pa

            def fe_part1(g):
                for bi in range(4):
                    blk = g * 4 + bi
                    xb_, s_, xn_ = xblk[bi % 2], ssb4[bi], xn[bi]
                    LD(xb_[:], x_loc[blk], [], [xb_.k])
                    ACT(junk[:], xb_[:], AF.Square, [xb_.k], [junk.k, s_.k], accum_out=s_[:])
                    rstd_from_ss(s_, 1024)
                    TS("dve", xn_[:], xb_[:], s_[:, 0:1], ALU.mult, [xb_.k, s_.k], [xn_.k])

            def fe_part2(g):
                hn = hnTs[g % 2]
                for bi in range(4):
                    xn_, pt_ = xn[bi], ptr[bi % 2]
                    for kc in range(8):
                        TR(pt_[:, kc * 128:(kc + 1) * 128], xn_[:, kc * 128:(kc + 1) * 128], identb[:],
                           [xn_.k, identb.k], [pt_.k])
                    CP("act", hn[:, :, bi * 128:(bi + 1) * 128],
                       pt_[:].rearrange("p (k t) -> p k t", k=8), [pt_.k], [hn.k])

            NGA = min(16, NGDBG) if upto >= 1 else 0
            if NGA:
                fe_part1(0)
                fe_part2(0)
            for g in range(NGA):
                full = g >= 7
                t0 = g * 512
                hnT = hnTs[g % 2]
                if g + 1 < NGA:
                    fe_part1(g + 1)
                if KSTOP < 2:
                    continue
                for bi in range(4):
                    for kc in range(8):
                        MM(pdt[:, bi * 16:(bi + 1) * 16], hnT[:, kc, bi * 128:(bi + 1) * 128],
                           W0[:, kc, 2560:2576], kc == 0, kc == 7, [hnT.k, W0.k], [pdt.k], sig=(kc == 7))
                TT("dve", dts[:], pdt[:].rearrange("p (b h) -> p b h", b=4),
                   dtbb[:].unsqueeze(1).to_broadcast([128, 4, 16]), ALU.add, [pdt.k, dtbb.k], [dts.k])
                ACT(dts[:], dts[:], AF.Exp, [dts.k], [dts.k])
                ACT(dts[:], dts[:], AF.Ln, [dts.k], [dts.k], bias=1.0)
                ST(dtS[g * 4:(g + 1) * 4].rearrange("b p h -> p b h"), dts[:], [dts.k], ["dtS"])
                def age_flush():
                    for ent in pend:
                        ent[0] += 1
                    flush(PDEPTH)

                def item_x(j, g=g, full=full):
                    pa = proj_fm(1024 + j * 128)
                    rw, ca, xa_ = raw[j % 3], cacc[j % 3], xa[j % 3]
                    CP("act", rw[:, 3:515], pa[:], [pa.k], [rw.k])
                    CP("pool", rw[:, 0:3], hist[:, j, :], [hist.k, rw.k], [rw.k])
                    TS("dve", ca[:], rw[:, 0:512], cwc[:, j, 0:1], ALU.mult, [rw.k, cwc.k, cbc.k], [ca.k],
                       s2=cbc[:, j:j + 1], op1=ALU.add)
                    for tp in range(1, 4):
                        STT(ca[:], rw[:, tp:tp + 512], cwc[:, j, tp:tp + 1], ca[:], ALU.mult, ALU.add,
                            [rw.k, cwc.k, ca.k], [ca.k])
                    CP("pool", hist[:, j, :], rw[:, 512:515], [rw.k], [hist.k])
                    ACT(xa_[:], ca[:], AF.Silu, [ca.k], [xa_.k])

                    def after_x():
                        if j < 10:
                            pt_ = ptr[j % 2]
                            for bi in range(4):
                                TR(pt_[:, bi * 128:(bi + 1) * 128], xa_[:, bi * 128:(bi + 1) * 128], identb[:],
                                   [xa_.k, identb.k], [pt_.k])
                            if j < 8:
                                CP("act", xsg[:, :, j * 128:(j + 1) * 128],
                                   pt_[:, 0:512].rearrange("p (b t) -> p b t", b=4), [pt_.k], [xsg.k])
                            else:
                                CP("act", bg[:, :, (j - 8) * 128:(j - 7) * 128],
                                   pt_[:, 0:512].rearrange("p (b t) -> p b t", b=4), [pt_.k], [bg.k])
                                ST(bTS[g * 4:(g + 1) * 4, :, j - 8, :].rearrange("b p t -> p b t"),
                                   xa_[:].rearrange("p (b t) -> p b t", b=4), [xa_.k], ["bTS"])
                        elif full:
                            ST(cTS[g * 4 - FB0:(g + 1) * 4 - FB0, :, j - 10, :].rearrange("b p t -> p b t"),
                               xa_[:].rearrange("p (b t) -> p b t", b=4), [xa_.k], ["cTS"])
                    defer(after_x)

                def item_g(c, t0=t0):
                    pa = proj_fm(5648 + c * 128)
                    fo_ = fo[c % 2]
                    ACT(fo_[:], pa[:], AF.Silu, [pa.k], [fo_.k])
                    ST(gsT[c, :, t0 - FB0 * 128:t0 - FB0 * 128 + 512], fo_[:], [fo_.k], ["gsT"])

                def item_tm(kind, bi, g=g):
                    cbase = 0 if kind == "z" else 4624
                    tm = tmo[bi % 2]
                    for hf in range(2):
                        pa = pacc[rot["acc"] % 3]
                        rot["acc"] += 1
                        for kc in range(8):
                            MM(pa[:], hnT[:, kc, bi * 128:(bi + 1) * 128],
                               W0[:, kc, cbase + hf * 512:cbase + (hf + 1) * 512], kc == 0, kc == 7,
                               [hnT.k, W0.k], [pa.k], sig=(kc == 7))
                        age_flush()
                        if kind == "z":
                            ACT(tm[:, hf * 512:(hf + 1) * 512], pa[:], AF.Silu, [pa.k], [tm.k])
                        else:
                            CP("act" if hf == 0 else "dve", tm[:, hf * 512:(hf + 1) * 512], pa[:], [pa.k], [tm.k])
                    blk = g * 4 + bi
                    if kind == "z":
                        ST(zs[blk - FB0], tm[:], [tm.k], ["zs"])
                    else:
                        ST(vS[blk], tm[:], [tm.k], ["vS"])

                rope_n = [0]

                def item_rope(kind, h, t0=t0):
                    col0 = (2576 if kind == "q" else 3600) + h * 128
                    pa = proj_fm(col0)
                    n = rope_n[0]
                    rope_n[0] += 1
                    qb_, t1_, t2_, qr_, pp = qb[n % 3], t1[n % 3], t2[n % 3], qr[n % 3], pperm[n % 2]
                    CP("act", qb_[:], pa[:], [pa.k], [qb_.k])

                    def after_q():
                        MM(pp[:], permb[:], qb_[:], True, True, [permb.k, qb_.k], [pp.k])
                        TT("dve", t1_[:], pa[:], cosg[:], ALU.mult, [pa.k, cosg.k], [t1_.k])
                        TT("dve", t2_[:], pp[:], sing[:], ALU.mult, [pp.k, sing.k], [t2_.k])
                        TT("dve", qr_[:], t1_[:], t2_[:], ALU.add, [t1_.k, t2_.k], [qr_.k])
                        if kind == "q":
                            ST(qT[h, :, t0 - FB0 * 128:t0 - FB0 * 128 + 512], qr_[:], [qr_.k], ["qT"])
                        else:
                            ST(kT[h, :, t0:t0 + 512], qr_[:], [qr_.k], ["kT"])
                    defer(after_q)

                LD(cosg[:], cosT[:, t0:t0 + 512], [], [cosg.k])
                LD(sing[:], sinT[:, t0:t0 + 512], [], [sing.k])
                light = []
                if full:
                    light += [lambda c=c: item_g(c) for c in range(8)]
                    light += [lambda bi=bi: item_tm("z", bi) for bi in range(4)]
                light += [lambda bi=bi: item_tm("v", bi) for bi in range(4)]
                ropes = [lambda kind=kind, h=h: item_rope(kind, h)
                         for kind in ((["q"] if full else []) + ["k"]) for h in range(8)]
                li = 0
                for j in range(12):
                    item_x(j)
                    take = (len(light) * (j + 1)) // 12 - (len(light) * j) // 12
                    for _ in range(take):
                        light[li]()
                        li += 1
                    if j == 5 and g + 1 < NGA:
                        fe_part2(g + 1)
                flush()
                ST(xsS[g * 4:(g + 1) * 4].rearrange("b p f -> p b f"), xsg[:], [xsg.k], ["xsS"])
                ST(bS[g * 4:(g + 1) * 4].rearrange("b p f -> p b f"), bg[:], [bg.k], ["bS"])
                for it in ropes:
                    it()
                flush()
            S.barrier()

        if upto >= 2:
          with contextlib.ExitStack() as es:
            anegb = mk(es, "anegb", [128, 16], F32)
            LD(anegb[:], alog.partition_broadcast(128), [], [anegb.k])
            ACT(anegb[:], anegb[:], AF.Exp, [anegb.k], [anegb.k])
            TS("dve", anegb[:], anegb[:], -1.0, ALU.mult, [anegb.k], [anegb.k])
            dfl = mk(es, "dfl", [128, 1024], F32)
            LD(dfl[:], dfull.partition_broadcast(128), [], [dfl.k])
            snwb = mk(es, "snwb", [128, 1024], F32)
            LD(snwb[:], snw.partition_broadcast(128), [], [snwb.k])
            Sf = mk(es, "Sf", [128, 2, 512], F32)
            Sb = mk(es, "Sb", [128, 2, 512], BF16)
            S.emit("dve", lambda e: e.memset(Sf[:], 0.0), [], [Sf.k])
            S.emit("pool", lambda e: e.memset(Sb[:], 0.0), [], [Sb.k])
            xs_ = [mk(es, "xs", [128, 1024], BF16) for _ in range(2)]
            bt_ = [mk(es, "bt", [128, 256], BF16) for _ in range(2)]
            bT_ = [mk(es, "bT", [128, 2, 128], BF16) for _ in range(2)]
            cT_ = [mk(es, "cT", [128, 2, 128], BF16) for _ in range(2)]
            dt_ = [mk(es, "dt", [128, 16], F32) for _ in range(2)]
            z_ = [mk(es, "z", [128, 1024], BF16) for _ in range(2)]
            a_2 = [mk(es, "a", [128, 16], F32) for _ in range(2)]
            cs_2 = [mk(es, "cs", [128, 16], F32) for _ in range(2)]
            e_2 = [mk(es, "e", [128, 16], F32) for _ in range(2)]
            dec_2 = [mk(es, "dec", [128, 16], F32) for _ in range(2)]
            cd_2 = [mk(es, "cd", [128, 16], F32) for _ in range(2)]
            wv_2 = [mk(es, "wv", [128, 16], F32) for _ in range(2)]
            xd_2 = [mk(es, "xd", [128, 1024], BF16) for _ in range(2)]
            xdd_2 = [mk(es, "xdd", [128, 1024], BF16) for _ in range(2)]
            cbm4 = [mk(es, "cbm", [128, 128], F32) for _ in range(4)]
            lh = [mk(es, "lh", [128, 4, 128], F32) for _ in range(2)]
            eD = [mk(es, "eD", [128, 512], F32) for _ in range(2)]
            Mt = [mk(es, "Mt", [128, 4, 128], BF16) for _ in range(2)]
            y12 = [mk(es, "y1", [128, 1024], F32) for _ in range(2)]
            y32 = [mk(es, "y3", [128, 1024], F32) for _ in range(2)]
            yj2 = [mk(es, "yj", [128, 512], F32) for _ in range(2)]
            ss22 = [mk(es, "ss2", [128, 2], F32) for _ in range(2)]
            yn2 = [mk(es, "yn", [128, 1024], BF16) for _ in range(2)]
            ynT = [mk(es, "ynT", [128, 8, 128], BF16) for _ in range(2)]
            ps_s = mk(es, "ps_s", [128, 512], F32, psum=True)
            ps_D = mk(es, "ps_D", [128, 512], F32, psum=True)
            ps_yo = [mk(es, "ps_yo", [128, 512], F32, psum=True) for _ in range(2)]
            ps_y = [mk(es, "ps_y", [128, 512], F32, psum=True) for _ in range(2)]
            ps_st = mk(es, "ps_st", [128, 512], F32, psum=True)
            ps_tr = mk(es, "ps_tr", [128, 1024], BF16, psum=True)
            for c in range(NBLK):
                full = c >= HALO
                i2 = c % 2
                xs, bt, bT, cT, dt, z = xs_[i2], bt_[i2], bT_[i2], cT_[i2], dt_[i2], z_[i2]
                a_, cs_, e_, dec_, cd_, wv_, xd_, xdd_ = a_2[i2], cs_2[i2], e_2[i2], dec_2[i2], cd_2[i2], wv_2[i2], xd_2[i2], xdd_2[i2]
                y1, y3, yj, ss2, yn = y12[i2], y32[i2], yj2[i2], ss22[i2], yn2[i2]
                cbm = cbm4[i2 * 2:i2 * 2 + 2]
                LD(xs[:], xsS[c], ["xsS"], [xs.k])
                LD(bt[:], bS[c], ["bS"], [bt.k])
                LD(dt[:], dtS[c], ["dtS"], [dt.k])
                if full:
                    LD(bT[:], bTS[c], ["bTS"], [bT.k])
                    LD(cT[:], cTS[c - FB0], ["cTS"], [cT.k])
                    LD(z[:], zs[c - FB0], ["zs"], [z.k])
                TT("dve", a_[:], dt[:], anegb[:], ALU.mult, [dt.k, anegb.k], [a_.k])
                MM(ps_s[:, 0:16], mle, a_[:], True, True, [cstf.k, a_.k], ["P:ps_s"])
                MM(ps_s[:, 16:32], onesf, a_[:], True, True, [cstf.k, a_.k], ["P:ps_s"])
                CP("act", cs_[:], ps_s[:, 0:16], ["P:ps_s"], [cs_.k])
                TT("dve", dec_[:], ps_s[:, 16:32], cs_[:], ALU.subtract, ["P:ps_s", cs_.k], [dec_.k])
                ACT(dec_[:], dec_[:], AF.Exp, [dec_.k], [dec_.k])
                ACT(cd_[:], ps_s[:, 16:32], AF.Exp, ["P:ps_s"], [cd_.k])
                TT("dve", wv_[:], dt[:], dec_[:], ALU.mult, [dt.k, dec_.k], [wv_.k])
                TT("dve", xdd_[:].rearrange("p (h d) -> p h d", h=16), xs[:].rearrange("p (h d) -> p h d", h=16),
                   wv_[:].unsqueeze(2).to_broadcast([128, 16, 64]), ALU.mult, [xs.k, wv_.k], [xdd_.k])
                if full:
                    ACT(e_[:], cs_[:], AF.Exp, [cs_.k], [e_.k])
                    TT("dve", xd_[:].rearrange("p (h d) -> p h d", h=16), xs[:].rearrange("p (h d) -> p h d", h=16),
                       dt[:].unsqueeze(2).to_broadcast([128, 16, 64]), ALU.mult, [xs.k, dt.k], [xd_.k])
                    for g in range(2):
                        MM(ps_s[:, 128 + g * 128:256 + g * 128], bT[:, g, :], cT[:, g, :], True, True,
                           [bT.k, cT.k], ["P:ps_s"])
                        TT("dve", cbm[g][:], ps_s[:, 128 + g * 128:256 + g * 128], mle, ALU.mult,
                           ["P:ps_s", cstf.k], [cbm[g].k])
                        MM(ps_yo[g][:], cT[:, g, :], Sb[:, g, :], True, True, [cT.k, Sb.k], [ps_yo[g].k])
                    for hb in range(4):
                        g = hb // 2
                        h0 = hb * 4
                        l_, e2, m_ = lh[hb % 2], eD[hb % 2], Mt[hb % 2]
                        TT("dve", l_[:], cstf[:, 3:4, :].to_broadcast([128, 4, 128]),
                           a_[:, h0:h0 + 4].unsqueeze(2).to_broadcast([128, 4, 128]), ALU.mult, [cstf.k, a_.k], [l_.k])
                        for i in range(4):
                            MM(ps_D[:, i * 128:(i + 1) * 128], l_[:, i, :], mle, True, True, [l_.k, cstf.k], ["P:ps_D"])
                        ACT(e2[:], ps_D[:], AF.Exp, ["P:ps_D"], [e2.k])
                        TT("dve", m_[:], e2[:].rearrange("p (i t) -> p i t", i=4),
                           cbm[g][:].unsqueeze(1).to_broadcast([128, 4, 128]), ALU.mult, [e2.k, cbm[g].k], [m_.k])
                        for i in range(4):
                            h = h0 + i
                            MM(ps_y[g][:, (h % 8) * 64:(h % 8 + 1) * 64], m_[:, i, :], xd_[:, h * 64:(h + 1) * 64], True, True,
                               [m_.k, xd_.k], [ps_y[g].k])
                for g in range(2):
                    MM(ps_st[:], bt[:, g * 128:(g + 1) * 128], xdd_[:, g * 512:(g + 1) * 512], True, True,
                       [bt.k, xdd_.k], [ps_st.k])
                    if full:
                        pass
                    TT("dve", Sf[:, g, :].rearrange("p (h d) -> p h d", h=8), Sf[:, g, :].rearrange("p (h d) -> p h d", h=8),
                       cd_[:, g * 8:(g + 1) * 8].unsqueeze(2).to_broadcast([128, 8, 64]), ALU.mult,
                       [Sf.k, cd_.k], [Sf.k])
                    TT("dve", Sf[:, g, :], Sf[:, g, :], ps_st[:], ALU.add, [Sf.k, ps_st.k], [Sf.k])
                if c == HALO:
                    TS("dve", Sf[:], Sf[:], flg[:, 0:1], ALU.mult, [Sf.k, flg.k], [Sf.k])
                CP("act", Sb[:], Sf[:], [Sf.k], [Sb.k])
                if full:
                    for g in range(2):
                        sl = slice(g * 512, (g + 1) * 512)
                        TT("dve", y1[:, sl].rearrange("p (h d) -> p h d", h=8),
                           ps_yo[g][:].rearrange("p (h d) -> p h d", h=8),
                           e_[:, g * 8:(g + 1) * 8].unsqueeze(2).to_broadcast([128, 8, 64]), ALU.mult,
                           [ps_yo[g].k, e_.k], [y1.k])
                        TT("dve", y1[:, sl], y1[:, sl], ps_y[g][:], ALU.add, [y1.k, ps_y[g].k], [y1.k])
                    TT("dve", y3[:], xs[:], dfl[:], ALU.mult, [xs.k, dfl.k], [y3.k])
                    TT("dve", y3[:], y3[:], y1[:], ALU.add, [y3.k, y1.k], [y3.k])
                    TT("dve", y3[:], y3[:], z[:], ALU.mult, [y3.k, z.k], [y3.k])
                    for g in range(2):
                        ACT(yj[:], y3[:, g * 512:(g + 1) * 512], AF.Square, [y3.k], [yj.k, ss2.k + str(g)],
                            accum_out=ss2[:, g:g + 1])
                    ACT(ss2[:], ss2[:], AF.Sqrt, [ss2.k + "0", ss2.k + "1"], [ss2.k], scale=1.0 / 512, bias=EPS)
                    RECIP(ss2[:], ss2[:], [ss2.k], [ss2.k])
                    for g in range(2):
                        sl = slice(g * 512, (g + 1) * 512)
                        STT(yn[:, sl], y3[:, sl], ss2[:, g:g + 1], snwb[:, sl], ALU.mult, ALU.mult,
                            [y3.k, ss2.k, snwb.k], [yn.k])
                    for kc in range(8):
                        TR(ps_tr[:, kc * 128:(kc + 1) * 128], yn[:, kc * 128:(kc + 1) * 128], identb[:],
                           [yn.k, identb.k], [ps_tr.k])
                    yT = ynT[i2]
                    CP("act", yT[:], ps_tr[:].rearrange("p (k t) -> p k t", k=8), [ps_tr.k], [yT.k])
                    ST(ymix[c - HALO, :, 0:8, :], yT[:], [yT.k], ["ymixA"])
            S.barrier()

        esD = contextlib.ExitStack()
        if upto >= 3:
          woD = load_w_bf16(esD, w_out0, 2048, 1024, None, "wo0", prefetch=True)
          gwD = load_w_bf16(esD, gate_w[0], 1024, 1024, None, "gw0", prefetch=True)
          pwD = load_w_bf16(esD, ple_w[0], 256, 1024, None, "pw0", prefetch=True)
          with contextlib.ExitStack() as es:
            lamb = mk(es, "lamb", [128, 256], F32)
            LD(lamb[:], lamv.partition_broadcast(128), [], [lamb.k])
            lt = mk(es, "lt", [128, 128], F32)
            lsum = mk(es, "lsum", [128, 2], F32)
            neglam = mk(es, "neglam", [128, 1], F32)
            for i in range(2):
                S.emit("dve", lambda e, i=i: e.tensor_tensor_reduce(
                    out=lt[:, i * 64:(i + 1) * 64], in0=lamb[:, i * 128:i * 128 + 64], in1=lamb[:, i * 128 + 64:i * 128 + 128],
                    op0=ALU.mult, op1=ALU.add, scale=1.0, scalar=0.0, accum_out=lsum[:, i:i + 1]) if False else
                    e.tensor_tensor(out=lt[:, i * 64:(i + 1) * 64], in0=lamb[:, i * 128:i * 128 + 64],
                                    in1=lamb[:, i * 128 + 64:i * 128 + 128], op=ALU.mult), [lamb.k], [lt.k])
                S.emit("dve", lambda e, i=i: e.reduce_sum(out=lsum[:, i:i + 1], in_=lt[:, i * 64:(i + 1) * 64],
                                                         axis=mybir.AxisListType.X), [lt.k], [lsum.k])
            ACT(lsum[:], lsum[:], AF.Exp, [lsum.k], [lsum.k])
            STT(neglam[:], lsum[:, 1:2], -0.2, lsum[:, 0:1], ALU.add, ALU.subtract, [lsum.k], [neglam.k])
            subc = mk(es, "subc", [128, 1], F32)
            LD(subc[:], subw, [], [subc.k])
            TS("dve", subc[:], subc[:], 0.8, ALU.mult, [subc.k], [subc.k])
            Kh = [mk(es, "Kh", [128, 8192], BF16) for _ in range(2)]
            Vh = [mk(es, "Vh", [128, NBLK, 128], BF16) for _ in range(2)]
            Qh = [mk(es, "Qh", [128, NHB * 128], BF16) for _ in range(2)]
            Pt = [mk(es, "Pt", [128, 2, 512], BF16) for _ in range(NPT)]
            accs = [mk(es, "accs", [128, 2, 512], F32) for _ in range(2)]
            tmps = [mk(es, "tmps", [128, 2, 512], BF16) for _ in range(2)]
            gst = [mk(es, "gst", [128, 512], BF16) for _ in range(2)]
            r12 = mk(es, "r12", [128, 2, 512], F32)
            o12 = mk(es, "o12", [128, 2, 512], F32)
            osq = mk(es, "osq", [128, 512], F32)
            og = [mk(es, "og", [128, 512], BF16) for _ in range(2)]
            psS = [mk(es, "psS", [128, 2, 512], F32, psum=True) for _ in range(NPS)]
            psO = [mk(es, "psO", [128, 2, 512], F32, psum=True) for _ in range(4 - NPS)]
            nmask = mk(es, "nmask", [128, 128], BF16)
            TS("dve", nmask[:], mgt, -30000.0, ALU.mult, [cstf.k], [nmask.k])
            un = 0
            nch = 0
            fin2 = []

            def load_head(h):
                K, V, Q = Kh[h % 2], Vh[h % 2], Qh[h % 2]
                LD(K[:], kT[h], ["kT"], [K.k])
                LD(V[:], vS[:, :, h * 128:(h + 1) * 128].rearrange("b p e -> p b e"), ["vS"], [V.k])
                LD(Q[:], qT[h, :, (HALO - FB0) * 128:], ["qT"], [Q.k])
            load_head(0)
            all_chunks = []
            for h in range(8):
                chunks = [(0, 128, [(kb, "pre0") for kb in range(HALO)] + [(HALO, "diag0")], 0)]
                for qc in range(8):
                    lst = [(kb, "pre") for kb in range(32)]
                    lst += [(32 + kb, "full") for kb in range(4 * qc)]
                    lst += [(32 + 4 * qc + r, f"diag{r}") for r in range(4)]
                    chunks.append((128 + qc * 512, 512, lst, 1 + qc * 4))
                for ci, ch in enumerate(chunks):
                    all_chunks.append((h, ci) + ch)

            def emit_qk(K, Q, q0, qw, lst, ik):
                nonlocal un
                kb, kind = lst[ik]
                c0 = int(kind[4:]) * 128 if kind.startswith("diag") else 0
                wcol = qw - c0
                sS = psS[un % NPS]
                pP = Pt[un % NPT]
                un += 1
                dg_ = kind.startswith("diag")
                MM(sS[:, 0, 0:wcol], K[0:64, kb * 128:(kb + 1) * 128], Q[0:64, q0 + c0:q0 + qw], True, not dg_,
                   [K.k, Q.k], [sS.k], sig=False)
                MM(sS[:, 1, 0:wcol], K[64:128, kb * 128:(kb + 1) * 128], Q[64:128, q0 + c0:q0 + qw], True, not dg_,
                   [K.k, Q.k], [sS.k], sig=not dg_)
                if dg_:
                    MM(sS[:, 0, 0:128], identb[:], nmask[:], False, True, [identb.k, nmask.k], [sS.k], sig=False)
                    MM(sS[:, 1, 0:128], identb[:], nmask[:], False, True, [identb.k, nmask.k], [sS.k])
                return (ik, kb, kind, c0, wcol, sS, pP)

            pre = None
            if True:
                for idx, (h, ci, q0, qw, lst, yb0) in enumerate(all_chunks):
                    K, V, Q = Kh[h % 2], Vh[h % 2], Qh[h % 2]
                    if ci == 1 and h + 1 < 8:
                        load_head(h + 1)
                    gs_ = gst[nch % 2]
                    acc = accs[nch % 2]
                    started = set()
                    hold = {"n": 0}
                    nfull = sum(1 for (_, kd) in lst if not kd.startswith("diag"))
                    pO = psO[nch % (4 - NPS)]
                    nch += 1
                    LD(gs_[:, 0:qw], gsT[h, :, (HALO - FB0) * 128 + q0:(HALO - FB0) * 128 + q0 + qw], ["gsT"], [gs_.k])
                    nk = len(lst)

                    def emit_rest(info):
                        ik, kb, kind, c0, wcol, sS, pP = info
                        if kind == "pre":
                            ACT(pP[:, :, 0:wcol], sS[:, :, 0:wcol], AF.Exp, [sS.k, flg.k], [pP.k], scale=0.125,
                                bias=flg[:, 1:2])
                        else:
                            ACT(pP[:, :, 0:wcol], sS[:, :, 0:wcol], AF.Exp, [sS.k], [pP.k], scale=0.125)
                        st, sp_ = (ik == 0), (ik == nk - 1)
                        MM(pO[:, 0, c0:qw], V[:, kb, :], pP[:, 0, 0:wcol], st, sp_, [V.k, pP.k], [pO.k], sig=False)
                        MM(pO[:, 1, c0:qw], V[:, kb, :], pP[:, 1, 0:wcol], st, sp_, [V.k, pP.k], [pO.k])
                        def acc_add(src, lo, hi, c_lo):
                            if acc.k not in started:
                                started.add(acc.k)
                                assert c_lo == 0
                                CP("dve", acc[:, :, 0:qw], src[:, :, 0:qw], [src.k], [acc.k])
                            else:
                                TT("dve", acc[:, :, c_lo:qw], acc[:, :, c_lo:qw], src[:, :, lo:hi], ALU.add,
                                   [acc.k, src.k], [acc.k])
                        if kind.startswith("diag"):
                            acc_add(pP, 0, wcol, c0)
                        else:
                            gpos = ik % 4
                            last_full = (ik == nfull - 1)
                            if gpos == 0:
                                if last_full:
                                    acc_add(pP, 0, qw, 0)
                                else:
                                    hold["p"] = pP
                            elif gpos == 1:
                                hold["t"] = tmps[hold["n"] % 2]
                                hold["n"] += 1
                                TT("dve", hold["t"][:, :, 0:qw], hold["p"][:, :, 0:qw], pP[:, :, 0:qw], ALU.add,
                                   [hold["p"].k, pP.k], [hold["t"].k])
                                if last_full:
                                    acc_add(hold["t"], 0, qw, 0)
                            else:
                                t_ = hold["t"]
                                TT("dve", t_[:, :, 0:qw], t_[:, :, 0:qw], pP[:, :, 0:qw], ALU.add, [t_.k, pP.k], [t_.k])
                                if gpos == 3 or last_full:
                                    acc_add(t_, 0, qw, 0)

                    infos = pre if pre is not None else [emit_qk(K, Q, q0, qw, lst, ik) for ik in range(QKD)]
                    pre = None
                    for ik in range(nk):
                        if ik + QKD < nk:
                            infos.append(emit_qk(K, Q, q0, qw, lst, ik + QKD))
                        emit_rest(infos[ik])
                        while fin2 and fin2[0][0] <= ik:
                            fin2.pop(0)[1]()
                    if idx + 1 < len(all_chunks):
                        h2, ci2, q02, qw2, lst2, _ = all_chunks[idx + 1]
                        pre = [emit_qk(Kh[h2 % 2], Qh[h2 % 2], q02, qw2, lst2, ik) for ik in range(QKD)]
                    pD = psS[un % NPS]
                    for mp in range(2):
                        MM(pD[:, mp, 0:qw], onesf, acc[:, mp, 0:qw], True, True, [cstf.k, acc.k], [pD.k])
                    S.emit("dve", lambda e, pO=pO, qw=qw: e.tensor_copy(out=o12[:, :, 0:qw], in_=pO[:, :, 0:qw]), [pO.k], [o12.k])
                    S.emit("dve", lambda e, pD=pD, qw=qw: e.tensor_copy(out=r12[:, :, 0:qw], in_=pD[:, :, 0:qw]), [pD.k], [r12.k])
                    w_ = slice(0, qw)

                    def st2a(qw=qw, mp=0):
                        S.emit("dve", lambda e: e.reciprocal(out=r12[:, mp, 0:qw], in_=r12[:, mp, 0:qw]), [r12.k], [r12.k])

                    def st2a1(qw=qw):
                        st2a(qw, 1)

                    def st2b(qw=qw, w_=w_):
                        TT("dve", o12[:, :, 0:qw], o12[:, :, 0:qw], r12[:, :, 0:qw], ALU.mult, [o12.k, r12.k], [o12.k])
                        STT(o12[:, 0, w_], o12[:, 1, w_], neglam[:, 0:1], o12[:, 0, w_], ALU.mult, ALU.add,
                            [neglam.k, o12.k], [o12.k])
                        TT("dve", osq[:, w_], o12[:, 0, w_], o12[:, 0, w_], ALU.mult, [o12.k], [osq.k])

                    def st2c(w_=w_, qw=qw, gs_=gs_, yb0=yb0, h=h, og_=og[nch % 2]):
                        pss = psS[un % NPS]
                        MM(pss[:, 0, w_], onesf, osq[:, w_], True, True, [cstf.k, osq.k], [pss.k])
                        CP("dve", r12[:, 1, w_], pss[:, 0, w_], [pss.k], [r12.k])

                    def st2d(w_=w_, qw=qw, gs_=gs_, yb0=yb0, h=h, og_=og[nch % 2]):
                        ACT(r12[:, 0, w_], r12[:, 1, w_], AF.Ln, [r12.k], [r12.k], scale=1.0 / 128, bias=EPS)
                        ACT(r12[:, 0, w_], r12[:, 0, w_], AF.Exp, [r12.k], [r12.k], scale=-0.5)
                        TT("dve", o12[:, 0, w_], o12[:, 0, w_], r12[:, 0, w_], ALU.mult, [o12.k, r12.k], [o12.k])
                        STT(og_[:, w_], o12[:, 0, w_], subc[:, 0:1], gs_[:, w_], ALU.mult, ALU.mult,
                            [o12.k, subc.k, gs_.k], [og_.k])
                        nb = qw // 128
                        ST(ymix[yb0:yb0 + nb, :, 8 + h, :].rearrange("b p t -> p b t"),
                           og_[:, w_].rearrange("p (b t) -> p b t", b=nb), [og_.k], ["ymixB"])
                    fin2.extend([(4, st2a), (8, st2a1), (12, st2b), (16, st2c), (21, st2d)])
            while fin2:
                fin2.pop(0)[1]()
            S.barrier()

        def ple_tail(es_bufs, li, ps_m, hres, pblk, w, mid=None):
            hp, hb, hT, pb, pT, sg, ps_g, ps_e, ps_trl = (es_bufs[k] for k in
                                                         ("hp", "hb", "hT", "pb", "pT", "sg", "ps_g", "ps_e", "ps_tr"))
            gatew, plew = w
            for hf in range(2):
                sl = slice(hf * 512, (hf + 1) * 512)
                TT("dve", hp[:, sl], ps_m[hf][:], hres[:, sl], ALU.add, [ps_m[hf].k, hres.k], [hp.k])
            if mid is not None:
                mid()
            CP("act", hb[:], hp[:], [hp.k], [hb.k])
            CP("pool", pb[:], pblk[:], [pblk.k], [pb.k])
            pt_ = ps_trl[0]
            for kc in range(8):
                TR(pt_[:, kc * 128:(kc + 1) * 128], hb[:, kc * 128:(kc + 1) * 128], identb[:], [hb.k, identb.k], [pt_.k])
            CP("act", hT[:], pt_[:].rearrange("p (k t) -> p k t", k=8), [pt_.k], [hT.k])
            pt2 = ps_trl[1]
            for kc in range(2):
                TR(pt2[:, kc * 128:(kc + 1) * 128], pb[:, kc * 128:(kc + 1) * 128], identb[:], [pb.k, identb.k], [pt2.k])
            CP("dve", pT[:], pt2[:, 0:256].rearrange("p (k t) -> p k t", k=2), [pt2.k], [pT.k])
            for hf in range(2):
                sl = slice(hf * 512, (hf + 1) * 512)
                for kc in range(8):
                    MM(ps_g[hf][:], hT[:, kc, :], gatew[:, kc, sl], kc == 0, kc == 7, [hT.k, gatew.k], [ps_g[hf].k], sig=(kc == 7))
                for kc in range(2):
                    MM(ps_e[hf][:], pT[:, kc, :], plew[:, kc, sl], kc == 0, kc == 1, [pT.k, plew.k], [ps_e[hf].k], sig=(kc == 1))
                ACT(sg[:, sl], ps_g[hf][:], AF.Sigmoid, [ps_g[hf].k], [sg.k])
                TT("dve", sg[:, sl], ps_e[hf][:], sg[:, sl], ALU.mult, [ps_e[hf].k, sg.k], [sg.k])
            TT("dve", hp[:], hp[:], sg[:], ALU.add, [hp.k, sg.k], [hp.k])
            return hp

        def tail_bufs(es):
            d = {}
            d["hp"] = mk(es, "hp", [128, 1024], F32)
            d["hb"] = mk(es, "hb", [128, 1024], BF16)
            d["hT"] = mk(es, "hT", [128, 8, 128], BF16)
            d["pb"] = mk(es, "pb", [128, 256], BF16)
            d["pT"] = mk(es, "pT", [128, 2, 128], BF16)
            d["sg"] = mk(es, "sg", [128, 1024], F32)
            return d

        if upto >= 4:
          with contextlib.ExitStack() as es:
            wo, gw, pw = woD, gwD, pwD
            bufs = tail_bufs(es)
            bufs["ps_g"] = [mk(es, "ps_g", [128, 512], F32, psum=True) for _ in range(2)]
            bufs["ps_e"] = [mk(es, "ps_e", [128, 512], F32, psum=True) for _ in range(2)]
            bufs["ps_tr"] = [mk(es, "ps_tr", [128, 1024], BF16, psum=True) for _ in range(2)]
            ps_m = [mk(es, "ps_m", [128, 512], F32, psum=True) for _ in range(2)]
            ym = [mk(es, "ym", [128, 16, 128], BF16) for _ in range(2)]
            xr = [mk(es, "xr", [128, 1024], F32) for _ in range(2)]
            pr = [mk(es, "pr", [128, 256], F32) for _ in range(2)]
            junk = mk(es, "junkD", [128, 1024], BF16)
            ss = mk(es, "ssD", [128, 1], F32)
            xn1 = mk(es, "xn1", [128, 1024], BF16)
            hnT1 = [mk(es, "hnT1", [128, 8, 128], BF16) for _ in range(2)]
            def outprojD(b):
                i2 = b % 2
                LD(ym[i2][:], ymix[b], ["ymixA", "ymixB"], [ym[i2].k])
                LD(xr[i2][:], x_loc[HALO + b], [], [xr[i2].k])
                LD(pr[i2][:], p0[b], [], [pr[i2].k])
                for hf in range(2):
                    for kc in range(16):
                        MM(ps_m[hf][:], ym[i2][:, kc, :], wo[:, kc, hf * 512:(hf + 1) * 512], kc == 0, kc == 15,
                           [ym[i2].k, wo.k], [ps_m[hf].k], sig=(kc == 15))
            outprojD(0)
            for b in range(NHB):
                i2 = b % 2
                hp = ple_tail(bufs, 0, ps_m, xr[i2], pr[i2], (gw, pw),
                              mid=(lambda b=b: outprojD(b + 1)) if b + 1 < NHB else None)
                ST(h1S[b], hp[:], [hp.k], ["h1S"])
                ACT(junk[:], hp[:], AF.Square, [hp.k], [junk.k, ss.k], accum_out=ss[:])
                rstd_from_ss(ss, 1024)
                TS("dve", xn1[:], hp[:], ss[:, 0:1], ALU.mult, [hp.k, ss.k], [xn1.k])
                pt_ = bufs["ps_tr"][0]
                for kc in range(8):
                    TR(pt_[:, kc * 128:(kc + 1) * 128], xn1[:, kc * 128:(kc + 1) * 128], identb[:],
                       [xn1.k, identb.k], [pt_.k])
                CP("act", hnT1[i2][:], pt_[:].rearrange("p (k t) -> p k t", k=8), [pt_.k], [hnT1[i2].k])
                ST(hn1T[b], hnT1[i2][:], [hnT1[i2].k], ["hn1T"])
            S.barrier()

        esD.close()
        esE = contextlib.ExitStack()
        if upto >= 5:
          woE = load_w_bf16(esE, w_out1, 2048, 1024, None, "wo1", prefetch=True)
          gwE = load_w_bf16(esE, gate_w[1], 1024, 1024, None, "gw1", prefetch=True)
          pwE = load_w_bf16(esE, ple_w[1], 256, 1024, None, "pw1", prefetch=True)
          with contextlib.ExitStack() as es:
            nwc1 = mk(es, "nwc1", [128, 8], F32)
            LD(nwc1[:], nw[1].rearrange("(k p) o -> p (k o)", p=128), [], [nwc1.k])
            W1 = load_w_bf16(es, w_in1, 1024, 6144, nwc1, "W1")
            hg = [mk(es, "hg", [128, 8, 512], BF16) for _ in range(2)]
            sgm = [mk(es, "sgm", [128, 512], F32) for _ in range(2)]
            glu = [mk(es, "glu", [128, 512], BF16) for _ in range(2)]
            sgo = [mk(es, "sgo", [128, 512], BF16) for _ in range(2)]
            pacc = [mk(es, "paccE", [128, 512], F32, psum=True) for _ in range(4)]
            na = 0
            groups = [(0, 1)] + [(1 + 4 * G, 4) for G in range(8)]
            for gi, (b0, nb) in enumerate(groups):
                tw = nb * 128
                hg_ = hg[gi % 2]
                for bi in range(nb):
                    LD(hg_[:, :, bi * 128:(bi + 1) * 128], hn1T[b0 + bi], ["hn1T"], [hg_.k])

                def proj(col0):
                    nonlocal na
                    pa = pacc[na % 4]
                    na += 1
                    for kc in range(8):
                        MM(pa[:, 0:tw], W1[:, kc, col0:col0 + 128], hg_[:, kc, 0:tw], kc == 0, kc == 7,
                           [W1.k, hg_.k], [pa.k], sig=(kc == 7))
                    return pa
                for j in range(16):
                    pu = proj(j * 128)
                    pg = proj(2048 + j * 128)
                    sg_, gl_ = sgm[j % 2], glu[j % 2]
                    ACT(sg_[:, 0:tw], pg[:, 0:tw], AF.Sigmoid, [pg.k], [sg_.k])
                    if gi == 0:
                        STT(gl_[:, 0:tw], pu[:, 0:tw], flg[:, 0:1], sg_[:, 0:tw], ALU.mult, ALU.mult,
                            [pu.k, flg.k, sg_.k], [gl_.k])
                    else:
                        TT("dve", gl_[:, 0:tw], pu[:, 0:tw], sg_[:, 0:tw], ALU.mult, [pu.k, sg_.k], [gl_.k])
                    ST(gluT[j, :, b0 * 128:b0 * 128 + tw], gl_[:, 0:tw], [gl_.k], ["gluT"])
                if gi > 0:
                    for j in range(16):
                        pg = proj(4096 + j * 128)
                        so = sgo[j % 2]
                        ACT(so[:, 0:tw], pg[:, 0:tw], AF.Silu, [pg.k], [so.k])
                        ST(sg1T[j, :, (b0 - 1) * 128:(b0 - 1) * 128 + tw], so[:, 0:tw], [so.k], ["sg1T"])
            S.barrier()

        if upto >= 6:
          with contextlib.ExitStack() as es:
            wo, gw, pw = woE, gwE, pwE
            ccwc = mk(es, "ccwc", [128, 16, 31], F32)
            LD(ccwc[:], ccw.rearrange("(c p) j -> p c j", p=128), [], [ccwc.k])
            colp = mk(es, "colp", [128, 3, 16], F32)
            for i, src in enumerate((ccb, lnw, lnb)):
                LD(colp[:, i, :], src.rearrange("(c p) o -> p (c o)", p=128), [], [colp.k + str(i)])
            fnb = mk(es, "fnb", [128, 1024], F32)
            LD(fnb[:], fnw.partition_broadcast(128), [], [fnb.k])
            bufs = tail_bufs(es)
            psb = [mk(es, "psb", [128, 512], F32, psum=True) for _ in range(6)]
            bufs["ps_g"] = [psb[2], psb[3]]
            bufs["ps_e"] = [psb[4], psb[5]]
            bufs["ps_tr"] = [mk(es, "ps_tr", [128, 1024], BF16, psum=True) for _ in range(2)]
            ps_m = [psb[0], psb[1]]
            gl = mk(es, "gl", [128, 16, 544], BF16)
            sgg = mk(es, "sgg", [128, 16, 512], BF16)
            cv = mk(es, "cv", [128, 16, 512], F32)
            dg = [mk(es, "dg", [128, 31, 128], BF16) for _ in range(2)]
            sqs = [mk(es, "sqs", [128, 512], F32) for _ in range(2)]
            mean = mk(es, "mean", [128, 512], F32)
            var = mk(es, "var", [128, 512], F32)
            tn = [mk(es, "tn", [128, 512], F32) for _ in range(2)]
            aT = sgg
            hr = [mk(es, "hr", [128, 1024], F32) for _ in range(2)]
            pr = [mk(es, "pr1", [128, 256], F32) for _ in range(2)]
            junk = mk(es, "junkE", [128, 1024], BF16)
            ss = mk(es, "ssE", [128, 1], F32)
            ot = [mk(es, "ot", [128, 1024], F32) for _ in range(2)]
            def load_gl(G):
                c0 = 128 + G * 512 - 32
                LD(gl[:], gluT[:, :, c0:c0 + 544].rearrange("j p t -> p j t"), ["gluT"], [gl.k])
            load_gl(0)
            for G in range(8):
                LD(sgg[:], sg1T[:, :, G * 512:(G + 1) * 512].rearrange("j p t -> p j t"), ["sg1T"],
                   [sgg.k + str(j) for j in range(16)])

                def stats(j):
                    sq_ = sqs[j % 2]
                    ACT(sq_[:], cv[:, j, :], AF.Square, [cv.k + str(j)], [sq_.k])
                    MM(psb[2][:], onesf, cv[:, j, :], j == 0, j == 15, [cstf.k, cv.k + str(j)], [psb[2].k], sig=False)
                    MM(psb[3][:], onesf, sq_[:], j == 0, j == 15, [cstf.k, sq_.k], [psb[3].k])
                for j in range(16):
                    d_ = dg[j % 2]
                    TT("dve", d_[:], identb[:].unsqueeze(1).to_broadcast([128, 31, 128]),
                       ccwc[:, j, :].unsqueeze(2).to_broadcast([128, 31, 128]), ALU.mult, [identb.k, ccwc.k], [d_.k])
                    pc = psb[j % 2]
                    for tp in range(31):
                        MM(pc[:], d_[:, tp, :], gl[:, j, 2 + tp:2 + tp + 512], tp == 0, tp == 30, [d_.k, gl.k], [pc.k], sig=(tp == 30))
                    ACT(cv[:, j, :], pc[:], AF.Identity, [pc.k, colp.k + "0"], [cv.k + str(j)], bias=colp[:, 0, j:j + 1])
                    if j > 0:
                        stats(j - 1)
                stats(15)
                if G + 1 < 8:
                    load_gl(G + 1)
                ACT(mean[:], psb[2][:], AF.Copy, [psb[2].k], [mean.k], scale=1.0 / 2048)
                TT("dve", var[:], mean[:], mean[:], ALU.mult, [mean.k], [var.k])
                STT(var[:], psb[3][:], 1.0 / 2048, var[:], ALU.mult, ALU.subtract, [psb[3].k, var.k], [var.k])
                ACT(var[:], var[:], AF.Sqrt, [var.k], [var.k], bias=EPS)
                RECIP(var[:], var[:], [var.k], [var.k])
                for j in range(16):
                    t_ = tn[j % 2]
                    TT("dve", t_[:], cv[:, j, :], mean[:], ALU.subtract, [cv.k + str(j), mean.k], [t_.k])
                    TT("dve", t_[:], t_[:], var[:], ALU.mult, [t_.k, var.k], [t_.k])
                    ACT(t_[:], t_[:], AF.Silu, [t_.k, colp.k + "1", colp.k + "2"], [t_.k],
                        scale=colp[:, 1, j:j + 1], bias=colp[:, 2, j:j + 1])
                    TT("dve", aT[:, j, :], t_[:], sgg[:, j, :], ALU.mult, [t_.k, sgg.k + str(j)], [sgg.k + str(j)])
                def outprojE(bi, G=G):
                    b = G * 4 + bi
                    i2 = bi % 2
                    LD(hr[i2][:], h1S[1 + b], ["h1S"], [hr[i2].k])
                    LD(pr[i2][:], p1[b], [], [pr[i2].k])
                    for hf in range(2):
                        for kc in range(16):
                            MM(ps_m[hf][:], aT[:, kc, bi * 128:(bi + 1) * 128], wo[:, kc, hf * 512:(hf + 1) * 512],
                               kc == 0, kc == 15, [aT.k + str(kc), wo.k], [ps_m[hf].k], sig=(kc == 15))
                outprojE(0)
                for bi in range(4):
                    b = G * 4 + bi
                    i2 = bi % 2
                    hp = ple_tail(bufs, 1, ps_m, hr[i2], pr[i2], (gw, pw),
                                  mid=(lambda bi=bi: outprojE(bi + 1)) if bi + 1 < 4 else None)
                    ACT(junk[:], hp[:], AF.Square, [hp.k], [junk.k, ss.k], accum_out=ss[:])
                    rstd_from_ss(ss, 1024)
                    STT(ot[i2][:], hp[:], ss[:, 0:1], fnb[:], ALU.mult, ALU.mult, [hp.k, ss.k, fnb.k], [ot[i2].k])
                    ST(out[b], ot[i2][:], [ot[i2].k], ["out"])
            S.barrier()
        esE.close()
        S.barrier()
    S.finish()
    return nc


def _consts():
    ident = np.eye(128, dtype=np.float32)
    perm = np.zeros((128, 128), np.float32)
    for m in range(128):
        d = m % 64
        k = m + 32 if d < 32 else m - 32
        perm[k, m] = 1.0
    j = np.arange(128)
    mle = (j[:, None] <= j[None, :]).astype(np.float32)
    mgt = (j[:, None] > j[None, :]).astype(np.float32)
    ones = np.ones((128, 128), np.float32)
    return np.stack([ident, perm, mle, mgt, ones])


def _rope_tables(pos):
    inv = (10000.0 ** (-np.arange(0, 64, 2, dtype=np.float32) / 64)).astype(np.float32)
    ang = pos.astype(np.float32)[None, :] * inv[:, None]
    c = np.cos(ang).astype(np.float32)
    s = np.sin(ang).astype(np.float32)
    cos128 = np.concatenate([c, c, c, c], axis=0)
    sin128 = np.concatenate([-s, s, -s, s], axis=0)
    return np.ascontiguousarray(cos128), np.ascontiguousarray(sin128)


def make_in_maps(inp, cores=range(8)):
    f = lambda a: np.ascontiguousarray(np.asarray(a, dtype=np.float32))
    x = np.asarray(inp["x"], np.float32)
    p = np.asarray(inp["p"], np.float32)
    shared = {
        "cst": _consts(),
        "w_in0": f(inp["even_w_in"][0]),
        "nw": f(np.asarray(inp["norm_w"])[:, :, None]),
        "cw0": f(np.asarray(inp["ssd_conv_w"])[0].T),
        "cb0": f(np.asarray(inp["ssd_conv_b"])[0][:, None]),
        "dtb": f(np.asarray(inp["ssd_dt_bias"])[0][None]),
        "alog": f(np.asarray(inp["ssd_a_log"])[0][None]),
        "dfull": f(np.repeat(np.asarray(inp["ssd_d"])[0], 64)[None]),
        "snw": f(np.asarray(inp["ssd_norm_w"])[0][None]),
        "lamv": f(np.asarray(inp["diff_lambda"])[0].reshape(1, 256)),
        "subw": f(np.asarray(inp["diff_subln_w"])[0][:, None]),
        "w_out0": f(inp["even_w_out"][0]),
        "ple_w": f(inp["ple_w"]),
        "gate_w": f(inp["ple_gate_w"]),
        "w_in1": f(inp["conf_w_in"][0]),
        "ccw": f(np.asarray(inp["conf_conv_w"])[0].T),
        "ccb": f(np.asarray(inp["conf_conv_b"])[0][:, None]),
        "lnw": f(np.asarray(inp["conf_ln_w"])[0][:, None]),
        "lnb": f(np.asarray(inp["conf_ln_b"])[0][:, None]),
        "w_out1": f(inp["conf_w_out"][0]),
        "fnw": f(np.asarray(inp["final_norm_w"])[None]),
    }
    maps = []
    for c in cores:
        b, s = c // 2, c % 2
        t0 = s * 4096
        xl = np.zeros((8192, 1024), np.float32)
        p0 = np.zeros((NHB * 128, 256), np.float32)
        if s == 1:
            xl[:] = x[b]
            p0[:] = p[0, b, 4096 - 128:8192]
        else:
            xl[4096:] = x[b, 0:4096]
            p0[128:] = p[0, b, 0:4096]
        p1 = p[1, b, t0:t0 + 4096]
        pos = np.arange(8192) + (t0 - 4096)
        cs, sn = _rope_tables(pos)
        fl = np.zeros((128, 2), np.float32)
        fl[:, 0] = 1.0 if s == 1 else 0.0
        fl[:, 1] = 0.0 if s == 1 else -30000.0
        m = dict(shared)
        m.update({
            "x_loc": xl.reshape(NBLK, 128, 1024),
            "p0": p0.reshape(NHB, 128, 256),
            "p1": f(p1).reshape(32, 128, 256),
            "flagc": fl, "cosT": cs, "sinT": sn,
        })
        maps.append(m)
    return maps


def kernel(**inputs):
    nc = bass.Bass("TRN2", target_bir_lowering=False)
    build(nc)
    maps = make_in_maps(inputs)
    res = run_bass_kernel_spmd(nc, maps, core_ids=list(range(8)))
    out = np.zeros((4, 8192, 1024), np.float32)
    for c in range(8):
        b, s = c // 2, c % 2
        out[b, s * 4096:(s + 1) * 4096] = np.asarray(res.results[c]["out"]).reshape(4096, 1024)
    return out
```

```python
import contextlib
import math
import numpy as np
import concourse.bass as bass
import concourse.mybir as mybir
from concourse.bass_utils import run_bass_kernel_spmd

F32 = mybir.dt.float32
BF16 = mybir.dt.bfloat16
AF = mybir.ActivationFunctionType
ALU = mybir.AluOpType

EPS = 1e-6
NBLK = 64
HALO = 31
FB0 = 28
NFB = NBLK - FB0
NHB = NBLK - HALO
import os
NGDBG = int(os.environ.get('K_NG', '16'))
KSTOP = int(os.environ.get('K_STOP', '99'))
NPT = int(os.environ.get('K_NPT', '6'))
QKD = int(os.environ.get('K_QKD', '2'))
EMBED_WAIT = int(os.environ.get('K_EMB', '2'))
NPS = int(os.environ.get('K_NPS', '3'))


class Sched:
    def __init__(self, nc, n_dma_sems=14):
        self.nc = nc
        self.engs = ("pe", "act", "dve", "pool", "sp")
        self.ops = {k: [] for k in self.engs}
        self.psem = {k: nc.alloc_semaphore(f"prog_{k}") for k in ("pe", "act", "dve", "pool")}
        self.cnt = {k: 0 for k in self.psem}
        self.waited = {k: {} for k in self.engs}
        self.rings = {q: [[nc.alloc_semaphore(f"dma_{q}_{i}"), 0] for i in range(n_dma_sems)]
                      for q in ("sp", "pool", "act")}
        self.ring_pos = {q: 0 for q in self.rings}
        self.last_w = {}
        self.readers = {}
        self.sems = {}
        self.unsignaled = {k: False for k in self.engs}

    def _waits(self, eng, deps):
        need = {}
        for (sem, val, src) in deps:
            if src == "pe" and eng == "pe":
                continue
            sid = id(sem)
            self.sems[sid] = sem
            if self.waited[eng].get(sid, 0) >= val:
                continue
            if need.get(sid, 0) < val:
                need[sid] = val
        out = []
        for sid, val in need.items():
            self.waited[eng][sid] = val
            out.append((self.sems[sid], val))
        return out

    def _deps(self, reads, writes):
        deps = []
        for k in reads:
            t = self.last_w.get(k)
            if t is not None:
                deps.append(t)
        for k in writes:
            t = self.last_w.get(k)
            if t is not None:
                deps.append(t)
            deps.extend(self.readers.get(k, {}).values())
        return deps

    def _record(self, tok, reads, writes):
        sid = id(tok[0])
        for k in writes:
            self.last_w[k] = tok
            self.readers[k] = {}
        for k in reads:
            self.readers.setdefault(k, {})[sid] = tok

    def emit(self, eng, fn, reads=(), writes=(), signal=True):
        xr = [k for k in reads if k.startswith("P:")]
        if xr:
            writes = list(writes) + xr
            reads = [k for k in reads if not k.startswith("P:")]
        waits = self._waits(eng, self._deps(reads, writes))
        if signal:
            self.cnt[eng] += 1
            tok = (self.psem[eng], self.cnt[eng], eng)
            self.ops[eng].append((waits, fn, (self.psem[eng], 1)))
            self.unsignaled[eng] = False
        else:
            tok = (self.psem[eng], self.cnt[eng] + 1, eng)
            self.ops[eng].append((waits, fn, None))
            self.unsignaled[eng] = True
        self._record(tok, reads, writes)

    def dma(self, q, out, in_, reads=(), writes=(), **kw):
        ring = self.rings[q]
        i = self.ring_pos[q]
        self.ring_pos[q] = (i + 1) % len(ring)
        sem, c = ring[i]
        deps = self._deps(reads, writes)
        if c > 0:
            deps.append((sem, c, None))
        waits = self._waits(q, deps)
        ring[i][1] = c + 16
        tok = (sem, c + 16, None)
        self.ops[q].append((waits, lambda e: e.dma_start(out=out, in_=in_, **kw), (sem, 16)))
        self._record(tok, reads, writes)

    def barrier(self):
        assert not any(self.unsignaled.values()), self.unsignaled
        toks = [(self.psem[k], self.cnt[k], None) for k in self.psem if self.cnt[k] > 0]
        for q in self.rings:
            for sem, c in self.rings[q]:
                if c > 0:
                    toks.append((sem, c, None))
        for e in self.engs:
            waits = self._waits(e, toks)
            if waits:
                self.ops[e].append((waits, None, None))
        self.last_w = {}
        self.readers = {}

    def finish(self):
        nc = self.nc
        with nc.allow_non_contiguous_dma(reason="small strided parameter loads"), nc.Block() as block:
            def replay(name):
                def f(e):
                    emb = EMBED_WAIT and name != "sp" or EMBED_WAIT == 2
                    for waits, fn, inc in self.ops[name]:
                        if fn is None or not emb or not waits:
                            for sem, val in waits:
                                e.wait_ge(sem, val)
                            if fn is not None:
                                ins = fn(e)
                                if inc is not None:
                                    ins.then_inc(inc[0], inc[1])
                        else:
                            for sem, val in waits[:-1]:
                                e.wait_ge(sem, val)
                            ins = fn(e)
                            ins._wait_ge(waits[-1][0], waits[-1][1])
                            if inc is not None:
                                ins.then_inc(inc[0], inc[1])
                return f
            block.sync(replay("sp"))
            block.scalar(replay("act"))
            block.vector(replay("dve"))
            block.gpsimd(replay("pool"))
            block.tensor(replay("pe"))


class B:
    def __init__(self, t, k):
        self.t = t
        self.k = k

    def __getitem__(self, idx):
        return self.t[idx]


def build(nc, dbg=False, upto=99):
    S = Sched(nc)
    kind_s = "ExternalOutput" if dbg else "Internal"

    def din(name, shape, dt=F32):
        return nc.dram_tensor(name, list(shape), dt, kind="ExternalInput").ap()

    def dscr(name, shape, dt):
        return nc.dram_tensor(name, list(shape), dt, kind=kind_s).ap()

    x_loc = din("x_loc", [NBLK, 128, 1024])
    p0 = din("p0", [NHB, 128, 256])
    p1 = din("p1", [32, 128, 256])
    flagc = din("flagc", [128, 2])
    cosT = din("cosT", [128, 8192])
    sinT = din("sinT", [128, 8192])
    cst = din("cst", [5, 128, 128])
    w_in0 = din("w_in0", [1024, 6672])
    nw = din("nw", [2, 1024, 1])
    cw0 = din("cw0", [1536, 4])
    cb0 = din("cb0", [1536, 1])
    dtb = din("dtb", [1, 16])
    alog = din("alog", [1, 16])
    dfull = din("dfull", [1, 1024])
    snw = din("snw", [1, 1024])
    lamv = din("lamv", [1, 256])
    subw = din("subw", [128, 1])
    w_out0 = din("w_out0", [2048, 1024])
    ple_w = din("ple_w", [2, 256, 1024])
    gate_w = din("gate_w", [2, 1024, 1024])
    w_in1 = din("w_in1", [1024, 6144])
    ccw = din("ccw", [2048, 31])
    ccb = din("ccb", [2048, 1])
    lnw = din("lnw", [2048, 1])
    lnb = din("lnb", [2048, 1])
    w_out1 = din("w_out1", [2048, 1024])
    fnw = din("fnw", [1, 1024])
    out = nc.dram_tensor("out", [32, 128, 1024], F32, kind="ExternalOutput").ap()

    kT = dscr("kT", [8, 128, 8192], BF16)
    vS = dscr("vS", [NBLK, 128, 1024], BF16)
    qT = dscr("qT", [8, 128, NFB * 128], BF16)
    zs = dscr("zs", [NFB, 128, 1024], BF16)
    gsT = dscr("gsT", [8, 128, NFB * 128], BF16)
    xsS = dscr("xsS", [NBLK, 128, 1024], BF16)
    bS = dscr("bS", [NBLK, 128, 256], BF16)
    bTS = dscr("bTS", [NBLK, 128, 2, 128], BF16)
    cTS = dscr("cTS", [NFB, 128, 2, 128], BF16)
    dtS = dscr("dtS", [NBLK, 128, 16], F32)
    ymix = dscr("ymix", [NHB, 128, 16, 128], BF16)
    h1S = dscr("h1S", [NHB, 128, 1024], F32)
    hn1T = dscr("hn1T", [NHB, 128, 8, 128], BF16)
    gluT = dscr("gluT", [16, 128, NHB * 128], BF16)
    sg1T = dscr("sg1T", [16, 128, 4096], BF16)

    uid = [0]

    def mk(es, name, shape, dt, psum=False):
        uid[0] += 1
        nm = f"{name}_{uid[0]}"
        if psum:
            t = es.enter_context(nc.psum_tensor(nm, list(shape), dt))
        else:
            t = es.enter_context(nc.sbuf_tensor(nm, list(shape), dt))
        return B(t, ("P:" + nm) if psum else nm)

    def MM(o, lhsT, rhs, start, stop, r, w, sig=True):
        S.emit("pe", lambda e: e.matmul(out=o, lhsT=lhsT, rhs=rhs, start=start, stop=stop), r, w, signal=sig)

    def TR(o, in_, ident, r, w, sig=True):
        S.emit("pe", lambda e: e.transpose(out=o, in_=in_, identity=ident), r, w, signal=sig)

    def ACT(o, in_, func, r, w, **kw):
        S.emit("act", lambda e: e.activation(out=o, in_=in_, func=func, **kw), r, w)

    def TT(eng, o, a, b, op, r, w):
        S.emit(eng, lambda e: e.tensor_tensor(out=o, in0=a, in1=b, op=op), r, w)

    def TS(eng, o, a, s1, op0, r, w, s2=None, op1=None):
        if op1 is None:
            S.emit(eng, lambda e: e.tensor_scalar(out=o, in0=a, scalar1=s1, scalar2=None, op0=op0), r, w)
        else:
            S.emit(eng, lambda e: e.tensor_scalar(out=o, in0=a, scalar1=s1, scalar2=s2, op0=op0, op1=op1), r, w)

    def STT(o, a, s, b, op0, op1, r, w):
        S.emit("dve", lambda e: e.scalar_tensor_tensor(out=o, in0=a, scalar=s, in1=b, op0=op0, op1=op1), r, w)

    def CP(eng, o, a, r, w):
        if eng == "act":
            ACT(o, a, AF.Copy, r, w)
        else:
            S.emit(eng, lambda e: e.tensor_copy(out=o, in_=a), r, w)

    def RECIP(o, a, r, w):
        S.emit("dve", lambda e: e.reciprocal(out=o, in_=a), r, w)

    def LD(o, in_, r, w, q="sp", **kw):
        S.dma(q, o, in_, r, w, **kw)

    def ST(o, in_, r, w, q="pool", **kw):
        S.dma(q, o, in_, r, w, **kw)

    def rstd_from_ss(ss, n, eps=EPS):
        ACT(ss[:], ss[:], AF.Sqrt, [ss.k], [ss.k], scale=1.0 / n, bias=eps)
        RECIP(ss[:], ss[:], [ss.k], [ss.k])

    gs = contextlib.ExitStack()
    with gs:
        cstf = mk(gs, "cstf", [128, 5, 128], F32)
        LD(cstf[:], cst.rearrange("c p f -> p c f"), [], [cstf.k])
        identb = mk(gs, "identb", [128, 128], BF16)
        permb = mk(gs, "permb", [128, 128], BF16)
        CP("dve", identb[:], cstf[:, 0, :], [cstf.k], [identb.k])
        CP("dve", permb[:], cstf[:, 1, :], [cstf.k], [permb.k])
        mle = cstf[:, 2, :]
        mgt = cstf[:, 3, :]
        onesf = cstf[:, 4, :]
        mleb = mk(gs, "mleb", [128, 128], BF16)
        CP("dve", mleb[:], mle, [cstf.k], [mleb.k])
        flg = mk(gs, "flg", [128, 2], F32)
        LD(flg[:], flagc, [], [flg.k])

        def load_w_bf16(es, src, rows, cols, scale_col=None, name="w", prefetch=False):
            kcs = rows // 128
            wt = mk(es, name, [128, kcs, cols], BF16)
            CH = 512
            ls = es if prefetch else contextlib.ExitStack()
            nst = 3 if prefetch else 6
            stg = [mk(ls, "wstg", [128, CH], F32) for _ in range(nst)]
            n = 0
            for kc in range(kcs):
                for c0 in range(0, cols, CH):
                    cw = min(CH, cols - c0)
                    st = stg[n % nst]
                    LD(st[:, 0:cw], src[kc * 128:(kc + 1) * 128, c0:c0 + cw], [], [st.k])
                    eng = "pool" if prefetch else ("dve", "act")[n % 2]
                    o_, i_ = wt[:, kc, c0:c0 + cw], st[:, 0:cw]
                    wk = wt.k + ("" if not prefetch else "")
                    if eng == "act":
                        if scale_col is not None:
                            ACT(o_, i_, AF.Copy, [st.k, scale_col.k], [wk], scale=scale_col[:, kc:kc + 1])
                        else:
                            ACT(o_, i_, AF.Copy, [st.k], [wk])
                    elif scale_col is not None:
                        TS(eng, o_, i_, scale_col[:, kc:kc + 1], ALU.mult, [st.k, scale_col.k], [wk],
                           s2=0.0, op1=ALU.add)
                    else:
                        TS(eng, o_, i_, 1.0, ALU.mult, [st.k], [wk], s2=0.0, op1=ALU.add)
                    n += 1
            if not prefetch:
                S.barrier()
                ls.close()
            return wt

        with contextlib.ExitStack() as es:
            nwc = mk(es, "nwc", [128, 8], F32)
            LD(nwc[:], nw[0].rearrange("(k p) o -> p (k o)", p=128), [], [nwc.k])
            W0 = load_w_bf16(es, w_in0, 1024, 6672, nwc, "W0")
            cwc = mk(es, "cwc", [128, 12, 4], F32)
            LD(cwc[:], cw0.rearrange("(c p) j -> p c j", p=128), [], [cwc.k])
            cbc = mk(es, "cbc", [128, 12], F32)
            LD(cbc[:], cb0.rearrange("(c p) o -> p (c o)", p=128), [], [cbc.k])
            dtbb = mk(es, "dtbb", [128, 16], F32)
            LD(dtbb[:], dtb.partition_broadcast(128), [], [dtbb.k])
            hist = mk(es, "hist", [128, 12, 3], F32)
            S.emit("dve", lambda e: e.memset(hist[:], 0.0), [], [hist.k])
            xblk = [mk(es, "xblk", [128, 1024], F32) for _ in range(2)]
            junk = mk(es, "junk", [128, 1024], BF16)
            ssb = [mk(es, "ssb", [128, 1], F32) for _ in range(2)]
            xn = [mk(es, "xn", [128, 1024], BF16) for _ in range(4)]
            hnTs = [mk(es, "hnT", [128, 8, 512], BF16) for _ in range(2)]
            hnT = hnTs[0]
            ssb4 = [mk(es, "ssb4", [128, 1], F32) for _ in range(4)]
            pend = []
            raw = [mk(es, "raw", [128, 515], F32) for _ in range(3)]
            cacc = [mk(es, "cacc", [128, 512], F32) for _ in range(3)]
            xa = [mk(es, "xa", [128, 512], BF16) for _ in range(3)]
            xsg = mk(es, "xsg", [128, 4, 1024], BF16)
            bg = mk(es, "bg", [128, 4, 256], BF16)
            cosg = mk(es, "cosg", [128, 512], F32)
            sing = mk(es, "sing", [128, 512], F32)
            qb = [mk(es, "qb", [128, 512], BF16) for _ in range(3)]
            t1 = [mk(es, "t1", [128, 512], F32) for _ in range(3)]
            t2 = [mk(es, "t2", [128, 512], F32) for _ in range(3)]
            qr = [mk(es, "qr", [128, 512], BF16) for _ in range(3)]
            fo = [mk(es, "fo", [128, 512], BF16) for _ in range(2)]
            tmo = [mk(es, "tmo", [128, 1024], BF16) for _ in range(2)]
            dts = mk(es, "dts", [128, 4, 16], F32)
            pacc = [mk(es, "pacc", [128, 512], F32, psum=True) for _ in range(3)]
            ptr = [mk(es, "ptr", [128, 1024], BF16, psum=True) for _ in range(2)]
            pperm = [mk(es, "pperm", [128, 512], F32, psum=True) for _ in range(2)]
            pdt = mk(es, "pdt", [128, 64], F32, psum=True)
            rot = {"acc": 0, "i": 0}

            PDEPTH = 2

            def flush(min_age=0):
                keep, run = [], []
                for ent in pend:
                    (run if ent[0] >= min_age else keep).append(ent)
                pend[:] = keep
                for ent in run:
                    ent[1]()

            def defer(fn):
                pend.append([0, fn])

            def proj_fm(col0, M=128):
                pa = pacc[rot["acc"] % 3]
                rot["acc"] += 1
                for kc in range(8):
                    MM(pa[0:M, :], W0[:, kc, col0:col0 + M], hnT[:, kc, :], kc == 0, kc == 7,
                       [W0.k, hnT.k], [pa.k], sig=(kc == 7))
                for ent in pend:
                    ent[0] += 1
                flush(PDEPTH)
                return pa

            def fe_part1(g):
                for bi in range(4):
                    blk = g * 4 + bi
                    xb_, s_, xn_ = xblk[bi % 2], ssb4[bi], xn[bi]
                    LD(xb_[:], x_loc[blk], [], [xb_.k])
                    ACT(junk[:], xb_[:], AF.Square, [xb_.k], [junk.k, s_.k], accum_out=s_[:])
                    rstd_from_ss(s_, 1024)
                    TS("dve", xn_[:], xb_[:], s_[:, 0:1], ALU.mult, [xb_.k, s_.k], [xn_.k])

            def fe_part2(g):
                hn = hnTs[g % 2]
                for bi in range(4):
                    xn_, pt_ = xn[bi], ptr[bi % 2]
                    for kc in range(8):
                        TR(pt_[:, kc * 128:(kc + 1) * 128], xn_[:, kc * 128:(kc + 1) * 128], identb[:],
                           [xn_.k, identb.k], [pt_.k])
                    CP("act", hn[:, :, bi * 128:(bi + 1) * 128],
                       pt_[:].rearrange("p (k t) -> p k t", k=8), [pt_.k], [hn.k])

            NGA = min(16, NGDBG) if upto >= 1 else 0
            if NGA:
                fe_part1(0)
                fe_part2(0)
            for g in range(NGA):
                full = g >= 7
                t0 = g * 512
                hnT = hnTs[g % 2]
                if g + 1 < NGA:
                    fe_part1(g + 1)
                if KSTOP < 2:
                    continue
                for bi in range(4):
                    for kc in range(8):
                        MM(pdt[:, bi * 16:(bi + 1) * 16], hnT[:, kc, bi * 128:(bi + 1) * 128],
                           W0[:, kc, 2560:2576], kc == 0, kc == 7, [hnT.k, W0.k], [pdt.k], sig=(kc == 7))
                TT("dve", dts[:], pdt[:].rearrange("p (b h) -> p b h", b=4),
                   dtbb[:].unsqueeze(1).to_broadcast([128, 4, 16]), ALU.add, [pdt.k, dtbb.k], [dts.k])
                ACT(dts[:], dts[:], AF.Exp, [dts.k], [dts.k])
                ACT(dts[:], dts[:], AF.Ln, [dts.k], [dts.k], bias=1.0)
                ST(dtS[g * 4:(g + 1) * 4].rearrange("b p h -> p b h"), dts[:], [dts.k], ["dtS"])
                def age_flush():
                    for ent in pend:
                        ent[0] += 1
                    flush(PDEPTH)

                def item_x(j, g=g, full=full):
                    pa = proj_fm(1024 + j * 128)
                    rw, ca, xa_ = raw[j % 3], cacc[j % 3], xa[j % 3]
                    CP("act", rw[:, 3:515], pa[:], [pa.k], [rw.k])
                    CP("pool", rw[:, 0:3], hist[:, j, :], [hist.k, rw.k], [rw.k])
                    TS("dve", ca[:], rw[:, 0:512], cwc[:, j, 0:1], ALU.mult, [rw.k, cwc.k, cbc.k], [ca.k],
                       s2=cbc[:, j:j + 1], op1=ALU.add)
                    for tp in range(1, 4):
                        STT(ca[:], rw[:, tp:tp + 512], cwc[:, j, tp:tp + 1], ca[:], ALU.mult, ALU.add,
                            [rw.k, cwc.k, ca.k], [ca.k])
                    CP("pool", hist[:, j, :], rw[:, 512:515], [rw.k], [hist.k])
                    ACT(xa_[:], ca[:], AF.Silu, [ca.k], [xa_.k])

                    def after_x():
                        if j < 10:
                            pt_ = ptr[j % 2]
                            for bi in range(4):
                                TR(pt_[:, bi * 128:(bi + 1) * 128], xa_[:, bi * 128:(bi + 1) * 128], identb[:],
                                   [xa_.k, identb.k], [pt_.k])
                            if j < 8:
                                CP("act", xsg[:, :, j * 128:(j + 1) * 128],
                                   pt_[:, 0:512].rearrange("p (b t) -> p b t", b=4), [pt_.k], [xsg.k])
                            else:
                                CP("act", bg[:, :, (j - 8) * 128:(j - 7) * 128],
                                   pt_[:, 0:512].rearrange("p (b t) -> p b t", b=4), [pt_.k], [bg.k])
                                ST(bTS[g * 4:(g + 1) * 4, :, j - 8, :].rearrange("b p t -> p b t"),
                                   xa_[:].rearrange("p (b t) -> p b t", b=4), [xa_.k], ["bTS"])
                        elif full:
                            ST(cTS[g * 4 - FB0:(g + 1) * 4 - FB0, :, j - 10, :].rearrange("b p t -> p b t"),
                               xa_[:].rearrange("p (b t) -> p b t", b=4), [xa_.k], ["cTS"])
                    defer(after_x)

                def item_g(c, t0=t0):
                    pa = proj_fm(5648 + c * 128)
                    fo_ = fo[c % 2]
                    ACT(fo_[:], pa[:], AF.Silu, [pa.k], [fo_.k])
                    ST(gsT[c, :, t0 - FB0 * 128:t0 - FB0 * 128 + 512], fo_[:], [fo_.k], ["gsT"])

                def item_tm(kind, bi, g=g):
                    cbase = 0 if kind == "z" else 4624
                    tm = tmo[bi % 2]
                    for hf in range(2):
                        pa = pacc[rot["acc"] % 3]
                        rot["acc"] += 1
                        for kc in range(8):
                            MM(pa[:], hnT[:, kc, bi * 128:(bi + 1) * 128],
                               W0[:, kc, cbase + hf * 512:cbase + (hf + 1) * 512], kc == 0, kc == 7,
                               [hnT.k, W0.k], [pa.k], sig=(kc == 7))
                        age_flush()
                        if kind == "z":
                            ACT(tm[:, hf * 512:(hf + 1) * 512], pa[:], AF.Silu, [pa.k], [tm.k])
                        else:
                            CP("act" if hf == 0 else "dve", tm[:, hf * 512:(hf + 1) * 512], pa[:], [pa.k], [tm.k])
                    blk = g * 4 + bi
                    if kind == "z":
                        ST(zs[blk - FB0], tm[:], [tm.k], ["zs"])
                    else:
                        ST(vS[blk], tm[:], [tm.k], ["vS"])

                rope_n = [0]

                def item_rope(kind, h, t0=t0):
                    col0 = (2576 if kind == "q" else 3600) + h * 128
                    pa = proj_fm(col0)
                    n = rope_n[0]
                    rope_n[0] += 1
                    qb_, t1_, t2_, qr_, pp = qb[n % 3], t1[n % 3], t2[n % 3], qr[n % 3], pperm[n % 2]
                    CP("act", qb_[:], pa[:], [pa.k], [qb_.k])

                    def after_q():
                        MM(pp[:], permb[:], qb_[:], True, True, [permb.k, qb_.k], [pp.k])
                        TT("dve", t1_[:], pa[:], cosg[:], ALU.mult, [pa.k, cosg.k], [t1_.k])
                        TT("dve", t2_[:], pp[:], sing[:], ALU.mult, [pp.k, sing.k], [t2_.k])
                        TT("dve", qr_[:], t1_[:], t2_[:], ALU.add, [t1_.k, t2_.k], [qr_.k])
                        if kind == "q":
                            ST(qT[h, :, t0 - FB0 * 128:t0 - FB0 * 128 + 512], qr_[:], [qr_.k], ["qT"])
                        else:
                            ST(kT[h, :, t0:t0 + 512], qr_[:], [qr_.k], ["kT"])
                    defer(after_q)

                LD(cosg[:], cosT[:, t0:t0 + 512], [], [cosg.k])
                LD(sing[:], sinT[:, t0:t0 + 512], [], [sing.k])
                light = []
                if full:
                    light += [lambda c=c: item_g(c) for c in range(8)]
                    light += [lambda bi=bi: item_tm("z", bi) for bi in range(4)]
                light += [lambda bi=bi: item_tm("v", bi) for bi in range(4)]
                ropes = [lambda kind=kind, h=h: item_rope(kind, h)
                         for kind in ((["q"] if full else []) + ["k"]) for h in range(8)]
                li = 0
                for j in range(12):
                    item_x(j)
                    take = (len(light) * (j + 1)) // 12 - (len(light) * j) // 12
                    for _ in range(take):
                        light[li]()
                        li += 1
                    if j == 5 and g + 1 < NGA:
                        fe_part2(g + 1)
                flush()
                ST(xsS[g * 4:(g + 1) * 4].rearrange("b p f -> p b f"), xsg[:], [xsg.k], ["xsS"])
                ST(bS[g * 4:(g + 1) * 4].rearrange("b p f -> p b f"), bg[:], [bg.k], ["bS"])
                for it in ropes:
                    it()
                flush()
            S.barrier()

        if upto >= 2:
          with contextlib.ExitStack() as es:
            anegb = mk(es, "anegb", [128, 16], F32)
            LD(anegb[:], alog.partition_broadcast(128), [], [anegb.k])
            ACT(anegb[:], anegb[:], AF.Exp, [anegb.k], [anegb.k])
            TS("dve", anegb[:], anegb[:], -1.0, ALU.mult, [anegb.k], [anegb.k])
            dfl = mk(es, "dfl", [128, 1024], F32)
            LD(dfl[:], dfull.partition_broadcast(128), [], [dfl.k])
            snwb = mk(es, "snwb", [128, 1024], F32)
            LD(snwb[:], snw.partition_broadcast(128), [], [snwb.k])
            Sf = mk(es, "Sf", [128, 2, 512], F32)
            Sb = mk(es, "Sb", [128, 2, 512], BF16)
            S.emit("dve", lambda e: e.memset(Sf[:], 0.0), [], [Sf.k])
            S.emit("pool", lambda e: e.memset(Sb[:], 0.0), [], [Sb.k])
            xs_ = [mk(es, "xs", [128, 1024], BF16) for _ in range(2)]
            bt_ = [mk(es, "bt", [128, 256], BF16) for _ in range(2)]
            bT_ = [mk(es, "bT", [128, 2, 128], BF16) for _ in range(2)]
            cT_ = [mk(es, "cT", [128, 2, 128], BF16) for _ in range(2)]
            dt_ = [mk(es, "dt", [128, 16], F32) for _ in range(2)]
            z_ = [mk(es, "z", [128, 1024], BF16) for _ in range(2)]
            a_2 = [mk(es, "a", [128, 16], F32) for _ in range(2)]
            cs_2 = [mk(es, "cs", [128, 16], F32) for _ in range(2)]
            e_2 = [mk(es, "e", [128, 16], F32) for _ in range(2)]
            dec_2 = [mk(es, "dec", [128, 16], F32) for _ in range(2)]
            cd_2 = [mk(es, "cd", [128, 16], F32) for _ in range(2)]
            wv_2 = [mk(es, "wv", [128, 16], F32) for _ in range(2)]
            xd_2 = [mk(es, "xd", [128, 1024], BF16) for _ in range(2)]
            xdd_2 = [mk(es, "xdd", [128, 1024], BF16) for _ in range(2)]
            cbm4 = [mk(es, "cbm", [128, 128], F32) for _ in range(4)]
            lh = [mk(es, "lh", [128, 4, 128], F32) for _ in range(2)]
            eD = [mk(es, "eD", [128, 512], F32) for _ in range(2)]
            Mt = [mk(es, "Mt", [128, 4, 128], BF16) for _ in range(2)]
            y12 = [mk(es, "y1", [128, 1024], F32) for _ in range(2)]
            y32 = [mk(es, "y3", [128, 1024], F32) for _ in range(2)]
            yj2 = [mk(es, "yj", [128, 512], F32) for _ in range(2)]
            ss22 = [mk(es, "ss2", [128, 2], F32) for _ in range(2)]
            yn2 = [mk(es, "yn", [128, 1024], BF16) for _ in range(2)]
            ynT = [mk(es, "ynT", [128, 8, 128], BF16) for _ in range(2)]
            ps_s = mk(es, "ps_s", [128, 512], F32, psum=True)
            ps_Ds = [mk(es, "ps_D", [128, 512], F32, psum=True) for _ in range(2)]
            ps_yo1 = mk(es, "ps_yo", [128, 512], F32, psum=True)
            ps_y = [mk(es, "ps_y", [128, 512], F32, psum=True) for _ in range(2)]
            ps_st = mk(es, "ps_st", [128, 512], F32, psum=True)
            ps_tr = mk(es, "ps_tr", [128, 1024], BF16, psum=True)
            for c in range(NBLK):
                full = c >= HALO
                i2 = c % 2
                xs, bt, bT, cT, dt, z = xs_[i2], bt_[i2], bT_[i2], cT_[i2], dt_[i2], z_[i2]
                a_, cs_, e_, dec_, cd_, wv_, xd_, xdd_ = a_2[i2], cs_2[i2], e_2[i2], dec_2[i2], cd_2[i2], wv_2[i2], xd_2[i2], xdd_2[i2]
                y1, y3, yj, ss2, yn = y12[i2], y32[i2], yj2[i2], ss22[i2], yn2[i2]
                cbm = cbm4[i2 * 2:i2 * 2 + 2]
                LD(xs[:], xsS[c], ["xsS"], [xs.k])
                LD(bt[:], bS[c], ["bS"], [bt.k])
                LD(dt[:], dtS[c], ["dtS"], [dt.k])
                if full:
                    LD(bT[:], bTS[c], ["bTS"], [bT.k])
                    LD(cT[:], cTS[c - FB0], ["cTS"], [cT.k])
                    LD(z[:], zs[c - FB0], ["zs"], [z.k])
                TT("dve", a_[:], dt[:], anegb[:], ALU.mult, [dt.k, anegb.k], [a_.k])
                MM(ps_s[:, 0:16], mle, a_[:], True, True, [cstf.k, a_.k], ["P:ps_s"])
                MM(ps_s[:, 16:32], onesf, a_[:], True, True, [cstf.k, a_.k], ["P:ps_s"])
                CP("act", cs_[:], ps_s[:, 0:16], ["P:ps_s"], [cs_.k])
                TT("dve", dec_[:], ps_s[:, 16:32], cs_[:], ALU.subtract, ["P:ps_s", cs_.k], [dec_.k])
                ACT(dec_[:], dec_[:], AF.Exp, [dec_.k], [dec_.k])
                ACT(cd_[:], ps_s[:, 16:32], AF.Exp, ["P:ps_s"], [cd_.k])
                TT("dve", wv_[:], dt[:], dec_[:], ALU.mult, [dt.k, dec_.k], [wv_.k])
                TT("dve", xdd_[:].rearrange("p (h d) -> p h d", h=16), xs[:].rearrange("p (h d) -> p h d", h=16),
                   wv_[:].unsqueeze(2).to_broadcast([128, 16, 64]), ALU.mult, [xs.k, wv_.k], [xdd_.k])
                if full:
                    ACT(e_[:], cs_[:], AF.Exp, [cs_.k], [e_.k])
                    TT("dve", xd_[:].rearrange("p (h d) -> p h d", h=16), xs[:].rearrange("p (h d) -> p h d", h=16),
                       dt[:].unsqueeze(2).to_broadcast([128, 16, 64]), ALU.mult, [xs.k, dt.k], [xd_.k])
                    for g in range(2):
                        MM(ps_s[:, 128 + g * 128:256 + g * 128], bT[:, g, :], cT[:, g, :], True, True,
                           [bT.k, cT.k], ["P:ps_s"])
                        TT("dve", cbm[g][:], ps_s[:, 128 + g * 128:256 + g * 128], mle, ALU.mult,
                           ["P:ps_s", cstf.k], [cbm[g].k])
                        MM(ps_yo1[:], cT[:, g, :], Sb[:, g, :], True, True, [cT.k, Sb.k], [ps_yo1.k])
                        TT("dve", y1[:, g * 512:(g + 1) * 512].rearrange("p (h d) -> p h d", h=8),
                           ps_yo1[:].rearrange("p (h d) -> p h d", h=8),
                           e_[:, g * 8:(g + 1) * 8].unsqueeze(2).to_broadcast([128, 8, 64]), ALU.mult,
                           [ps_yo1.k, e_.k], [y1.k])
                    for hb in range(4):
                        g = hb // 2
                        h0 = hb * 4
                        l_, e2, m_ = lh[hb % 2], eD[hb % 2], Mt[hb % 2]
                        TT("dve", l_[:], cstf[:, 3:4, :].to_broadcast([128, 4, 128]),
                           a_[:, h0:h0 + 4].unsqueeze(2).to_broadcast([128, 4, 128]), ALU.mult, [cstf.k, a_.k], [l_.k])
                        ps_D = ps_Ds[hb % 2]
                        for i in range(4):
                            MM(ps_D[:, i * 128:(i + 1) * 128], l_[:, i, :], mle, True, True, [l_.k, cstf.k], [ps_D.k],
                               sig=(i == 3))
                        ACT(e2[:], ps_D[:], AF.Exp, [ps_D.k], [e2.k])
                        TT("dve", m_[:], e2[:].rearrange("p (i t) -> p i t", i=4),
                           cbm[g][:].unsqueeze(1).to_broadcast([128, 4, 128]), ALU.mult, [e2.k, cbm[g].k], [m_.k])
                        for i in range(4):
                            h = h0 + i
                            MM(ps_y[g][:, (h % 8) * 64:(h % 8 + 1) * 64], m_[:, i, :], xd_[:, h * 64:(h + 1) * 64], True, True,
                               [m_.k, xd_.k], [ps_y[g].k])
                for g in range(2):
                    MM(ps_st[:], bt[:, g * 128:(g + 1) * 128], xdd_[:, g * 512:(g + 1) * 512], True, True,
                       [bt.k, xdd_.k], [ps_st.k])
                    if full:
                        pass
                    TT("dve", Sf[:, g, :].rearrange("p (h d) -> p h d", h=8), Sf[:, g, :].rearrange("p (h d) -> p h d", h=8),
                       cd_[:, g * 8:(g + 1) * 8].unsqueeze(2).to_broadcast([128, 8, 64]), ALU.mult,
                       [Sf.k, cd_.k], [Sf.k])
                    TT("dve", Sf[:, g, :], Sf[:, g, :], ps_st[:], ALU.add, [Sf.k, ps_st.k], [Sf.k])
                if c == HALO:
                    TS("dve", Sf[:], Sf[:], flg[:, 0:1], ALU.mult, [Sf.k, flg.k], [Sf.k])
                CP("act", Sb[:], Sf[:], [Sf.k], [Sb.k])
                if full:
                    for g in range(2):
                        sl = slice(g * 512, (g + 1) * 512)
                        TT("dve", y1[:, sl], y1[:, sl], ps_y[g][:], ALU.add, [y1.k, ps_y[g].k], [y1.k])
                    TT("dve", y3[:], xs[:], dfl[:], ALU.mult, [xs.k, dfl.k], [y3.k])
                    TT("dve", y3[:], y3[:], y1[:], ALU.add, [y3.k, y1.k], [y3.k])
                    TT("dve", y3[:], y3[:], z[:], ALU.mult, [y3.k, z.k], [y3.k])
                    for g in range(2):
                        ACT(yj[:], y3[:, g * 512:(g + 1) * 512], AF.Square, [y3.k], [yj.k, ss2.k + str(g)],
                            accum_out=ss2[:, g:g + 1])
                    ACT(ss2[:], ss2[:], AF.Ln, [ss2.k + "0", ss2.k + "1"], [ss2.k], scale=1.0 / 512, bias=EPS)
                    ACT(ss2[:], ss2[:], AF.Exp, [ss2.k], [ss2.k], scale=-0.5)
                    for g in range(2):
                        sl = slice(g * 512, (g + 1) * 512)
                        STT(yn[:, sl], y3[:, sl], ss2[:, g:g + 1], snwb[:, sl], ALU.mult, ALU.mult,
                            [y3.k, ss2.k, snwb.k], [yn.k])
                    for kc in range(8):
                        TR(ps_tr[:, kc * 128:(kc + 1) * 128], yn[:, kc * 128:(kc + 1) * 128], identb[:],
                           [yn.k, identb.k], [ps_tr.k])
                    yT = ynT[i2]
                    CP("act", yT[:], ps_tr[:].rearrange("p (k t) -> p k t", k=8), [ps_tr.k], [yT.k])
                    ST(ymix[c - HALO, :, 0:8, :], yT[:], [yT.k], ["ymixA"])
            S.barrier()

        esD = contextlib.ExitStack()
        if upto >= 3:
          woD = load_w_bf16(esD, w_out0, 2048, 1024, None, "wo0", prefetch=True)
          gwD = load_w_bf16(esD, gate_w[0], 1024, 1024, None, "gw0", prefetch=True)
          pwD = load_w_bf16(esD, ple_w[0], 256, 1024, None, "pw0", prefetch=True)
          with contextlib.ExitStack() as es:
            lamb = mk(es, "lamb", [128, 256], F32)
            LD(lamb[:], lamv.partition_broadcast(128), [], [lamb.k])
            lt = mk(es, "lt", [128, 128], F32)
            lsum = mk(es, "lsum", [128, 2], F32)
            neglam = mk(es, "neglam", [128, 1], F32)
            for i in range(2):
                S.emit("dve", lambda e, i=i: e.tensor_tensor_reduce(
                    out=lt[:, i * 64:(i + 1) * 64], in0=lamb[:, i * 128:i * 128 + 64], in1=lamb[:, i * 128 + 64:i * 128 + 128],
                    op0=ALU.mult, op1=ALU.add, scale=1.0, scalar=0.0, accum_out=lsum[:, i:i + 1]) if False else
                    e.tensor_tensor(out=lt[:, i * 64:(i + 1) * 64], in0=lamb[:, i * 128:i * 128 + 64],
                                    in1=lamb[:, i * 128 + 64:i * 128 + 128], op=ALU.mult), [lamb.k], [lt.k])
                S.emit("dve", lambda e, i=i: e.reduce_sum(out=lsum[:, i:i + 1], in_=lt[:, i * 64:(i + 1) * 64],
                                                         axis=mybir.AxisListType.X), [lt.k], [lsum.k])
            ACT(lsum[:], lsum[:], AF.Exp, [lsum.k], [lsum.k])
            STT(neglam[:], lsum[:, 1:2], -0.2, lsum[:, 0:1], ALU.add, ALU.subtract, [lsum.k], [neglam.k])
            subc = mk(es, "subc", [128, 1], F32)
            LD(subc[:], subw, [], [subc.k])
            TS("dve", subc[:], subc[:], 0.8, ALU.mult, [subc.k], [subc.k])
            Kh = [mk(es, "Kh", [128, 8192], BF16) for _ in range(2)]
            Vh = [mk(es, "Vh", [128, NBLK, 128], BF16) for _ in range(2)]
            Qh = [mk(es, "Qh", [128, NHB * 128], BF16) for _ in range(2)]
            Pt = [mk(es, "Pt", [128, 2, 512], BF16) for _ in range(NPT)]
            accs = [mk(es, "accs", [128, 2, 512], F32) for _ in range(2)]
            tmps = [mk(es, "tmps", [128, 2, 512], BF16) for _ in range(2)]
            gst = [mk(es, "gst", [128, 512], BF16) for _ in range(2)]
            r12 = mk(es, "r12", [128, 2, 512], F32)
            o12 = mk(es, "o12", [128, 2, 512], F32)
            osq = mk(es, "osq", [128, 512], F32)
            og = [mk(es, "og", [128, 512], BF16) for _ in range(2)]
            psS = [mk(es, "psS", [128, 2, 512], F32, psum=True) for _ in range(NPS)]
            psO = [mk(es, "psO", [128, 2, 512], F32, psum=True) for _ in range(4 - NPS)]
            nmask = mk(es, "nmask", [128, 128], BF16)
            TS("dve", nmask[:], mgt, -30000.0, ALU.mult, [cstf.k], [nmask.k])
            un = 0
            nch = 0
            fin2 = []

            def load_head(h):
                K, V, Q = Kh[h % 2], Vh[h % 2], Qh[h % 2]
                LD(K[:], kT[h], ["kT"], [K.k])
                LD(V[:], vS[:, :, h * 128:(h + 1) * 128].rearrange("b p e -> p b e"), ["vS"], [V.k])
                LD(Q[:], qT[h, :, (HALO - FB0) * 128:], ["qT"], [Q.k])
            load_head(0)
            all_chunks = []
            for h in range(8):
                chunks = [(0, 128, [(kb, "pre0") for kb in range(HALO)] + [(HALO, "diag0")], 0)]
                for qc in range(8):
                    lst = [(kb, "pre") for kb in range(32)]
                    lst += [(32 + kb, "full") for kb in range(4 * qc)]
                    lst += [(32 + 4 * qc + r, f"diag{r}") for r in range(4)]
                    chunks.append((128 + qc * 512, 512, lst, 1 + qc * 4))
                for ci, ch in enumerate(chunks):
                    all_chunks.append((h, ci) + ch)

            def emit_qk(K, Q, q0, qw, lst, ik):
                nonlocal un
                kb, kind = lst[ik]
                c0 = int(kind[4:]) * 128 if kind.startswith("diag") else 0
                wcol = qw - c0
                sS = psS[un % NPS]
                pP = Pt[un % NPT]
                un += 1
                dg_ = kind.startswith("diag")
                MM(sS[:, 0, 0:wcol], K[0:64, kb * 128:(kb + 1) * 128], Q[0:64, q0 + c0:q0 + qw], True, not dg_,
                   [K.k, Q.k], [sS.k], sig=False)
                MM(sS[:, 1, 0:wcol], K[64:128, kb * 128:(kb + 1) * 128], Q[64:128, q0 + c0:q0 + qw], True, not dg_,
                   [K.k, Q.k], [sS.k], sig=not dg_)
                if dg_:
                    MM(sS[:, 0, 0:128], identb[:], nmask[:], False, True, [identb.k, nmask.k], [sS.k], sig=False)
                    MM(sS[:, 1, 0:128], identb[:], nmask[:], False, True, [identb.k, nmask.k], [sS.k])
                return (ik, kb, kind, c0, wcol, sS, pP)

            pre = None
            if True:
                for idx, (h, ci, q0, qw, lst, yb0) in enumerate(all_chunks):
                    K, V, Q = Kh[h % 2], Vh[h % 2], Qh[h % 2]
                    if ci == 1 and h + 1 < 8:
                        load_head(h + 1)
                    gs_ = gst[nch % 2]
                    acc = accs[nch % 2]
                    started = set()
                    hold = {"n": 0}
                    nfull = sum(1 for (_, kd) in lst if not kd.startswith("diag"))
                    pO = psO[nch % (4 - NPS)]
                    nch += 1
                    LD(gs_[:, 0:qw], gsT[h, :, (HALO - FB0) * 128 + q0:(HALO - FB0) * 128 + q0 + qw], ["gsT"], [gs_.k])
                    nk = len(lst)

                    def emit_rest(info):
                        ik, kb, kind, c0, wcol, sS, pP = info
                        if kind == "pre":
                            ACT(pP[:, :, 0:wcol], sS[:, :, 0:wcol], AF.Exp, [sS.k, flg.k], [pP.k], scale=0.125,
                                bias=flg[:, 1:2])
                        else:
                            ACT(pP[:, :, 0:wcol], sS[:, :, 0:wcol], AF.Exp, [sS.k], [pP.k], scale=0.125)
                        st, sp_ = (ik == 0), (ik == nk - 1)
                        MM(pO[:, 0, c0:qw], V[:, kb, :], pP[:, 0, 0:wcol], st, sp_, [V.k, pP.k], [pO.k], sig=False)
                        MM(pO[:, 1, c0:qw], V[:, kb, :], pP[:, 1, 0:wcol], st, sp_, [V.k, pP.k], [pO.k])
                        def acc_add(src, lo, hi, c_lo):
                            if acc.k not in started:
                                started.add(acc.k)
                                assert c_lo == 0
                                CP("dve", acc[:, :, 0:qw], src[:, :, 0:qw], [src.k], [acc.k])
                            else:
                                TT("dve", acc[:, :, c_lo:qw], acc[:, :, c_lo:qw], src[:, :, lo:hi], ALU.add,
                                   [acc.k, src.k], [acc.k])
                        if kind.startswith("diag"):
                            acc_add(pP, 0, wcol, c0)
                        else:
                            gpos = ik % 4
                            last_full = (ik == nfull - 1)
                            if gpos == 0:
                                if last_full:
                                    acc_add(pP, 0, qw, 0)
                                else:
                                    hold["p"] = pP
                            elif gpos == 1:
                                hold["t"] = tmps[hold["n"] % 2]
                                hold["n"] += 1
                                TT("dve", hold["t"][:, :, 0:qw], hold["p"][:, :, 0:qw], pP[:, :, 0:qw], ALU.add,
                                   [hold["p"].k, pP.k], [hold["t"].k])
                                if last_full:
                                    acc_add(hold["t"], 0, qw, 0)
                            else:
                                t_ = hold["t"]
                                TT("dve", t_[:, :, 0:qw], t_[:, :, 0:qw], pP[:, :, 0:qw], ALU.add, [t_.k, pP.k], [t_.k])
                                if gpos == 3 or last_full:
                                    acc_add(t_, 0, qw, 0)

                    infos = pre if pre is not None else [emit_qk(K, Q, q0, qw, lst, ik) for ik in range(QKD)]
                    pre = None
                    for ik in range(nk):
                        if ik + QKD < nk:
                            infos.append(emit_qk(K, Q, q0, qw, lst, ik + QKD))
                        emit_rest(infos[ik])
                        while fin2 and fin2[0][0] <= ik:
                            fin2.pop(0)[1]()
                    if idx + 1 < len(all_chunks):
                        h2, ci2, q02, qw2, lst2, _ = all_chunks[idx + 1]
                        pre = [emit_qk(Kh[h2 % 2], Qh[h2 % 2], q02, qw2, lst2, ik) for ik in range(QKD)]
                    pD = psS[un % NPS]
                    for mp in range(2):
                        MM(pD[:, mp, 0:qw], onesf, acc[:, mp, 0:qw], True, True, [cstf.k, acc.k], [pD.k])
                    S.emit("dve", lambda e, pO=pO, qw=qw: e.tensor_copy(out=o12[:, :, 0:qw], in_=pO[:, :, 0:qw]), [pO.k], [o12.k])
                    S.emit("dve", lambda e, pD=pD, qw=qw: e.tensor_copy(out=r12[:, :, 0:qw], in_=pD[:, :, 0:qw]), [pD.k], [r12.k])
                    w_ = slice(0, qw)

                    def st2a(qw=qw, mp=0):
                        S.emit("dve", lambda e: e.reciprocal(out=r12[:, mp, 0:qw], in_=r12[:, mp, 0:qw]), [r12.k], [r12.k])

                    def st2a1(qw=qw):
                        st2a(qw, 1)

                    def st2b(qw=qw, w_=w_):
                        TT("dve", o12[:, :, 0:qw], o12[:, :, 0:qw], r12[:, :, 0:qw], ALU.mult, [o12.k, r12.k], [o12.k])
                        STT(o12[:, 0, w_], o12[:, 1, w_], neglam[:, 0:1], o12[:, 0, w_], ALU.mult, ALU.add,
                            [neglam.k, o12.k], [o12.k])
                        TT("dve", osq[:, w_], o12[:, 0, w_], o12[:, 0, w_], ALU.mult, [o12.k], [osq.k])

                    def st2c(w_=w_, qw=qw, gs_=gs_, yb0=yb0, h=h, og_=og[nch % 2]):
                        pss = psS[un % NPS]
                        MM(pss[:, 0, w_], onesf, osq[:, w_], True, True, [cstf.k, osq.k], [pss.k])
                        CP("dve", r12[:, 1, w_], pss[:, 0, w_], [pss.k], [r12.k])

                    def st2d(w_=w_, qw=qw, gs_=gs_, yb0=yb0, h=h, og_=og[nch % 2]):
                        ACT(r12[:, 0, w_], r12[:, 1, w_], AF.Ln, [r12.k], [r12.k], scale=1.0 / 128, bias=EPS)
                        ACT(r12[:, 0, w_], r12[:, 0, w_], AF.Exp, [r12.k], [r12.k], scale=-0.5)
                        TT("dve", o12[:, 0, w_], o12[:, 0, w_], r12[:, 0, w_], ALU.mult, [o12.k, r12.k], [o12.k])
                        STT(og_[:, w_], o12[:, 0, w_], subc[:, 0:1], gs_[:, w_], ALU.mult, ALU.mult,
                            [o12.k, subc.k, gs_.k], [og_.k])
                        nb = qw // 128
                        ST(ymix[yb0:yb0 + nb, :, 8 + h, :].rearrange("b p t -> p b t"),
                           og_[:, w_].rearrange("p (b t) -> p b t", b=nb), [og_.k], ["ymixB"])
                    fin2.extend([(4, st2a), (8, st2a1), (12, st2b), (16, st2c), (21, st2d)])
            while fin2:
                fin2.pop(0)[1]()
            S.barrier()

        def ple_tail(es_bufs, li, ps_m, hres, pblk, w, mid=None):
            hp, hb, hT, pb, pT, sg, ps_g, ps_e, ps_trl = (es_bufs[k] for k in
                                                         ("hp", "hb", "hT", "pb", "pT", "sg", "ps_g", "ps_e", "ps_tr"))
            gatew, plew = w
            for hf in range(2):
                sl = slice(hf * 512, (hf + 1) * 512)
                TT("dve", hp[:, sl], ps_m[hf][:], hres[:, sl], ALU.add, [ps_m[hf].k, hres.k], [hp.k])
            if mid is not None:
                mid()
            CP("act", hb[:], hp[:], [hp.k], [hb.k])
            CP("pool", pb[:], pblk[:], [pblk.k], [pb.k])
            pt_ = ps_trl[0]
            for kc in range(8):
                TR(pt_[:, kc * 128:(kc + 1) * 128], hb[:, kc * 128:(kc + 1) * 128], identb[:], [hb.k, identb.k], [pt_.k])
            CP("act", hT[:], pt_[:].rearrange("p (k t) -> p k t", k=8), [pt_.k], [hT.k])
            pt2 = ps_trl[1]
            for kc in range(2):
                TR(pt2[:, kc * 128:(kc + 1) * 128], pb[:, kc * 128:(kc + 1) * 128], identb[:], [pb.k, identb.k], [pt2.k])
            CP("dve", pT[:], pt2[:, 0:256].rearrange("p (k t) -> p k t", k=2), [pt2.k], [pT.k])
            for hf in range(2):
                sl = slice(hf * 512, (hf + 1) * 512)
                for kc in range(8):
                    MM(ps_g[hf][:], hT[:, kc, :], gatew[:, kc, sl], kc == 0, kc == 7, [hT.k, gatew.k], [ps_g[hf].k], sig=(kc == 7))
                for kc in range(2):
                    MM(ps_e[hf][:], pT[:, kc, :], plew[:, kc, sl], kc == 0, kc == 1, [pT.k, plew.k], [ps_e[hf].k], sig=(kc == 1))
                ACT(sg[:, sl], ps_g[hf][:], AF.Sigmoid, [ps_g[hf].k], [sg.k])
                TT("dve", sg[:, sl], ps_e[hf][:], sg[:, sl], ALU.mult, [ps_e[hf].k, sg.k], [sg.k])
            TT("dve", hp[:], hp[:], sg[:], ALU.add, [hp.k, sg.k], [hp.k])
            return hp

        def tail_bufs(es):
            d = {}
            d["hp"] = mk(es, "hp", [128, 1024], F32)
            d["hb"] = mk(es, "hb", [128, 1024], BF16)
            d["hT"] = mk(es, "hT", [128, 8, 128], BF16)
            d["pb"] = mk(es, "pb", [128, 256], BF16)
            d["pT"] = mk(es, "pT", [128, 2, 128], BF16)
            d["sg"] = mk(es, "sg", [128, 1024], F32)
            return d

        if upto >= 4:
          with contextlib.ExitStack() as es:
            wo, gw, pw = woD, gwD, pwD
            bufs = tail_bufs(es)
            bufs["ps_g"] = [mk(es, "ps_g", [128, 512], F32, psum=True) for _ in range(2)]
            bufs["ps_e"] = [mk(es, "ps_e", [128, 512], F32, psum=True) for _ in range(2)]
            bufs["ps_tr"] = [mk(es, "ps_tr", [128, 1024], BF16, psum=True) for _ in range(2)]
            ps_m = [mk(es, "ps_m", [128, 512], F32, psum=True) for _ in range(2)]
            ym = [mk(es, "ym", [128, 16, 128], BF16) for _ in range(2)]
            xr = [mk(es, "xr", [128, 1024], F32) for _ in range(2)]
            pr = [mk(es, "pr", [128, 256], F32) for _ in range(2)]
            junk = mk(es, "junkD", [128, 1024], BF16)
            ss = mk(es, "ssD", [128, 1], F32)
            xn1 = mk(es, "xn1", [128, 1024], BF16)
            hnT1 = [mk(es, "hnT1", [128, 8, 128], BF16) for _ in range(2)]
            def outprojD(b):
                i2 = b % 2
                LD(ym[i2][:], ymix[b], ["ymixA", "ymixB"], [ym[i2].k])
                LD(xr[i2][:], x_loc[HALO + b], [], [xr[i2].k])
                LD(pr[i2][:], p0[b], [], [pr[i2].k])
                for hf in range(2):
                    for kc in range(16):
                        MM(ps_m[hf][:], ym[i2][:, kc, :], wo[:, kc, hf * 512:(hf + 1) * 512], kc == 0, kc == 15,
                           [ym[i2].k, wo.k], [ps_m[hf].k], sig=(kc == 15))
            outprojD(0)
            for b in range(NHB):
                i2 = b % 2
                hp = ple_tail(bufs, 0, ps_m, xr[i2], pr[i2], (gw, pw),
                              mid=(lambda b=b: outprojD(b + 1)) if b + 1 < NHB else None)
                ST(h1S[b], hp[:], [hp.k], ["h1S"])
                ACT(junk[:], hp[:], AF.Square, [hp.k], [junk.k, ss.k], accum_out=ss[:])
                rstd_from_ss(ss, 1024)
                TS("dve", xn1[:], hp[:], ss[:, 0:1], ALU.mult, [hp.k, ss.k], [xn1.k])
                pt_ = bufs["ps_tr"][0]
                for kc in range(8):
                    TR(pt_[:, kc * 128:(kc + 1) * 128], xn1[:, kc * 128:(kc + 1) * 128], identb[:],
                       [xn1.k, identb.k], [pt_.k])
                CP("act", hnT1[i2][:], pt_[:].rearrange("p (k t) -> p k t", k=8), [pt_.k], [hnT1[i2].k])
                ST(hn1T[b], hnT1[i2][:], [hnT1[i2].k], ["hn1T"])
            S.barrier()

        esD.close()
        esE = contextlib.ExitStack()
        if upto >= 5:
          woE = load_w_bf16(esE, w_out1, 2048, 1024, None, "wo1", prefetch=True)
          gwE = load_w_bf16(esE, gate_w[1], 1024, 1024, None, "gw1", prefetch=True)
          pwE = load_w_bf16(esE, ple_w[1], 256, 1024, None, "pw1", prefetch=True)
          with contextlib.ExitStack() as es:
            nwc1 = mk(es, "nwc1", [128, 8], F32)
            LD(nwc1[:], nw[1].rearrange("(k p) o -> p (k o)", p=128), [], [nwc1.k])
            W1 = load_w_bf16(es, w_in1, 1024, 6144, nwc1, "W1")
            hg = [mk(es, "hg", [128, 8, 512], BF16) for _ in range(2)]
            sgm = [mk(es, "sgm", [128, 512], F32) for _ in range(2)]
            glu = [mk(es, "glu", [128, 512], BF16) for _ in range(2)]
            sgo = [mk(es, "sgo", [128, 512], BF16) for _ in range(2)]
            pacc = [mk(es, "paccE", [128, 512], F32, psum=True) for _ in range(4)]
            na = 0
            groups = [(0, 1)] + [(1 + 4 * G, 4) for G in range(8)]
            for gi, (b0, nb) in enumerate(groups):
                tw = nb * 128
                hg_ = hg[gi % 2]
                for bi in range(nb):
                    LD(hg_[:, :, bi * 128:(bi + 1) * 128], hn1T[b0 + bi], ["hn1T"], [hg_.k])

                def proj(col0):
                    nonlocal na
                    pa = pacc[na % 4]
                    na += 1
                    for kc in range(8):
                        MM(pa[:, 0:tw], W1[:, kc, col0:col0 + 128], hg_[:, kc, 0:tw], kc == 0, kc == 7,
                           [W1.k, hg_.k], [pa.k], sig=(kc == 7))
                    return pa
                for j in range(16):
                    pu = proj(j * 128)
                    pg = proj(2048 + j * 128)
                    sg_, gl_ = sgm[j % 2], glu[j % 2]
                    ACT(sg_[:, 0:tw], pg[:, 0:tw], AF.Sigmoid, [pg.k], [sg_.k])
                    if gi == 0:
                        STT(gl_[:, 0:tw], pu[:, 0:tw], flg[:, 0:1], sg_[:, 0:tw], ALU.mult, ALU.mult,
                            [pu.k, flg.k, sg_.k], [gl_.k])
                    else:
                        TT("dve", gl_[:, 0:tw], pu[:, 0:tw], sg_[:, 0:tw], ALU.mult, [pu.k, sg_.k], [gl_.k])
                    ST(gluT[j, :, b0 * 128:b0 * 128 + tw], gl_[:, 0:tw], [gl_.k], ["gluT"])
                if gi > 0:
                    for j in range(16):
                        pg = proj(4096 + j * 128)
                        so = sgo[j % 2]
                        ACT(so[:, 0:tw], pg[:, 0:tw], AF.Silu, [pg.k], [so.k])
                        ST(sg1T[j, :, (b0 - 1) * 128:(b0 - 1) * 128 + tw], so[:, 0:tw], [so.k], ["sg1T"])
            S.barrier()

        if upto >= 6:
          with contextlib.ExitStack() as es:
            wo, gw, pw = woE, gwE, pwE
            ccwc = mk(es, "ccwc", [128, 16, 31], F32)
            LD(ccwc[:], ccw.rearrange("(c p) j -> p c j", p=128), [], [ccwc.k])
            colp = mk(es, "colp", [128, 3, 16], F32)
            for i, src in enumerate((ccb, lnw, lnb)):
                LD(colp[:, i, :], src.rearrange("(c p) o -> p (c o)", p=128), [], [colp.k + str(i)])
            fnb = mk(es, "fnb", [128, 1024], F32)
            LD(fnb[:], fnw.partition_broadcast(128), [], [fnb.k])
            bufs = tail_bufs(es)
            psb = [mk(es, "psb", [128, 512], F32, psum=True) for _ in range(6)]
            bufs["ps_g"] = [psb[2], psb[3]]
            bufs["ps_e"] = [psb[4], psb[5]]
            bufs["ps_tr"] = [mk(es, "ps_tr", [128, 1024], BF16, psum=True) for _ in range(2)]
            ps_m = [psb[0], psb[1]]
            gl = mk(es, "gl", [128, 16, 544], BF16)
            sgg = mk(es, "sgg", [128, 16, 512], BF16)
            cv = mk(es, "cv", [128, 16, 512], F32)
            dg = [mk(es, "dg", [128, 31, 128], BF16) for _ in range(2)]
            sqs = [mk(es, "sqs", [128, 512], F32) for _ in range(2)]
            mean = mk(es, "mean", [128, 512], F32)
            var = mk(es, "var", [128, 512], F32)
            tn = [mk(es, "tn", [128, 512], F32) for _ in range(2)]
            aT = sgg
            hr = [mk(es, "hr", [128, 1024], F32) for _ in range(2)]
            pr = [mk(es, "pr1", [128, 256], F32) for _ in range(2)]
            junk = mk(es, "junkE", [128, 1024], BF16)
            ss = mk(es, "ssE", [128, 1], F32)
            ot = [mk(es, "ot", [128, 1024], F32) for _ in range(2)]
            def load_gl(G):
                c0 = 128 + G * 512 - 32
                LD(gl[:], gluT[:, :, c0:c0 + 544].rearrange("j p t -> p j t"), ["gluT"], [gl.k])
            load_gl(0)
            for G in range(8):
                LD(sgg[:], sg1T[:, :, G * 512:(G + 1) * 512].rearrange("j p t -> p j t"), ["sg1T"],
                   [sgg.k + str(j) for j in range(16)])

                def stats(j):
                    sq_ = sqs[j % 2]
                    ACT(sq_[:], cv[:, j, :], AF.Square, [cv.k + str(j)], [sq_.k])
                    MM(psb[2][:], onesf, cv[:, j, :], j == 0, j == 15, [cstf.k, cv.k + str(j)], [psb[2].k], sig=False)
                    MM(psb[3][:], onesf, sq_[:], j == 0, j == 15, [cstf.k, sq_.k], [psb[3].k])
                for j in range(16):
                    d_ = dg[j % 2]
                    TT("dve", d_[:], identb[:].unsqueeze(1).to_broadcast([128, 31, 128]),
                       ccwc[:, j, :].unsqueeze(2).to_broadcast([128, 31, 128]), ALU.mult, [identb.k, ccwc.k], [d_.k])
                    pc = psb[j % 2]
                    for tp in range(31):
                        MM(pc[:], d_[:, tp, :], gl[:, j, 2 + tp:2 + tp + 512], tp == 0, tp == 30, [d_.k, gl.k], [pc.k], sig=(tp == 30))
                    ACT(cv[:, j, :], pc[:], AF.Identity, [pc.k, colp.k + "0"], [cv.k + str(j)], bias=colp[:, 0, j:j + 1])
                    if j > 0:
                        stats(j - 1)
                stats(15)
                if G + 1 < 8:
                    load_gl(G + 1)
                ACT(mean[:], psb[2][:], AF.Copy, [psb[2].k], [mean.k], scale=1.0 / 2048)
                TT("dve", var[:], mean[:], mean[:], ALU.mult, [mean.k], [var.k])
                STT(var[:], psb[3][:], 1.0 / 2048, var[:], ALU.mult, ALU.subtract, [psb[3].k, var.k], [var.k])
                ACT(var[:], var[:], AF.Sqrt, [var.k], [var.k], bias=EPS)
                RECIP(var[:], var[:], [var.k], [var.k])
                for j in range(16):
                    t_ = tn[j % 2]
                    TT("dve", t_[:], cv[:, j, :], mean[:], ALU.subtract, [cv.k + str(j), mean.k], [t_.k])
                    TT("dve", t_[:], t_[:], var[:], ALU.mult, [t_.k, var.k], [t_.k])
                    ACT(t_[:], t_[:], AF.Silu, [t_.k, colp.k + "1", colp.k + "2"], [t_.k],
                        scale=colp[:, 1, j:j + 1], bias=colp[:, 2, j:j + 1])
                    TT("dve", aT[:, j, :], t_[:], sgg[:, j, :], ALU.mult, [t_.k, sgg.k + str(j)], [sgg.k + str(j)])
                def outprojE(bi, G=G):
                    b = G * 4 + bi
                    i2 = bi % 2
                    LD(hr[i2][:], h1S[1 + b], ["h1S"], [hr[i2].k])
                    LD(pr[i2][:], p1[b], [], [pr[i2].k])
                    for hf in range(2):
                        for kc in range(16):
                            MM(ps_m[hf][:], aT[:, kc, bi * 128:(bi + 1) * 128], wo[:, kc, hf * 512:(hf + 1) * 512],
                               kc == 0, kc == 15, [aT.k + str(kc), wo.k], [ps_m[hf].k], sig=(kc == 15))
                outprojE(0)
                for bi in range(4):
                    b = G * 4 + bi
                    i2 = bi % 2
                    hp = ple_tail(bufs, 1, ps_m, hr[i2], pr[i2], (gw, pw),
                                  mid=(lambda bi=bi: outprojE(bi + 1)) if bi + 1 < 4 else None)
                    ACT(junk[:], hp[:], AF.Square, [hp.k], [junk.k, ss.k], accum_out=ss[:])
                    rstd_from_ss(ss, 1024)
                    STT(ot[i2][:], hp[:], ss[:, 0:1], fnb[:], ALU.mult, ALU.mult, [hp.k, ss.k, fnb.k], [ot[i2].k])
                    ST(out[b], ot[i2][:], [ot[i2].k], ["out"])
            S.barrier()
        esE.close()
        S.barrier()
    S.finish()
    return nc


def _consts():
    ident = np.eye(128, dtype=np.float32)
    perm = np.zeros((128, 128), np.float32)
    for m in range(128):
        d = m % 64
        k = m + 32 if d < 32 else m - 32
        perm[k, m] = 1.0
    j = np.arange(128)
    mle = (j[:, None] <= j[None, :]).astype(np.float32)
    mgt = (j[:, None] > j[None, :]).astype(np.float32)
    ones = np.ones((128, 128), np.float32)
    return np.stack([ident, perm, mle, mgt, ones])


def _rope_tables(pos):
    inv = (10000.0 ** (-np.arange(0, 64, 2, dtype=np.float32) / 64)).astype(np.float32)
    ang = pos.astype(np.float32)[None, :] * inv[:, None]
    c = np.cos(ang).astype(np.float32)
    s = np.sin(ang).astype(np.float32)
    cos128 = np.concatenate([c, c, c, c], axis=0)
    sin128 = np.concatenate([-s, s, -s, s], axis=0)
    return np.ascontiguousarray(cos128), np.ascontiguousarray(sin128)


def make_in_maps(inp, cores=range(8)):
    f = lambda a: np.ascontiguousarray(np.asarray(a, dtype=np.float32))
    x = np.asarray(inp["x"], np.float32)
    p = np.asarray(inp["p"], np.float32)
    shared = {
        "cst": _consts(),
        "w_in0": f(inp["even_w_in"][0]),
        "nw": f(np.asarray(inp["norm_w"])[:, :, None]),
        "cw0": f(np.asarray(inp["ssd_conv_w"])[0].T),
        "cb0": f(np.asarray(inp["ssd_conv_b"])[0][:, None]),
        "dtb": f(np.asarray(inp["ssd_dt_bias"])[0][None]),
        "alog": f(np.asarray(inp["ssd_a_log"])[0][None]),
        "dfull": f(np.repeat(np.asarray(inp["ssd_d"])[0], 64)[None]),
        "snw": f(np.asarray(inp["ssd_norm_w"])[0][None]),
        "lamv": f(np.asarray(inp["diff_lambda"])[0].reshape(1, 256)),
        "subw": f(np.asarray(inp["diff_subln_w"])[0][:, None]),
        "w_out0": f(inp["even_w_out"][0]),
        "ple_w": f(inp["ple_w"]),
        "gate_w": f(inp["ple_gate_w"]),
        "w_in1": f(inp["conf_w_in"][0]),
        "ccw": f(np.asarray(inp["conf_conv_w"])[0].T),
        "ccb": f(np.asarray(inp["conf_conv_b"])[0][:, None]),
        "lnw": f(np.asarray(inp["conf_ln_w"])[0][:, None]),
        "lnb": f(np.asarray(inp["conf_ln_b"])[0][:, None]),
        "w_out1": f(inp["conf_w_out"][0]),
        "fnw": f(np.asarray(inp["final_norm_w"])[None]),
    }
    maps = []
    for c in cores:
        b, s = c // 2, c % 2
        t0 = s * 4096
        xl = np.zeros((8192, 1024), np.float32)
        p0 = np.zeros((NHB * 128, 256), np.float32)
        if s == 1:
            xl[:] = x[b]
            p0[:] = p[0, b, 4096 - 128:8192]
        else:
            xl[4096:] = x[b, 0:4096]
            p0[128:] = p[0, b, 0:4096]
        p1 = p[1, b, t0:t0 + 4096]
        pos = np.arange(8192) + (t0 - 4096)
        cs, sn = _rope_tables(pos)
        fl = np.zeros((128, 2), np.float32)
        fl[:, 0] = 1.0 if s == 1 else 0.0
        fl[:, 1] = 0.0 if s == 1 else -30000.0
        m = dict(shared)
        m.update({
            "x_loc": xl.reshape(NBLK, 128, 1024),
            "p0": p0.reshape(NHB, 128, 256),
            "p1": f(p1).reshape(32, 128, 256),
            "flagc": fl, "cosT": cs, "sinT": sn,
        })
        maps.append(m)
    return maps


def kernel(**inputs):
    nc = bass.Bass("TRN2", target_bir_lowering=False)
    build(nc)
    maps = make_in_maps(inputs)
    res = run_bass_kernel_spmd(nc, maps, core_ids=list(range(8)))
    out = np.zeros((4, 8192, 1024), np.float32)
    for c in range(8):
        b, s = c // 2, c % 2
        out[b, s * 4096:(s + 1) * 4096] = np.asarray(res.results[c]["out"]).reshape(4096, 1024)
    return out
```

```python
import contextlib
import math
import numpy as np
import concourse.bass as bass
import concourse.mybir as mybir
from concourse.bass_utils import run_bass_kernel_spmd

F32 = mybir.dt.float32
BF16 = mybir.dt.bfloat16
AF = mybir.ActivationFunctionType
ALU = mybir.AluOpType

EPS = 1e-6
NBLK = 64
HALO = 31
FB0 = 28
NFB = NBLK - FB0
NHB = NBLK - HALO
import os
NGDBG = int(os.environ.get('K_NG', '16'))
KSTOP = int(os.environ.get('K_STOP', '99'))
NPT = int(os.environ.get('K_NPT', '4'))
QKD = int(os.environ.get('K_QKD', '2'))
EMBED_WAIT = int(os.environ.get('K_EMB', '2'))
NPS = int(os.environ.get('K_NPS', '3'))


class Sched:
    def __init__(self, nc, n_dma_sems=14):
        self.nc = nc
        self.engs = ("pe", "act", "dve", "pool", "sp")
        self.ops = {k: [] for k in self.engs}
        self.psem = {k: nc.alloc_semaphore(f"prog_{k}") for k in ("pe", "act", "dve", "pool")}
        self.cnt = {k: 0 for k in self.psem}
        self.waited = {k: {} for k in self.engs}
        self.rings = {q: [[nc.alloc_semaphore(f"dma_{q}_{i}"), 0] for i in range(n_dma_sems)]
                      for q in ("sp", "pool", "act")}
        self.ring_pos = {q: 0 for q in self.rings}
        self.last_w = {}
        self.readers = {}
        self.sems = {}
        self.unsignaled = {k: False for k in self.engs}

    def _waits(self, eng, deps):
        need = {}
        for (sem, val, src) in deps:
            if src == "pe" and eng == "pe":
                continue
            sid = id(sem)
            self.sems[sid] = sem
            if self.waited[eng].get(sid, 0) >= val:
                continue
            if need.get(sid, 0) < val:
                need[sid] = val
        out = []
        for sid, val in need.items():
            self.waited[eng][sid] = val
            out.append((self.sems[sid], val))
        return out

    def _deps(self, reads, writes):
        deps = []
        for k in reads:
            t = self.last_w.get(k)
            if t is not None:
                deps.append(t)
        for k in writes:
            t = self.last_w.get(k)
            if t is not None:
                deps.append(t)
            deps.extend(self.readers.get(k, {}).values())
        return deps

    def _record(self, tok, reads, writes):
        sid = id(tok[0])
        for k in writes:
            self.last_w[k] = tok
            self.readers[k] = {}
        for k in reads:
            self.readers.setdefault(k, {})[sid] = tok

    def emit(self, eng, fn, reads=(), writes=(), signal=True):
        xr = [k for k in reads if k.startswith("P:")]
        if xr:
            writes = list(writes) + xr
            reads = [k for k in reads if not k.startswith("P:")]
        waits = self._waits(eng, self._deps(reads, writes))
        if signal:
            self.cnt[eng] += 1
            tok = (self.psem[eng], self.cnt[eng], eng)
            self.ops[eng].append((waits, fn, (self.psem[eng], 1)))
            self.unsignaled[eng] = False
        else:
            tok = (self.psem[eng], self.cnt[eng] + 1, eng)
            self.ops[eng].append((waits, fn, None))
            self.unsignaled[eng] = True
        self._record(tok, reads, writes)

    def dma(self, q, out, in_, reads=(), writes=(), **kw):
        ring = self.rings[q]
        i = self.ring_pos[q]
        self.ring_pos[q] = (i + 1) % len(ring)
        sem, c = ring[i]
        deps = self._deps(reads, writes)
        if c > 0:
            deps.append((sem, c, None))
        waits = self._waits(q, deps)
        ring[i][1] = c + 16
        tok = (sem, c + 16, None)
        self.ops[q].append((waits, lambda e: e.dma_start(out=out, in_=in_, **kw), (sem, 16)))
        self._record(tok, reads, writes)

    def barrier(self):
        assert not any(self.unsignaled.values()), self.unsignaled
        toks = [(self.psem[k], self.cnt[k], None) for k in self.psem if self.cnt[k] > 0]
        for q in self.rings:
            for sem, c in self.rings[q]:
                if c > 0:
                    toks.append((sem, c, None))
        for e in self.engs:
            waits = self._waits(e, toks)
            if waits:
                self.ops[e].append((waits, None, None))
        self.last_w = {}
        self.readers = {}

    def finish(self):
        nc = self.nc
        with nc.allow_non_contiguous_dma(reason="small strided parameter loads"), nc.Block() as block:
            def replay(name):
                def f(e):
                    emb = EMBED_WAIT and name != "sp" or EMBED_WAIT == 2
                    for waits, fn, inc in self.ops[name]:
                        if fn is None or not emb or not waits:
                            for sem, val in waits:
                                e.wait_ge(sem, val)
                            if fn is not None:
                                ins = fn(e)
                                if inc is not None:
                                    ins.then_inc(inc[0], inc[1])
                        else:
                            for sem, val in waits[:-1]:
                                e.wait_ge(sem, val)
                            ins = fn(e)
                            ins._wait_ge(waits[-1][0], waits[-1][1])
                            if inc is not None:
                                ins.then_inc(inc[0], inc[1])
                return f
            block.sync(replay("sp"))
            block.scalar(replay("act"))
            block.vector(replay("dve"))
            block.gpsimd(replay("pool"))
            block.tensor(replay("pe"))


class B:
    def __init__(self, t, k):
        self.t = t
        self.k = k

    def __getitem__(self, idx):
        return self.t[idx]


def build(nc, dbg=False, upto=99):
    S = Sched(nc)
    kind_s = "ExternalOutput" if dbg else "Internal"

    def din(name, shape, dt=F32):
        return nc.dram_tensor(name, list(shape), dt, kind="ExternalInput").ap()

    def dscr(name, shape, dt):
        return nc.dram_tensor(name, list(shape), dt, kind=kind_s).ap()

    x_loc = din("x_loc", [NBLK, 128, 1024])
    p0 = din("p0", [NHB, 128, 256])
    p1 = din("p1", [32, 128, 256])
    flagc = din("flagc", [128, 2])
    cosT = din("cosT", [128, 8192])
    sinT = din("sinT", [128, 8192])
    cst = din("cst", [5, 128, 128])
    w_in0 = din("w_in0", [1024, 6672])
    nw = din("nw", [2, 1024, 1])
    cw0 = din("cw0", [1536, 4])
    cb0 = din("cb0", [1536, 1])
    dtb = din("dtb", [1, 16])
    alog = din("alog", [1, 16])
    dfull = din("dfull", [1, 1024])
    snw = din("snw", [1, 1024])
    lamv = din("lamv", [1, 256])
    subw = din("subw", [128, 1])
    w_out0 = din("w_out0", [2048, 1024])
    ple_w = din("ple_w", [2, 256, 1024])
    gate_w = din("gate_w", [2, 1024, 1024])
    w_in1 = din("w_in1", [1024, 6144])
    ccw = din("ccw", [2048, 31])
    ccb = din("ccb", [2048, 1])
    lnw = din("lnw", [2048, 1])
    lnb = din("lnb", [2048, 1])
    w_out1 = din("w_out1", [2048, 1024])
    fnw = din("fnw", [1, 1024])
    out = nc.dram_tensor("out", [32, 128, 1024], F32, kind="ExternalOutput").ap()

    kT = dscr("kT", [8, 128, 8192], BF16)
    vS = dscr("vS", [NBLK, 128, 1024], BF16)
    qT = dscr("qT", [8, 128, NFB * 128], BF16)
    zs = dscr("zs", [NFB, 128, 1024], BF16)
    gsT = dscr("gsT", [8, 128, NFB * 128], BF16)
    xsS = dscr("xsS", [NBLK, 128, 1024], BF16)
    bS = dscr("bS", [NBLK, 128, 256], BF16)
    bTS = dscr("bTS", [NBLK, 128, 2, 128], BF16)
    cTS = dscr("cTS", [NFB, 128, 2, 128], BF16)
    dtS = dscr("dtS", [NBLK, 128, 16], F32)
    ymix = dscr("ymix", [NHB, 128, 16, 128], BF16)
    h1S = dscr("h1S", [NHB, 128, 1024], F32)
    hn1T = dscr("hn1T", [NHB, 128, 8, 128], BF16)
    gluT = dscr("gluT", [16, 128, NHB * 128], BF16)
    sg1T = dscr("sg1T", [16, 128, 4096], BF16)

    uid = [0]

    def mk(es, name, shape, dt, psum=False):
        uid[0] += 1
        nm = f"{name}_{uid[0]}"
        if psum:
            t = es.enter_context(nc.psum_tensor(nm, list(shape), dt))
        else:
            t = es.enter_context(nc.sbuf_tensor(nm, list(shape), dt))
        return B(t, ("P:" + nm) if psum else nm)

    def MM(o, lhsT, rhs, start, stop, r, w, sig=True):
        S.emit("pe", lambda e: e.matmul(out=o, lhsT=lhsT, rhs=rhs, start=start, stop=stop), r, w, signal=sig)

    def TR(o, in_, ident, r, w, sig=True):
        S.emit("pe", lambda e: e.transpose(out=o, in_=in_, identity=ident), r, w, signal=sig)

    def ACT(o, in_, func, r, w, **kw):
        S.emit("act", lambda e: e.activation(out=o, in_=in_, func=func, **kw), r, w)

    def TT(eng, o, a, b, op, r, w):
        S.emit(eng, lambda e: e.tensor_tensor(out=o, in0=a, in1=b, op=op), r, w)

    def TS(eng, o, a, s1, op0, r, w, s2=None, op1=None):
        if op1 is None:
            S.emit(eng, lambda e: e.tensor_scalar(out=o, in0=a, scalar1=s1, scalar2=None, op0=op0), r, w)
        else:
            S.emit(eng, lambda e: e.tensor_scalar(out=o, in0=a, scalar1=s1, scalar2=s2, op0=op0, op1=op1), r, w)

    def STT(o, a, s, b, op0, op1, r, w):
        S.emit("dve", lambda e: e.scalar_tensor_tensor(out=o, in0=a, scalar=s, in1=b, op0=op0, op1=op1), r, w)

    def CP(eng, o, a, r, w):
        if eng == "act":
            ACT(o, a, AF.Copy, r, w)
        else:
            S.emit(eng, lambda e: e.tensor_copy(out=o, in_=a), r, w)

    def RECIP(o, a, r, w):
        S.emit("dve", lambda e: e.reciprocal(out=o, in_=a), r, w)

    def LD(o, in_, r, w, q="sp", **kw):
        S.dma(q, o, in_, r, w, **kw)

    def ST(o, in_, r, w, q="pool", **kw):
        S.dma(q, o, in_, r, w, **kw)

    def rstd_from_ss(ss, n, eps=EPS):
        ACT(ss[:], ss[:], AF.Sqrt, [ss.k], [ss.k], scale=1.0 / n, bias=eps)
        RECIP(ss[:], ss[:], [ss.k], [ss.k])

    gs = contextlib.ExitStack()
    with gs:
        cstf = mk(gs, "cstf", [128, 5, 128], F32)
        LD(cstf[:], cst.rearrange("c p f -> p c f"), [], [cstf.k])
        identb = mk(gs, "identb", [128, 128], BF16)
        permb = mk(gs, "permb", [128, 128], BF16)
        CP("dve", identb[:], cstf[:, 0, :], [cstf.k], [identb.k])
        CP("dve", permb[:], cstf[:, 1, :], [cstf.k], [permb.k])
        mle = cstf[:, 2, :]
        mgt = cstf[:, 3, :]
        onesf = cstf[:, 4, :]
        mleb = mk(gs, "mleb", [128, 128], BF16)
        CP("dve", mleb[:], mle, [cstf.k], [mleb.k])
        flg = mk(gs, "flg", [128, 2], F32)
        LD(flg[:], flagc, [], [flg.k])

        def load_w_bf16(es, src, rows, cols, scale_col=None, name="w", prefetch=False):
            kcs = rows // 128
            wt = mk(es, name, [128, kcs, cols], BF16)
            CH = 512
            ls = es if prefetch else contextlib.ExitStack()
            nst = 3 if prefetch else 6
            stg = [mk(ls, "wstg", [128, CH], F32) for _ in range(nst)]
            n = 0
            for kc in range(kcs):
                for c0 in range(0, cols, CH):
                    cw = min(CH, cols - c0)
                    st = stg[n % nst]
                    LD(st[:, 0:cw], src[kc * 128:(kc + 1) * 128, c0:c0 + cw], [], [st.k])
                    eng = "pool" if prefetch else ("dve", "act")[n % 2]
                    o_, i_ = wt[:, kc, c0:c0 + cw], st[:, 0:cw]
                    wk = wt.k + ("" if not prefetch else "")
                    if eng == "act":
                        if scale_col is not None:
                            ACT(o_, i_, AF.Copy, [st.k, scale_col.k], [wk], scale=scale_col[:, kc:kc + 1])
                        else:
                            ACT(o_, i_, AF.Copy, [st.k], [wk])
                    elif scale_col is not None:
                        TS(eng, o_, i_, scale_col[:, kc:kc + 1], ALU.mult, [st.k, scale_col.k], [wk],
                           s2=0.0, op1=ALU.add)
                    else:
                        TS(eng, o_, i_, 1.0, ALU.mult, [st.k], [wk], s2=0.0, op1=ALU.add)
                    n += 1
            if not prefetch:
                S.barrier()
                ls.close()
            return wt

        with contextlib.ExitStack() as es:
            nwc = mk(es, "nwc", [128, 8], F32)
            LD(nwc[:], nw[0].rearrange("(k p) o -> p (k o)", p=128), [], [nwc.k])
            W0 = load_w_bf16(es, w_in0, 1024, 6672, nwc, "W0")
            cwc = mk(es, "cwc", [128, 12, 4], F32)
            LD(cwc[:], cw0.rearrange("(c p) j -> p c j", p=128), [], [cwc.k])
            cbc = mk(es, "cbc", [128, 12], F32)
            LD(cbc[:], cb0.rearrange("(c p) o -> p (c o)", p=128), [], [cbc.k])
            dtbb = mk(es, "dtbb", [128, 16], F32)
            LD(dtbb[:], dtb.partition_broadcast(128), [], [dtbb.k])
            hist = mk(es, "hist", [128, 12, 3], F32)
            S.emit("dve", lambda e: e.memset(hist[:], 0.0), [], [hist.k])
            xblk = [mk(es, "xblk", [128, 1024], F32) for _ in range(2)]
            junk = mk(es, "junk", [128, 1024], BF16)
            ssb = [mk(es, "ssb", [128, 1], F32) for _ in range(2)]
            xn = [mk(es, "xn", [128, 1024], BF16) for _ in range(4)]
            hnTs = [mk(es, "hnT", [128, 8, 512], BF16) for _ in range(2)]
            hnT = hnTs[0]
            ssb4 = [mk(es, "ssb4", [128, 1], F32) for _ in range(4)]
            pend = []
            raw = [mk(es, "raw", [128, 515], F32) for _ in range(3)]
            cacc = [mk(es, "cacc", [128, 512], F32) for _ in range(3)]
            xa = [mk(es, "xa", [128, 512], BF16) for _ in range(3)]
            xsg = mk(es, "xsg", [128, 4, 1024], BF16)
            bg = mk(es, "bg", [128, 4, 256], BF16)
            cosg = mk(es, "cosg", [128, 512], F32)
            sing = mk(es, "sing", [128, 512], F32)
            qb = [mk(es, "qb", [128, 512], BF16) for _ in range(3)]
            t1 = [mk(es, "t1", [128, 512], F32) for _ in range(3)]
            t2 = [mk(es, "t2", [128, 512], F32) for _ in range(3)]
            qr = [mk(es, "qr", [128, 512], BF16) for _ in range(3)]
            fo = [mk(es, "fo", [128, 512], BF16) for _ in range(2)]
            tmo = [mk(es, "tmo", [128, 1024], BF16) for _ in range(2)]
            dts = mk(es, "dts", [128, 4, 16], F32)
            pacc = [mk(es, "pacc", [128, 512], F32, psum=True) for _ in range(3)]
            ptr = [mk(es, "ptr", [128, 1024], BF16, psum=True) for _ in range(2)]
            pperm = [mk(es, "pperm", [128, 512], F32, psum=True) for _ in range(2)]
            pdt = mk(es, "pdt", [128, 64], F32, psum=True)
            rot = {"acc": 0, "i": 0}

            PDEPTH = 2

            def flush(min_age=0):
                keep, run = [], []
                for ent in pend:
                    (run if ent[0] >= min_age else keep).append(ent)
                pend[:] = keep
                for ent in run:
                    ent[1]()

            def defer(fn):
                pend.append([0, fn])

            def proj_fm(col0, M=128):
                pa = pacc[rot["acc"] % 3]
                rot["acc"] += 1
                for kc in range(8):
                    MM(pa[0:M, :], W0[:, kc, col0:col0 + M], hnT[:, kc, :], kc == 0, kc == 7,
                       [W0.k, hnT.k], [pa.k], sig=(kc == 7))
                for ent in pend:
                    ent[0] += 1
                flush(PDEPTH)
                return pa

            def fe_part1(g):
                for bi in range(4):
                    blk = g * 4 + bi
                    xb_, s_, xn_ = xblk[bi % 2], ssb4[bi], xn[bi]
                    LD(xb_[:], x_loc[blk], [], [xb_.k])
                    ACT(junk[:], xb_[:], AF.Square, [xb_.k], [junk.k, s_.k], accum_out=s_[:])
                    rstd_from_ss(s_, 1024)
                    TS("dve", xn_[:], xb_[:], s_[:, 0:1], ALU.mult, [xb_.k, s_.k], [xn_.k])

            def fe_part2(g):
                hn = hnTs[g % 2]
                for bi in range(4):
                    xn_, pt_ = xn[bi], ptr[bi % 2]
                    for kc in range(8):
                        TR(pt_[:, kc * 128:(kc + 1) * 128], xn_[:, kc * 128:(kc + 1) * 128], identb[:],
                           [xn_.k, identb.k], [pt_.k])
                    CP("act", hn[:, :, bi * 128:(bi + 1) * 128],
                       pt_[:].rearrange("p (k t) -> p k t", k=8), [pt_.k], [hn.k])

            NGA = min(16, NGDBG) if upto >= 1 else 0
            if NGA:
                fe_part1(0)
                fe_part2(0)
            for g in range(NGA):
                full = g >= 7
                t0 = g * 512
                hnT = hnTs[g % 2]
                if g + 1 < NGA:
                    fe_part1(g + 1)
                if KSTOP < 2:
                    continue
                for bi in range(4):
                    for kc in range(8):
                        MM(pdt[:, bi * 16:(bi + 1) * 16], hnT[:, kc, bi * 128:(bi + 1) * 128],
                           W0[:, kc, 2560:2576], kc == 0, kc == 7, [hnT.k, W0.k], [pdt.k], sig=(kc == 7))
                TT("dve", dts[:], pdt[:].rearrange("p (b h) -> p b h", b=4),
                   dtbb[:].unsqueeze(1).to_broadcast([128, 4, 16]), ALU.add, [pdt.k, dtbb.k], [dts.k])
                ACT(dts[:], dts[:], AF.Exp, [dts.k], [dts.k])
                ACT(dts[:], dts[:], AF.Ln, [dts.k], [dts.k], bias=1.0)
                ST(dtS[g * 4:(g + 1) * 4].rearrange("b p h -> p b h"), dts[:], [dts.k], ["dtS"])
                def age_flush():
                    for ent in pend:
                        ent[0] += 1
                    flush(PDEPTH)

                def item_x(j, g=g, full=full):
                    pa = proj_fm(1024 + j * 128)
                    rw, ca, xa_ = raw[j % 3], cacc[j % 3], xa[j % 3]
                    CP("act", rw[:, 3:515], pa[:], [pa.k], [rw.k])
                    CP("pool", rw[:, 0:3], hist[:, j, :], [hist.k, rw.k], [rw.k])
                    TS("dve", ca[:], rw[:, 0:512], cwc[:, j, 0:1], ALU.mult, [rw.k, cwc.k, cbc.k], [ca.k],
                       s2=cbc[:, j:j + 1], op1=ALU.add)
                    for tp in range(1, 4):
                        STT(ca[:], rw[:, tp:tp + 512], cwc[:, j, tp:tp + 1], ca[:], ALU.mult, ALU.add,
                            [rw.k, cwc.k, ca.k], [ca.k])
                    CP("pool", hist[:, j, :], rw[:, 512:515], [rw.k], [hist.k])
                    ACT(xa_[:], ca[:], AF.Silu, [ca.k], [xa_.k])

                    def after_x():
                        if j < 10:
                            pt_ = ptr[j % 2]
                            for bi in range(4):
                                TR(pt_[:, bi * 128:(bi + 1) * 128], xa_[:, bi * 128:(bi + 1) * 128], identb[:],
                                   [xa_.k, identb.k], [pt_.k])
                            if j < 8:
                                CP("act", xsg[:, :, j * 128:(j + 1) * 128],
                                   pt_[:, 0:512].rearrange("p (b t) -> p b t", b=4), [pt_.k], [xsg.k])
                            else:
                                CP("act", bg[:, :, (j - 8) * 128:(j - 7) * 128],
                                   pt_[:, 0:512].rearrange("p (b t) -> p b t", b=4), [pt_.k], [bg.k])
                                ST(bTS[g * 4:(g + 1) * 4, :, j - 8, :].rearrange("b p t -> p b t"),
                                   xa_[:].rearrange("p (b t) -> p b t", b=4), [xa_.k], ["bTS"])
                        elif full:
                            ST(cTS[g * 4 - FB0:(g + 1) * 4 - FB0, :, j - 10, :].rearrange("b p t -> p b t"),
                               xa_[:].rearrange("p (b t) -> p b t", b=4), [xa_.k], ["cTS"])
                    defer(after_x)

                def item_g(c, t0=t0):
                    pa = proj_fm(5648 + c * 128)
                    fo_ = fo[c % 2]
                    ACT(fo_[:], pa[:], AF.Silu, [pa.k], [fo_.k])
                    ST(gsT[c, :, t0 - FB0 * 128:t0 - FB0 * 128 + 512], fo_[:], [fo_.k], ["gsT"])

                def item_tm(kind, bi, g=g):
                    cbase = 0 if kind == "z" else 4624
                    tm = tmo[bi % 2]
                    for hf in range(2):
                        pa = pacc[rot["acc"] % 3]
                        rot["acc"] += 1
                        for kc in range(8):
                            MM(pa[:], hnT[:, kc, bi * 128:(bi + 1) * 128],
                               W0[:, kc, cbase + hf * 512:cbase + (hf + 1) * 512], kc == 0, kc == 7,
                               [hnT.k, W0.k], [pa.k], sig=(kc == 7))
                        age_flush()
                        if kind == "z":
                            ACT(tm[:, hf * 512:(hf + 1) * 512], pa[:], AF.Silu, [pa.k], [tm.k])
                        else:
                            CP("act" if hf == 0 else "dve", tm[:, hf * 512:(hf + 1) * 512], pa[:], [pa.k], [tm.k])
                    blk = g * 4 + bi
                    if kind == "z":
                        ST(zs[blk - FB0], tm[:], [tm.k], ["zs"])
                    else:
                        ST(vS[blk], tm[:], [tm.k], ["vS"])

                rope_n = [0]

                def item_rope(kind, h, t0=t0):
                    col0 = (2576 if kind == "q" else 3600) + h * 128
                    pa = proj_fm(col0)
                    n = rope_n[0]
                    rope_n[0] += 1
                    qb_, t1_, t2_, qr_, pp = qb[n % 3], t1[n % 3], t2[n % 3], qr[n % 3], pperm[n % 2]
                    CP("act", qb_[:], pa[:], [pa.k], [qb_.k])

                    def after_q():
                        MM(pp[:], permb[:], qb_[:], True, True, [permb.k, qb_.k], [pp.k])
                        TT("dve", t1_[:], pa[:], cosg[:], ALU.mult, [pa.k, cosg.k], [t1_.k])
                        TT("dve", t2_[:], pp[:], sing[:], ALU.mult, [pp.k, sing.k], [t2_.k])
                        TT("dve", qr_[:], t1_[:], t2_[:], ALU.add, [t1_.k, t2_.k], [qr_.k])
                        if kind == "q":
                            ST(qT[h, :, t0 - FB0 * 128:t0 - FB0 * 128 + 512], qr_[:], [qr_.k], ["qT"])
                        else:
                            ST(kT[h, :, t0:t0 + 512], qr_[:], [qr_.k], ["kT"])
                    defer(after_q)

                LD(cosg[:], cosT[:, t0:t0 + 512], [], [cosg.k])
                LD(sing[:], sinT[:, t0:t0 + 512], [], [sing.k])
                light = []
                if full:
                    light += [lambda c=c: item_g(c) for c in range(8)]
                    light += [lambda bi=bi: item_tm("z", bi) for bi in range(4)]
                light += [lambda bi=bi: item_tm("v", bi) for bi in range(4)]
                ropes = [lambda kind=kind, h=h: item_rope(kind, h)
                         for kind in ((["q"] if full else []) + ["k"]) for h in range(8)]
                li = 0
                for j in range(12):
                    item_x(j)
                    take = (len(light) * (j + 1)) // 12 - (len(light) * j) // 12
                    for _ in range(take):
                        light[li]()
                        li += 1
                    if j == 5 and g + 1 < NGA:
                        fe_part2(g + 1)
                flush()
                ST(xsS[g * 4:(g + 1) * 4].rearrange("b p f -> p b f"), xsg[:], [xsg.k], ["xsS"])
                ST(bS[g * 4:(g + 1) * 4].rearrange("b p f -> p b f"), bg[:], [bg.k], ["bS"])
                for it in ropes:
                    it()
                flush()
            S.barrier()

        if upto >= 2:
          with contextlib.ExitStack() as es:
            anegb = mk(es, "anegb", [128, 16], F32)
            LD(anegb[:], alog.partition_broadcast(128), [], [anegb.k])
            ACT(anegb[:], anegb[:], AF.Exp, [anegb.k], [anegb.k])
            TS("dve", anegb[:], anegb[:], -1.0, ALU.mult, [anegb.k], [anegb.k])
            dfl = mk(es, "dfl", [128, 1024], F32)
            LD(dfl[:], dfull.partition_broadcast(128), [], [dfl.k])
            snwb = mk(es, "snwb", [128, 1024], F32)
            LD(snwb[:], snw.partition_broadcast(128), [], [snwb.k])
            Sf = mk(es, "Sf", [128, 2, 512], F32)
            Sb = mk(es, "Sb", [128, 2, 512], BF16)
            S.emit("dve", lambda e: e.memset(Sf[:], 0.0), [], [Sf.k])
            S.emit("pool", lambda e: e.memset(Sb[:], 0.0), [], [Sb.k])
            xs_ = [mk(es, "xs", [128, 1024], BF16) for _ in range(2)]
            bt_ = [mk(es, "bt", [128, 256], BF16) for _ in range(2)]
            bT_ = [mk(es, "bT", [128, 2, 128], BF16) for _ in range(2)]
            cT_ = [mk(es, "cT", [128, 2, 128], BF16) for _ in range(2)]
            dt_ = [mk(es, "dt", [128, 16], F32) for _ in range(2)]
            z_ = [mk(es, "z", [128, 1024], BF16) for _ in range(2)]
            a_2 = [mk(es, "a", [128, 16], F32) for _ in range(2)]
            cs_2 = [mk(es, "cs", [128, 16], F32) for _ in range(2)]
            e_2 = [mk(es, "e", [128, 16], F32) for _ in range(2)]
            dec_2 = [mk(es, "dec", [128, 16], F32) for _ in range(2)]
            cd_2 = [mk(es, "cd", [128, 16], F32) for _ in range(2)]
            wv_2 = [mk(es, "wv", [128, 16], F32) for _ in range(2)]
            xd_2 = [mk(es, "xd", [128, 1024], BF16) for _ in range(2)]
            xdd_2 = [mk(es, "xdd", [128, 1024], BF16) for _ in range(2)]
            cbm4 = [mk(es, "cbm", [128, 128], F32) for _ in range(4)]
            lh = [mk(es, "lh", [128, 4, 128], F32) for _ in range(2)]
            eD = [mk(es, "eD", [128, 512], F32) for _ in range(2)]
            Mt = [mk(es, "Mt", [128, 4, 128], BF16) for _ in range(2)]
            y12 = [mk(es, "y1", [128, 1024], F32) for _ in range(2)]
            y32 = [mk(es, "y3", [128, 1024], F32) for _ in range(2)]
            yj2 = [mk(es, "yj", [128, 512], F32) for _ in range(2)]
            ss22 = [mk(es, "ss2", [128, 2], F32) for _ in range(2)]
            yn2 = [mk(es, "yn", [128, 1024], BF16) for _ in range(2)]
            ynT = [mk(es, "ynT", [128, 8, 128], BF16) for _ in range(2)]
            ps_s = mk(es, "ps_s", [128, 512], F32, psum=True)
            ps_Ds = [mk(es, "ps_D", [128, 512], F32, psum=True) for _ in range(2)]
            ps_yo1 = mk(es, "ps_yo", [128, 512], F32, psum=True)
            ps_y = [mk(es, "ps_y", [128, 512], F32, psum=True) for _ in range(2)]
            ps_st = mk(es, "ps_st", [128, 512], F32, psum=True)
            ps_tr = mk(es, "ps_tr", [128, 1024], BF16, psum=True)
            for c in range(NBLK):
                full = c >= HALO
                i2 = c % 2
                xs, bt, bT, cT, dt, z = xs_[i2], bt_[i2], bT_[i2], cT_[i2], dt_[i2], z_[i2]
                a_, cs_, e_, dec_, cd_, wv_, xd_, xdd_ = a_2[i2], cs_2[i2], e_2[i2], dec_2[i2], cd_2[i2], wv_2[i2], xd_2[i2], xdd_2[i2]
                y1, y3, yj, ss2, yn = y12[i2], y32[i2], yj2[i2], ss22[i2], yn2[i2]
                cbm = cbm4[i2 * 2:i2 * 2 + 2]
                LD(xs[:], xsS[c], ["xsS"], [xs.k])
                LD(bt[:], bS[c], ["bS"], [bt.k])
                LD(dt[:], dtS[c], ["dtS"], [dt.k])
                if full:
                    LD(bT[:], bTS[c], ["bTS"], [bT.k])
                    LD(cT[:], cTS[c - FB0], ["cTS"], [cT.k])
                    LD(z[:], zs[c - FB0], ["zs"], [z.k])
                TT("dve", a_[:], dt[:], anegb[:], ALU.mult, [dt.k, anegb.k], [a_.k])
                MM(ps_s[:, 0:16], mle, a_[:], True, True, [cstf.k, a_.k], ["P:ps_s"])
                MM(ps_s[:, 16:32], onesf, a_[:], True, True, [cstf.k, a_.k], ["P:ps_s"])
                CP("act", cs_[:], ps_s[:, 0:16], ["P:ps_s"], [cs_.k])
                TT("dve", dec_[:], ps_s[:, 16:32], cs_[:], ALU.subtract, ["P:ps_s", cs_.k], [dec_.k])
                ACT(dec_[:], dec_[:], AF.Exp, [dec_.k], [dec_.k])
                ACT(cd_[:], ps_s[:, 16:32], AF.Exp, ["P:ps_s"], [cd_.k])
                TT("dve", wv_[:], dt[:], dec_[:], ALU.mult, [dt.k, dec_.k], [wv_.k])
                TT("dve", xdd_[:].rearrange("p (h d) -> p h d", h=16), xs[:].rearrange("p (h d) -> p h d", h=16),
                   wv_[:].unsqueeze(2).to_broadcast([128, 16, 64]), ALU.mult, [xs.k, wv_.k], [xdd_.k])
                if full:
                    ACT(e_[:], cs_[:], AF.Exp, [cs_.k], [e_.k])
                    TT("dve", xd_[:].rearrange("p (h d) -> p h d", h=16), xs[:].rearrange("p (h d) -> p h d", h=16),
                       dt[:].unsqueeze(2).to_broadcast([128, 16, 64]), ALU.mult, [xs.k, dt.k], [xd_.k])
                    for g in range(2):
                        MM(ps_s[:, 128 + g * 128:256 + g * 128], bT[:, g, :], cT[:, g, :], True, True,
                           [bT.k, cT.k], ["P:ps_s"])
                        TT("dve", cbm[g][:], ps_s[:, 128 + g * 128:256 + g * 128], mle, ALU.mult,
                           ["P:ps_s", cstf.k], [cbm[g].k])
                        MM(ps_yo1[:], cT[:, g, :], Sb[:, g, :], True, True, [cT.k, Sb.k], [ps_yo1.k])
                        TT("dve", y1[:, g * 512:(g + 1) * 512].rearrange("p (h d) -> p h d", h=8),
                           ps_yo1[:].rearrange("p (h d) -> p h d", h=8),
                           e_[:, g * 8:(g + 1) * 8].unsqueeze(2).to_broadcast([128, 8, 64]), ALU.mult,
                           [ps_yo1.k, e_.k], [y1.k])
                    for hb in range(4):
                        g = hb // 2
                        h0 = hb * 4
                        l_, e2, m_ = lh[hb % 2], eD[hb % 2], Mt[hb % 2]
                        TT("dve", l_[:], cstf[:, 3:4, :].to_broadcast([128, 4, 128]),
                           a_[:, h0:h0 + 4].unsqueeze(2).to_broadcast([128, 4, 128]), ALU.mult, [cstf.k, a_.k], [l_.k])
                        ps_D = ps_Ds[hb % 2]
                        for i in range(4):
                            MM(ps_D[:, i * 128:(i + 1) * 128], l_[:, i, :], mle, True, True, [l_.k, cstf.k], [ps_D.k],
                               sig=(i == 3))
                        ACT(e2[:], ps_D[:], AF.Exp, [ps_D.k], [e2.k])
                        TT("dve", m_[:], e2[:].rearrange("p (i t) -> p i t", i=4),
                           cbm[g][:].unsqueeze(1).to_broadcast([128, 4, 128]), ALU.mult, [e2.k, cbm[g].k], [m_.k])
                        for i in range(4):
                            h = h0 + i
                            MM(ps_y[g][:, (h % 8) * 64:(h % 8 + 1) * 64], m_[:, i, :], xd_[:, h * 64:(h + 1) * 64], True, True,
                               [m_.k, xd_.k], [ps_y[g].k])
                for g in range(2):
                    MM(ps_st[:], bt[:, g * 128:(g + 1) * 128], xdd_[:, g * 512:(g + 1) * 512], True, True,
                       [bt.k, xdd_.k], [ps_st.k])
                    if full:
                        pass
                    TT("dve", Sf[:, g, :].rearrange("p (h d) -> p h d", h=8), Sf[:, g, :].rearrange("p (h d) -> p h d", h=8),
                       cd_[:, g * 8:(g + 1) * 8].unsqueeze(2).to_broadcast([128, 8, 64]), ALU.mult,
                       [Sf.k, cd_.k], [Sf.k])
                    TT("dve", Sf[:, g, :], Sf[:, g, :], ps_st[:], ALU.add, [Sf.k, ps_st.k], [Sf.k])
                if c == HALO:
                    TS("dve", Sf[:], Sf[:], flg[:, 0:1], ALU.mult, [Sf.k, flg.k], [Sf.k])
                CP("act", Sb[:], Sf[:], [Sf.k], [Sb.k])
                if full:
                    for g in range(2):
                        sl = slice(g * 512, (g + 1) * 512)
                        TT("dve", y1[:, sl], y1[:, sl], ps_y[g][:], ALU.add, [y1.k, ps_y[g].k], [y1.k])
                    TT("dve", y3[:], xs[:], dfl[:], ALU.mult, [xs.k, dfl.k], [y3.k])
                    TT("dve", y3[:], y3[:], y1[:], ALU.add, [y3.k, y1.k], [y3.k])
                    TT("dve", y3[:], y3[:], z[:], ALU.mult, [y3.k, z.k], [y3.k])
                    for g in range(2):
                        ACT(yj[:], y3[:, g * 512:(g + 1) * 512], AF.Square, [y3.k], [yj.k, ss2.k + str(g)],
                            accum_out=ss2[:, g:g + 1])
                    ACT(ss2[:], ss2[:], AF.Ln, [ss2.k + "0", ss2.k + "1"], [ss2.k], scale=1.0 / 512, bias=EPS)
                    ACT(ss2[:], ss2[:], AF.Exp, [ss2.k], [ss2.k], scale=-0.5)
                    for g in range(2):
                        sl = slice(g * 512, (g + 1) * 512)
                        STT(yn[:, sl], y3[:, sl], ss2[:, g:g + 1], snwb[:, sl], ALU.mult, ALU.mult,
                            [y3.k, ss2.k, snwb.k], [yn.k])
                    for kc in range(8):
                        TR(ps_tr[:, kc * 128:(kc + 1) * 128], yn[:, kc * 128:(kc + 1) * 128], identb[:],
                           [yn.k, identb.k], [ps_tr.k])
                    yT = ynT[i2]
                    CP("act", yT[:], ps_tr[:].rearrange("p (k t) -> p k t", k=8), [ps_tr.k], [yT.k])
                    ST(ymix[c - HALO, :, 0:8, :], yT[:], [yT.k], ["ymixA"])
            S.barrier()

        esD = contextlib.ExitStack()
        if upto >= 3:
          woD = load_w_bf16(esD, w_out0, 2048, 1024, None, "wo0", prefetch=True)
          gwD = load_w_bf16(esD, gate_w[0], 1024, 1024, None, "gw0", prefetch=True)
          pwD = load_w_bf16(esD, ple_w[0], 256, 1024, None, "pw0", prefetch=True)
          with contextlib.ExitStack() as es:
            lamb = mk(es, "lamb", [128, 256], F32)
            LD(lamb[:], lamv.partition_broadcast(128), [], [lamb.k])
            lt = mk(es, "lt", [128, 128], F32)
            lsum = mk(es, "lsum", [128, 2], F32)
            neglam = mk(es, "neglam", [128, 1], F32)
            for i in range(2):
                S.emit("dve", lambda e, i=i: e.tensor_tensor_reduce(
                    out=lt[:, i * 64:(i + 1) * 64], in0=lamb[:, i * 128:i * 128 + 64], in1=lamb[:, i * 128 + 64:i * 128 + 128],
                    op0=ALU.mult, op1=ALU.add, scale=1.0, scalar=0.0, accum_out=lsum[:, i:i + 1]) if False else
                    e.tensor_tensor(out=lt[:, i * 64:(i + 1) * 64], in0=lamb[:, i * 128:i * 128 + 64],
                                    in1=lamb[:, i * 128 + 64:i * 128 + 128], op=ALU.mult), [lamb.k], [lt.k])
                S.emit("dve", lambda e, i=i: e.reduce_sum(out=lsum[:, i:i + 1], in_=lt[:, i * 64:(i + 1) * 64],
                                                         axis=mybir.AxisListType.X), [lt.k], [lsum.k])
            ACT(lsum[:], lsum[:], AF.Exp, [lsum.k], [lsum.k])
            STT(neglam[:], lsum[:, 1:2], -0.2, lsum[:, 0:1], ALU.add, ALU.subtract, [lsum.k], [neglam.k])
            subc = mk(es, "subc", [128, 1], F32)
            LD(subc[:], subw, [], [subc.k])
            TS("dve", subc[:], subc[:], 0.8, ALU.mult, [subc.k], [subc.k])
            Kh = [mk(es, "Kh", [128, 8192], BF16) for _ in range(2)]
            Vh = [mk(es, "Vh", [128, NBLK, 128], BF16) for _ in range(2)]
            Qh = [mk(es, "Qh", [128, NHB * 128], BF16) for _ in range(2)]
            Pt = [mk(es, "Pt", [128, 2, 512], BF16) for _ in range(NPT)]
            accs = [mk(es, "accs", [128, 2, 512], F32) for _ in range(2)]
            tmps = [mk(es, "tmps", [128, 2, 512], BF16) for _ in range(2)]
            gst = [mk(es, "gst", [128, 512], BF16) for _ in range(2)]
            r12 = mk(es, "r12", [128, 2, 512], F32)
            o12 = mk(es, "o12", [128, 2, 512], F32)
            osq = mk(es, "osq", [128, 512], F32)
            og = [mk(es, "og", [128, 512], BF16) for _ in range(2)]
            psS = [mk(es, "psS", [128, 2, 512], F32, psum=True) for _ in range(NPS)]
            psO = [mk(es, "psO", [128, 2, 512], F32, psum=True) for _ in range(4 - NPS)]
            nmask = mk(es, "nmask", [128, 128], BF16)
            TS("dve", nmask[:], mgt, -30000.0, ALU.mult, [cstf.k], [nmask.k])
            un = 0
            nch = 0
            fin2 = []

            def load_head(h):
                K, V, Q = Kh[h % 2], Vh[h % 2], Qh[h % 2]
                LD(K[:], kT[h], ["kT"], [K.k])
                LD(V[:], vS[:, :, h * 128:(h + 1) * 128].rearrange("b p e -> p b e"), ["vS"], [V.k])
                LD(Q[:], qT[h, :, (HALO - FB0) * 128:], ["qT"], [Q.k])
            load_head(0)
            for h in range(8):
                K, V, Q = Kh[h % 2], Vh[h % 2], Qh[h % 2]
                chunks = [(0, 128, [(kb, "pre0") for kb in range(HALO)] + [(HALO, "diag0")], 0)]
                for qc in range(8):
                    lst = [(kb, "pre") for kb in range(32)]
                    lst += [(32 + kb, "full") for kb in range(4 * qc)]
                    lst += [(32 + 4 * qc + r, f"diag{r}") for r in range(4)]
                    chunks.append((128 + qc * 512, 512, lst, 1 + qc * 4))
                for ci, (q0, qw, lst, yb0) in enumerate(chunks):
                    if ci == 1 and h + 1 < 8:
                        load_head(h + 1)
                    gs_ = gst[nch % 2]
                    acc = accs[nch % 2]
                    started = set()
                    hold = {"n": 0}
                    nfull = sum(1 for (_, kd) in lst if not kd.startswith("diag"))
                    pO = psO[nch % (4 - NPS)]
                    nch += 1
                    LD(gs_[:, 0:qw], gsT[h, :, (HALO - FB0) * 128 + q0:(HALO - FB0) * 128 + q0 + qw], ["gsT"], [gs_.k])
                    nk = len(lst)

                    def emit_qk(ik):
                        nonlocal un
                        kb, kind = lst[ik]
                        c0 = int(kind[4:]) * 128 if kind.startswith("diag") else 0
                        wcol = qw - c0
                        sS = psS[un % NPS]
                        pP = Pt[un % NPT]
                        un += 1
                        dg_ = kind.startswith("diag")
                        MM(sS[:, 0, 0:wcol], K[0:64, kb * 128:(kb + 1) * 128], Q[0:64, q0 + c0:q0 + qw], True, not dg_,
                           [K.k, Q.k], [sS.k], sig=False)
                        MM(sS[:, 1, 0:wcol], K[64:128, kb * 128:(kb + 1) * 128], Q[64:128, q0 + c0:q0 + qw], True, not dg_,
                           [K.k, Q.k], [sS.k], sig=not dg_)
                        if dg_:
                            MM(sS[:, 0, 0:128], identb[:], nmask[:], False, True, [identb.k, nmask.k], [sS.k], sig=False)
                            MM(sS[:, 1, 0:128], identb[:], nmask[:], False, True, [identb.k, nmask.k], [sS.k])
                        return (ik, kb, kind, c0, wcol, sS, pP)

                    def emit_rest(info):
                        ik, kb, kind, c0, wcol, sS, pP = info
                        if kind == "pre":
                            ACT(pP[:, :, 0:wcol], sS[:, :, 0:wcol], AF.Exp, [sS.k, flg.k], [pP.k], scale=0.125,
                                bias=flg[:, 1:2])
                        else:
                            ACT(pP[:, :, 0:wcol], sS[:, :, 0:wcol], AF.Exp, [sS.k], [pP.k], scale=0.125)
                        st, sp_ = (ik == 0), (ik == nk - 1)
                        MM(pO[:, 0, c0:qw], V[:, kb, :], pP[:, 0, 0:wcol], st, sp_, [V.k, pP.k], [pO.k], sig=False)
                        MM(pO[:, 1, c0:qw], V[:, kb, :], pP[:, 1, 0:wcol], st, sp_, [V.k, pP.k], [pO.k])
                        def acc_add(src, lo, hi, c_lo):
                            if acc.k not in started:
                                started.add(acc.k)
                                assert c_lo == 0
                                CP("dve", acc[:, :, 0:qw], src[:, :, 0:qw], [src.k], [acc.k])
                            else:
                                TT("dve", acc[:, :, c_lo:qw], acc[:, :, c_lo:qw], src[:, :, lo:hi], ALU.add,
                                   [acc.k, src.k], [acc.k])
                        if kind.startswith("diag"):
                            acc_add(pP, 0, wcol, c0)
                        else:
                            gpos = ik % 4
                            last_full = (ik == nfull - 1)
                            if gpos == 0:
                                if last_full:
                                    acc_add(pP, 0, qw, 0)
                                else:
                                    hold["p"] = pP
                            elif gpos == 1:
                                hold["t"] = tmps[hold["n"] % 2]
                                hold["n"] += 1
                                TT("dve", hold["t"][:, :, 0:qw], hold["p"][:, :, 0:qw], pP[:, :, 0:qw], ALU.add,
                                   [hold["p"].k, pP.k], [hold["t"].k])
                                if last_full:
                                    acc_add(hold["t"], 0, qw, 0)
                            else:
                                t_ = hold["t"]
                                TT("dve", t_[:, :, 0:qw], t_[:, :, 0:qw], pP[:, :, 0:qw], ALU.add, [t_.k, pP.k], [t_.k])
                                if gpos == 3 or last_full:
                                    acc_add(t_, 0, qw, 0)

                    infos = [emit_qk(ik) for ik in range(min(QKD, nk))]
                    for ik in range(nk):
                        if ik + QKD < nk:
                            infos.append(emit_qk(ik + QKD))
                        emit_rest(infos[ik])
                        while fin2 and fin2[0][0] <= ik:
                            fin2.pop(0)[1]()
                    pD = psS[un % NPS]
                    un += 1
                    for mp in range(2):
                        MM(pD[:, mp, 0:qw], onesf, acc[:, mp, 0:qw], True, True, [cstf.k, acc.k], [pD.k])
                    S.emit("dve", lambda e, pO=pO, qw=qw: e.tensor_copy(out=o12[:, :, 0:qw], in_=pO[:, :, 0:qw]), [pO.k], [o12.k])
                    S.emit("dve", lambda e, pD=pD, qw=qw: e.tensor_copy(out=r12[:, :, 0:qw], in_=pD[:, :, 0:qw]), [pD.k], [r12.k])
                    w_ = slice(0, qw)

                    def st2a(qw=qw):
                        S.emit("dve", lambda e: e.reciprocal(out=r12[:, :, 0:qw], in_=r12[:, :, 0:qw]), [r12.k], [r12.k])

                    def st2b(qw=qw, w_=w_):
                        TT("dve", o12[:, :, 0:qw], o12[:, :, 0:qw], r12[:, :, 0:qw], ALU.mult, [o12.k, r12.k], [o12.k])
                        STT(o12[:, 0, w_], o12[:, 1, w_], neglam[:, 0:1], o12[:, 0, w_], ALU.mult, ALU.add,
                            [neglam.k, o12.k], [o12.k])
                        TT("dve", osq[:, w_], o12[:, 0, w_], o12[:, 0, w_], ALU.mult, [o12.k], [osq.k])

                    def st2c(w_=w_, qw=qw, gs_=gs_, yb0=yb0, h=h, og_=og[nch % 2]):
                        pss = psS[un % NPS]
                        MM(pss[:, 0, w_], onesf, osq[:, w_], True, True, [cstf.k, osq.k], [pss.k])
                        ACT(r12[:, 0, w_], pss[:, 0, w_], AF.Ln, [pss.k], [r12.k], scale=1.0 / 128, bias=EPS)
                        ACT(r12[:, 0, w_], r12[:, 0, w_], AF.Exp, [r12.k], [r12.k], scale=-0.5)
                        TT("dve", o12[:, 0, w_], o12[:, 0, w_], r12[:, 0, w_], ALU.mult, [o12.k, r12.k], [o12.k])
                        STT(og_[:, w_], o12[:, 0, w_], subc[:, 0:1], gs_[:, w_], ALU.mult, ALU.mult,
                            [o12.k, subc.k, gs_.k], [og_.k])
                        nb = qw // 128
                        ST(ymix[yb0:yb0 + nb, :, 8 + h, :].rearrange("b p t -> p b t"),
                           og_[:, w_].rearrange("p (b t) -> p b t", b=nb), [og_.k], ["ymixB"])
                    fin2.extend([(5, st2a), (9, st2b), (13, st2c)])
            while fin2:
                fin2.pop(0)[1]()
            S.barrier()

        def ple_tail(es_bufs, li, ps_m, hres, pblk, w):
            hp, hb, hT, pb, pT, sg, ps_g, ps_e, ps_trl = (es_bufs[k] for k in
                                                         ("hp", "hb", "hT", "pb", "pT", "sg", "ps_g", "ps_e", "ps_tr"))
            gatew, plew = w
            for hf in range(2):
                sl = slice(hf * 512, (hf + 1) * 512)
                TT("dve", hp[:, sl], ps_m[hf][:], hres[:, sl], ALU.add, [ps_m[hf].k, hres.k], [hp.k])
            CP("act", hb[:], hp[:], [hp.k], [hb.k])
            CP("pool", pb[:], pblk[:], [pblk.k], [pb.k])
            pt_ = ps_trl[0]
            for kc in range(8):
                TR(pt_[:, kc * 128:(kc + 1) * 128], hb[:, kc * 128:(kc + 1) * 128], identb[:], [hb.k, identb.k], [pt_.k])
            CP("act", hT[:], pt_[:].rearrange("p (k t) -> p k t", k=8), [pt_.k], [hT.k])
            pt2 = ps_trl[1]
            for kc in range(2):
                TR(pt2[:, kc * 128:(kc + 1) * 128], pb[:, kc * 128:(kc + 1) * 128], identb[:], [pb.k, identb.k], [pt2.k])
            CP("dve", pT[:], pt2[:, 0:256].rearrange("p (k t) -> p k t", k=2), [pt2.k], [pT.k])
            for hf in range(2):
                sl = slice(hf * 512, (hf + 1) * 512)
                for kc in range(8):
                    MM(ps_g[hf][:], hT[:, kc, :], gatew[:, kc, sl], kc == 0, kc == 7, [hT.k, gatew.k], [ps_g[hf].k], sig=(kc == 7))
                for kc in range(2):
                    MM(ps_e[hf][:], pT[:, kc, :], plew[:, kc, sl], kc == 0, kc == 1, [pT.k, plew.k], [ps_e[hf].k], sig=(kc == 1))
                ACT(sg[:, sl], ps_g[hf][:], AF.Sigmoid, [ps_g[hf].k], [sg.k])
                TT("dve", sg[:, sl], ps_e[hf][:], sg[:, sl], ALU.mult, [ps_e[hf].k, sg.k], [sg.k])
            TT("dve", hp[:], hp[:], sg[:], ALU.add, [hp.k, sg.k], [hp.k])
            return hp

        def tail_bufs(es):
            d = {}
            d["hp"] = mk(es, "hp", [128, 1024], F32)
            d["hb"] = mk(es, "hb", [128, 1024], BF16)
            d["hT"] = mk(es, "hT", [128, 8, 128], BF16)
            d["pb"] = mk(es, "pb", [128, 256], BF16)
            d["pT"] = mk(es, "pT", [128, 2, 128], BF16)
            d["sg"] = mk(es, "sg", [128, 1024], F32)
            return d

        if upto >= 4:
          with contextlib.ExitStack() as es:
            wo, gw, pw = woD, gwD, pwD
            bufs = tail_bufs(es)
            bufs["ps_g"] = [mk(es, "ps_g", [128, 512], F32, psum=True) for _ in range(2)]
            bufs["ps_e"] = [mk(es, "ps_e", [128, 512], F32, psum=True) for _ in range(2)]
            bufs["ps_tr"] = [mk(es, "ps_tr", [128, 1024], BF16, psum=True) for _ in range(2)]
            ps_m = [mk(es, "ps_m", [128, 512], F32, psum=True) for _ in range(2)]
            ym = [mk(es, "ym", [128, 16, 128], BF16) for _ in range(2)]
            xr = [mk(es, "xr", [128, 1024], F32) for _ in range(2)]
            pr = [mk(es, "pr", [128, 256], F32) for _ in range(2)]
            junk = mk(es, "junkD", [128, 1024], BF16)
            ss = mk(es, "ssD", [128, 1], F32)
            xn1 = mk(es, "xn1", [128, 1024], BF16)
            hnT1 = [mk(es, "hnT1", [128, 8, 128], BF16) for _ in range(2)]
            for b in range(NHB):
                i2 = b % 2
                LD(ym[i2][:], ymix[b], ["ymixA", "ymixB"], [ym[i2].k])
                LD(xr[i2][:], x_loc[HALO + b], [], [xr[i2].k])
                LD(pr[i2][:], p0[b], [], [pr[i2].k])
                for hf in range(2):
                    for kc in range(16):
                        MM(ps_m[hf][:], ym[i2][:, kc, :], wo[:, kc, hf * 512:(hf + 1) * 512], kc == 0, kc == 15,
                           [ym[i2].k, wo.k], [ps_m[hf].k], sig=(kc == 15))
                hp = ple_tail(bufs, 0, ps_m, xr[i2], pr[i2], (gw, pw))
                ST(h1S[b], hp[:], [hp.k], ["h1S"])
                ACT(junk[:], hp[:], AF.Square, [hp.k], [junk.k, ss.k], accum_out=ss[:])
                rstd_from_ss(ss, 1024)
                TS("dve", xn1[:], hp[:], ss[:, 0:1], ALU.mult, [hp.k, ss.k], [xn1.k])
                pt_ = bufs["ps_tr"][0]
                for kc in range(8):
                    TR(pt_[:, kc * 128:(kc + 1) * 128], xn1[:, kc * 128:(kc + 1) * 128], identb[:],
                       [xn1.k, identb.k], [pt_.k])
                CP("act", hnT1[i2][:], pt_[:].rearrange("p (k t) -> p k t", k=8), [pt_.k], [hnT1[i2].k])
                ST(hn1T[b], hnT1[i2][:], [hnT1[i2].k], ["hn1T"])
            S.barrier()

        esD.close()
        esE = contextlib.ExitStack()
        if upto >= 5:
          woE = load_w_bf16(esE, w_out1, 2048, 1024, None, "wo1", prefetch=True)
          gwE = load_w_bf16(esE, gate_w[1], 1024, 1024, None, "gw1", prefetch=True)
          pwE = load_w_bf16(esE, ple_w[1], 256, 1024, None, "pw1", prefetch=True)
          with contextlib.ExitStack() as es:
            nwc1 = mk(es, "nwc1", [128, 8], F32)
            LD(nwc1[:], nw[1].rearrange("(k p) o -> p (k o)", p=128), [], [nwc1.k])
            W1 = load_w_bf16(es, w_in1, 1024, 6144, nwc1, "W1")
            hg = [mk(es, "hg", [128, 8, 512], BF16) for _ in range(2)]
            sgm = [mk(es, "sgm", [128, 512], F32) for _ in range(2)]
            glu = [mk(es, "glu", [128, 512], BF16) for _ in range(2)]
            sgo = [mk(es, "sgo", [128, 512], BF16) for _ in range(2)]
            pacc = [mk(es, "paccE", [128, 512], F32, psum=True) for _ in range(4)]
            na = 0
            groups = [(0, 1)] + [(1 + 4 * G, 4) for G in range(8)]
            for gi, (b0, nb) in enumerate(groups):
                tw = nb * 128
                hg_ = hg[gi % 2]
                for bi in range(nb):
                    LD(hg_[:, :, bi * 128:(bi + 1) * 128], hn1T[b0 + bi], ["hn1T"], [hg_.k])

                def proj(col0):
                    nonlocal na
                    pa = pacc[na % 4]
                    na += 1
                    for kc in range(8):
                        MM(pa[:, 0:tw], W1[:, kc, col0:col0 + 128], hg_[:, kc, 0:tw], kc == 0, kc == 7,
                           [W1.k, hg_.k], [pa.k], sig=(kc == 7))
                    return pa
                for j in range(16):
                    pu = proj(j * 128)
                    pg = proj(2048 + j * 128)
                    sg_, gl_ = sgm[j % 2], glu[j % 2]
                    ACT(sg_[:, 0:tw], pg[:, 0:tw], AF.Sigmoid, [pg.k], [sg_.k])
                    if gi == 0:
                        STT(gl_[:, 0:tw], pu[:, 0:tw], flg[:, 0:1], sg_[:, 0:tw], ALU.mult, ALU.mult,
                            [pu.k, flg.k, sg_.k], [gl_.k])
                    else:
                        TT("dve", gl_[:, 0:tw], pu[:, 0:tw], sg_[:, 0:tw], ALU.mult, [pu.k, sg_.k], [gl_.k])
                    ST(gluT[j, :, b0 * 128:b0 * 128 + tw], gl_[:, 0:tw], [gl_.k], ["gluT"])
                if gi > 0:
                    for j in range(16):
                        pg = proj(4096 + j * 128)
                        so = sgo[j % 2]
                        ACT(so[:, 0:tw], pg[:, 0:tw], AF.Silu, [pg.k], [so.k])
                        ST(sg1T[j, :, (b0 - 1) * 128:(b0 - 1) * 128 + tw], so[:, 0:tw], [so.k], ["sg1T"])
            S.barrier()

        if upto >= 6:
          with contextlib.ExitStack() as es:
            wo, gw, pw = woE, gwE, pwE
            ccwc = mk(es, "ccwc", [128, 16, 31], F32)
            LD(ccwc[:], ccw.rearrange("(c p) j -> p c j", p=128), [], [ccwc.k])
            colp = mk(es, "colp", [128, 3, 16], F32)
            for i, src in enumerate((ccb, lnw, lnb)):
                LD(colp[:, i, :], src.rearrange("(c p) o -> p (c o)", p=128), [], [colp.k + str(i)])
            fnb = mk(es, "fnb", [128, 1024], F32)
            LD(fnb[:], fnw.partition_broadcast(128), [], [fnb.k])
            bufs = tail_bufs(es)
            psb = [mk(es, "psb", [128, 512], F32, psum=True) for _ in range(6)]
            bufs["ps_g"] = [psb[2], psb[3]]
            bufs["ps_e"] = [psb[4], psb[5]]
            bufs["ps_tr"] = [mk(es, "ps_tr", [128, 1024], BF16, psum=True) for _ in range(2)]
            ps_m = [psb[0], psb[1]]
            gl = mk(es, "gl", [128, 16, 544], BF16)
            sgg = mk(es, "sgg", [128, 16, 512], BF16)
            cv = mk(es, "cv", [128, 16, 512], F32)
            dg = [mk(es, "dg", [128, 31, 128], BF16) for _ in range(2)]
            sqs = [mk(es, "sqs", [128, 512], F32) for _ in range(2)]
            mean = mk(es, "mean", [128, 512], F32)
            var = mk(es, "var", [128, 512], F32)
            tn = [mk(es, "tn", [128, 512], F32) for _ in range(2)]
            aT = sgg
            hr = [mk(es, "hr", [128, 1024], F32) for _ in range(2)]
            pr = [mk(es, "pr1", [128, 256], F32) for _ in range(2)]
            junk = mk(es, "junkE", [128, 1024], BF16)
            ss = mk(es, "ssE", [128, 1], F32)
            ot = [mk(es, "ot", [128, 1024], F32) for _ in range(2)]
            def load_gl(G):
                c0 = 128 + G * 512 - 32
                LD(gl[:], gluT[:, :, c0:c0 + 544].rearrange("j p t -> p j t"), ["gluT"], [gl.k])
            load_gl(0)
            for G in range(8):
                LD(sgg[:], sg1T[:, :, G * 512:(G + 1) * 512].rearrange("j p t -> p j t"), ["sg1T"],
                   [sgg.k + str(j) for j in range(16)])

                def stats(j):
                    sq_ = sqs[j % 2]
                    ACT(sq_[:], cv[:, j, :], AF.Square, [cv.k + str(j)], [sq_.k])
                    MM(psb[2][:], onesf, cv[:, j, :], j == 0, j == 15, [cstf.k, cv.k + str(j)], [psb[2].k], sig=False)
                    MM(psb[3][:], onesf, sq_[:], j == 0, j == 15, [cstf.k, sq_.k], [psb[3].k])
                for j in range(16):
                    d_ = dg[j % 2]
                    TT("dve", d_[:], identb[:].unsqueeze(1).to_broadcast([128, 31, 128]),
                       ccwc[:, j, :].unsqueeze(2).to_broadcast([128, 31, 128]), ALU.mult, [identb.k, ccwc.k], [d_.k])
                    pc = psb[j % 2]
                    for tp in range(31):
                        MM(pc[:], d_[:, tp, :], gl[:, j, 2 + tp:2 + tp + 512], tp == 0, tp == 30, [d_.k, gl.k], [pc.k], sig=(tp == 30))
                    ACT(cv[:, j, :], pc[:], AF.Identity, [pc.k, colp.k + "0"], [cv.k + str(j)], bias=colp[:, 0, j:j + 1])
                    if j > 0:
                        stats(j - 1)
                stats(15)
                if G + 1 < 8:
                    load_gl(G + 1)
                ACT(mean[:], psb[2][:], AF.Copy, [psb[2].k], [mean.k], scale=1.0 / 2048)
                TT("dve", var[:], mean[:], mean[:], ALU.mult, [mean.k], [var.k])
                STT(var[:], psb[3][:], 1.0 / 2048, var[:], ALU.mult, ALU.subtract, [psb[3].k, var.k], [var.k])
                ACT(var[:], var[:], AF.Sqrt, [var.k], [var.k], bias=EPS)
                RECIP(var[:], var[:], [var.k], [var.k])
                for j in range(16):
                    t_ = tn[j % 2]
                    TT("dve", t_[:], cv[:, j, :], mean[:], ALU.subtract, [cv.k + str(j), mean.k], [t_.k])
                    TT("dve", t_[:], t_[:], var[:], ALU.mult, [t_.k, var.k], [t_.k])
                    ACT(t_[:], t_[:], AF.Silu, [t_.k, colp.k + "1", colp.k + "2"], [t_.k],
                        scale=colp[:, 1, j:j + 1], bias=colp[:, 2, j:j + 1])
                    TT("dve", aT[:, j, :], t_[:], sgg[:, j, :], ALU.mult, [t_.k, sgg.k + str(j)], [sgg.k + str(j)])
                for bi in range(4):
                    b = G * 4 + bi
                    i2 = bi % 2
                    LD(hr[i2][:], h1S[1 + b], ["h1S"], [hr[i2].k])
                    LD(pr[i2][:], p1[b], [], [pr[i2].k])
                    for hf in range(2):
                        for kc in range(16):
                            MM(ps_m[hf][:], aT[:, kc, bi * 128:(bi + 1) * 128], wo[:, kc, hf * 512:(hf + 1) * 512],
                               kc == 0, kc == 15, [aT.k + str(kc), wo.k], [ps_m[hf].k], sig=(kc == 15))
                    hp = ple_tail(bufs, 1, ps_m, hr[i2], pr[i2], (gw, pw))
                    ACT(junk[:], hp[:], AF.Square, [hp.k], [junk.k, ss.k], accum_out=ss[:])
                    rstd_from_ss(ss, 1024)
                    STT(ot[i2][:], hp[:], ss[:, 0:1], fnb[:], ALU.mult, ALU.mult, [hp.k, ss.k, fnb.k], [ot[i2].k])
                    ST(out[b], ot[i2][:], [ot[i2].k], ["out"])
            S.barrier()
        esE.close()
        S.barrier()
    S.finish()
    return nc


def _consts():
    ident = np.eye(128, dtype=np.float32)
    perm = np.zeros((128, 128), np.float32)
    for m in range(128):
        d = m % 64
        k = m + 32 if d < 32 else m - 32
        perm[k, m] = 1.0
    j = np.arange(128)
    mle = (j[:, None] <= j[None, :]).astype(np.float32)
    mgt = (j[:, None] > j[None, :]).astype(np.float32)
    ones = np.ones((128, 128), np.float32)
    return np.stack([ident, perm, mle, mgt, ones])


def _rope_tables(pos):
    inv = (10000.0 ** (-np.arange(0, 64, 2, dtype=np.float32) / 64)).astype(np.float32)
    ang = pos.astype(np.float32)[None, :] * inv[:, None]
    c = np.cos(ang).astype(np.float32)
    s = np.sin(ang).astype(np.float32)
    cos128 = np.concatenate([c, c, c, c], axis=0)
    sin128 = np.concatenate([-s, s, -s, s], axis=0)
    return np.ascontiguousarray(cos128), np.ascontiguousarray(sin128)


def make_in_maps(inp, cores=range(8)):
    f = lambda a: np.ascontiguousarray(np.asarray(a, dtype=np.float32))
    x = np.asarray(inp["x"], np.float32)
    p = np.asarray(inp["p"], np.float32)
    shared = {
        "cst": _consts(),
        "w_in0": f(inp["even_w_in"][0]),
        "nw": f(np.asarray(inp["norm_w"])[:, :, None]),
        "cw0": f(np.asarray(inp["ssd_conv_w"])[0].T),
        "cb0": f(np.asarray(inp["ssd_conv_b"])[0][:, None]),
        "dtb": f(np.asarray(inp["ssd_dt_bias"])[0][None]),
        "alog": f(np.asarray(inp["ssd_a_log"])[0][None]),
        "dfull": f(np.repeat(np.asarray(inp["ssd_d"])[0], 64)[None]),
        "snw": f(np.asarray(inp["ssd_norm_w"])[0][None]),
        "lamv": f(np.asarray(inp["diff_lambda"])[0].reshape(1, 256)),
        "subw": f(np.asarray(inp["diff_subln_w"])[0][:, None]),
        "w_out0": f(inp["even_w_out"][0]),
        "ple_w": f(inp["ple_w"]),
        "gate_w": f(inp["ple_gate_w"]),
        "w_in1": f(inp["conf_w_in"][0]),
        "ccw": f(np.asarray(inp["conf_conv_w"])[0].T),
        "ccb": f(np.asarray(inp["conf_conv_b"])[0][:, None]),
        "lnw": f(np.asarray(inp["conf_ln_w"])[0][:, None]),
        "lnb": f(np.asarray(inp["conf_ln_b"])[0][:, None]),
        "w_out1": f(inp["conf_w_out"][0]),
        "fnw": f(np.asarray(inp["final_norm_w"])[None]),
    }
    maps = []
    for c in cores:
        b, s = c // 2, c % 2
        t0 = s * 4096
        xl = np.zeros((8192, 1024), np.float32)
        p0 = np.zeros((NHB * 128, 256), np.float32)
        if s == 1:
            xl[:] = x[b]
            p0[:] = p[0, b, 4096 - 128:8192]
        else:
            xl[4096:] = x[b, 0:4096]
            p0[128:] = p[0, b, 0:4096]
        p1 = p[1, b, t0:t0 + 4096]
        pos = np.arange(8192) + (t0 - 4096)
        cs, sn = _rope_tables(pos)
        fl = np.zeros((128, 2), np.float32)
        fl[:, 0] = 1.0 if s == 1 else 0.0
        fl[:, 1] = 0.0 if s == 1 else -30000.0
        m = dict(shared)
        m.update({
            "x_loc": xl.reshape(NBLK, 128, 1024),
            "p0": p0.reshape(NHB, 128, 256),
            "p1": f(p1).reshape(32, 128, 256),
            "flagc": fl, "cosT": cs, "sinT": sn,
        })
        maps.append(m)
    return maps


def kernel(**inputs):
    nc = bass.Bass("TRN2", target_bir_lowering=False)
    build(nc)
    maps = make_in_maps(inputs)
    res = run_bass_kernel_spmd(nc, maps, core_ids=list(range(8)))
    out = np.zeros((4, 8192, 1024), np.float32)
    for c in range(8):
        b, s = c // 2, c % 2
        out[b, s * 4096:(s + 1) * 4096] = np.asarray(res.results[c]["out"]).reshape(4096, 1024)
    return out
```
